# Optimizing a Trainium2 kernel written in Bass

```python
import math
import jax
import jax.numpy as jnp
from jax import lax
import numpy as np

D_MODEL = 1024
BATCH = 8
SEQ = 2048
DEPTH = 4
DEC_BATCH = 128
DEC_SEQ = 4
PAST_LEN = 2048
PAGE_SIZE = 128

N_MIXERS = 3
N_A = (DEPTH + 2) // N_MIXERS
N_B = (DEPTH + 1) // N_MIXERS
N_C = DEPTH // N_MIXERS
CONV_A = 3
GDN_HEADS = 8
GDN_DK = 128
GDN_DV = 128
GDN_CONV = 4
GDN_CHUNK = 64
GDN_QK = GDN_HEADS * GDN_DK
GDN_V = GDN_HEADS * GDN_DV
GDN_CONV_CH = 2 * GDN_QK + GDN_V
GDN_PROJ = GDN_CONV_CH + GDN_V + 2 * GDN_HEADS
MOBA_HEADS = 8
MOBA_DH = 128
MOBA_W = MOBA_HEADS * MOBA_DH
MOBA_BLOCK = 256
MOBA_TOPK = 3
MOBA_QBLOCK = 32
MEM_LEN = 256
XA_HEADS = 4
XA_DH = D_MODEL // XA_HEADS
D_FF = -(-8 * D_MODEL // (3 * 256)) * 256
EPS = 1e-6
NEG_INF = -1e30

kernel_name = 'hybrid_conv_gdn_moba_step'


def rmsnorm(x, g):
    xf = x.astype(jnp.float32)
    y = xf * lax.rsqrt(jnp.mean(xf * xf, axis=-1, keepdims=True) + EPS)
    return (y * g.astype(jnp.float32)).astype(x.dtype)


def l2norm(x):
    xf = x.astype(jnp.float32)
    return xf * lax.rsqrt(jnp.sum(xf * xf, axis=-1, keepdims=True) + EPS)


def causal_depthwise_conv(x_pad, w, length):
    return sum(w[j] * x_pad[:, j:j + length] for j in range(w.shape[0]))


def shortconv_mixer(h, buf, w_in, w_conv, w_out):
    length = h.shape[1]
    gate_b, gate_c, u = jnp.split(h @ w_in, 3, axis=-1)
    z = jnp.concatenate([buf.astype(h.dtype), gate_c * u], axis=1)
    y = gate_b * causal_depthwise_conv(z, w_conv, length)
    return y @ w_out, z[:, -(CONV_A - 1):]


def gated_delta_rule(q, k, v, beta, g, s0):
    bsz, length, nh, dk = q.shape
    dv = v.shape[-1]
    c = math.gcd(length, GDN_CHUNK)
    n = length // c

    def blk(t):
        t = t.reshape((bsz, n, c, nh) + t.shape[3:])
        return jnp.moveaxis(t, (1, 3), (0, 2))

    q, k, v, beta, g = blk(q), blk(k), blk(v), blk(beta), blk(g)
    gam = jnp.cumsum(g, axis=-1)
    idx = jnp.arange(c)
    incl = idx[:, None] >= idx[None, :]
    strict = idx[:, None] > idx[None, :]
    diff = gam[..., :, None] - gam[..., None, :]
    decay = jnp.where(incl, jnp.exp(jnp.where(incl, diff, 0.0)), 0.0)
    kk = jnp.einsum('nbhid,nbhjd->nbhij', k, k)
    a_mat = jnp.where(strict, beta[..., :, None] * kk * decay, 0.0) + jnp.eye(c, dtype=jnp.float32)
    rhs = jnp.concatenate([beta[..., None] * v, (beta * jnp.exp(gam))[..., None] * k], axis=-1)
    sol = lax.linalg.triangular_solve(a_mat, rhs, left_side=True, lower=True, unit_diagonal=True)
    u_c, kd_c = sol[..., :dv], sol[..., dv:]
    aqk = jnp.einsum('nbhid,nbhjd->nbhij', q, k) * decay
    q_g = q * jnp.exp(gam)[..., None]
    g_last = gam[..., -1]
    k_dec = k * jnp.exp(g_last[..., None] - gam)[..., None]

    def step(s, xs):
        u_i, kd_i, aqk_i, qg_i, kdec_i, gl_i = xs
        w_i = u_i - jnp.einsum('bhcd,bhde->bhce', kd_i, s)
        o_i = jnp.einsum('bhcd,bhde->bhce', qg_i, s) + jnp.einsum('bhij,bhje->bhie', aqk_i, w_i)
        s = jnp.exp(gl_i)[..., None, None] * s + jnp.einsum('bhcd,bhce->bhde', kdec_i, w_i)
        return s, o_i

    s_fin, o = lax.scan(step, s0, (u_c, kd_c, aqk, q_g, k_dec, g_last))
    o = jnp.moveaxis(o, (0, 2), (1, 3)).reshape(bsz, length, nh, dv)
    return o, s_fin


def gdn_mixer(h, buf, s0, w_in, w_conv, a_log, dt_bias, g_norm, w_out):
    bsz, length, _ = h.shape
    proj = h @ w_in
    qkv, z, b_logit, a_logit = jnp.split(
        proj, [GDN_CONV_CH, GDN_CONV_CH + GDN_V, GDN_CONV_CH + GDN_V + GDN_HEADS], axis=-1)
    qkv_pad = jnp.concatenate([buf.astype(h.dtype), qkv], axis=1)
    qkv_c = jax.nn.silu(causal_depthwise_conv(qkv_pad, w_conv, length))
    q, k, v = jnp.split(qkv_c, [GDN_QK, 2 * GDN_QK], axis=-1)
    q = l2norm(q.reshape(bsz, length, GDN_HEADS, GDN_DK)) * GDN_DK ** -0.5
    k = l2norm(k.reshape(bsz, length, GDN_HEADS, GDN_DK))
    v = v.reshape(bsz, length, GDN_HEADS, GDN_DV).astype(jnp.float32)
    beta = jax.nn.sigmoid(b_logit.astype(jnp.float32))
    g = -jnp.exp(a_log.astype(jnp.float32)) * jax.nn.softplus(
        a_logit.astype(jnp.float32) + dt_bias.astype(jnp.float32))
    o, s_fin = gated_delta_rule(q, k, v, beta, g, s0.astype(jnp.float32))
    o = rmsnorm(o, g_norm) * jax.nn.silu(z.reshape(bsz, length, GDN_HEADS, GDN_DV).astype(jnp.float32))
    out = o.reshape(bsz, length, GDN_V).astype(h.dtype) @ w_out
    return out, qkv_pad[:, -(GDN_CONV - 1):], s_fin.astype(s0.dtype)


def moba_core(q, pos, k_sel, v_sel, valid, k_own, v_own, pos_own):
    scale = MOBA_DH ** -0.5
    s_own = jnp.einsum('qhd,rhd->qhr', q, k_own).astype(jnp.float32) * scale
    s_own = jnp.where(pos_own[None, None, :] <= pos[:, None, None], s_own, NEG_INF)
    if k_sel is None:
        p = jax.nn.softmax(s_own, axis=-1).astype(v_own.dtype)
        return jnp.einsum('qhr,rhd->qhd', p, v_own)
    s_sel = jnp.einsum('qhd,qhrd->qhr', q, k_sel).astype(jnp.float32) * scale
    if valid is not None:
        s_sel = jnp.where(valid, s_sel, NEG_INF)
    p = jax.nn.softmax(jnp.concatenate([s_sel, s_own], axis=-1), axis=-1).astype(v_own.dtype)
    r = k_sel.shape[2]
    return (jnp.einsum('qhr,qhrd->qhd', p[..., :r], v_sel)
            + jnp.einsum('qhr,rhd->qhd', p[..., r:], v_own))


def moba_prompt(h, w_qkv, w_out):
    bsz, length, _ = h.shape
    q, k, v = [t.reshape(bsz, length, MOBA_HEADS, MOBA_DH) for t in jnp.split(h @ w_qkv, 3, axis=-1)]
    nb = -(-length // MOBA_BLOCK)
    pad = ((0, 0), (0, nb * MOBA_BLOCK - length), (0, 0), (0, 0))
    kblk = jnp.pad(k, pad).reshape(bsz, nb, MOBA_BLOCK, MOBA_HEADS, MOBA_DH)
    vblk = jnp.pad(v, pad).reshape(bsz, nb, MOBA_BLOCK, MOBA_HEADS, MOBA_DH)
    kmean = kblk.astype(jnp.float32).mean(axis=2)
    n_sel = min(MOBA_TOPK, nb - 1)
    qb_len = math.gcd(length, MOBA_QBLOCK)
    nq = length // qb_len
    q_chunks = q.reshape(bsz * nq, qb_len, MOBA_HEADS, MOBA_DH)
    b_idx = jnp.repeat(jnp.arange(bsz), nq)
    q_start = jnp.tile(jnp.arange(nq) * qb_len, bsz)
    h_idx = jnp.arange(MOBA_HEADS)[None, :, None]

    def one(xs):
        qc, b, s0 = xs
        pos = s0 + jnp.arange(qb_len)
        own = s0 // MOBA_BLOCK
        kb, vb = kblk[b], vblk[b]
        k_own = lax.dynamic_index_in_dim(kb, own, axis=0, keepdims=False)
        v_own = lax.dynamic_index_in_dim(vb, own, axis=0, keepdims=False)
        pos_own = own * MOBA_BLOCK + jnp.arange(MOBA_BLOCK)
        if n_sel == 0:
            return moba_core(qc, pos, None, None, None, k_own, v_own, pos_own)
        s_blk = jnp.einsum('qhd,nhd->qhn', qc.astype(jnp.float32), kmean[b])
        s_blk = jnp.where(jnp.arange(nb) < own, s_blk, NEG_INF)
        vals, sel = lax.top_k(s_blk, n_sel)
        valid = jnp.repeat(vals > NEG_INF / 2, MOBA_BLOCK, axis=-1)
        k_sel = kb[sel, :, h_idx, :].reshape(qb_len, MOBA_HEADS, n_sel * MOBA_BLOCK, MOBA_DH)
        v_sel = vb[sel, :, h_idx, :].reshape(qb_len, MOBA_HEADS, n_sel * MOBA_BLOCK, MOBA_DH)
        return moba_core(qc, pos, k_sel, v_sel, valid, k_own, v_own, pos_own)

    o = lax.map(one, (q_chunks, b_idx, q_start))
    return o.reshape(bsz, length, MOBA_W) @ w_out, k, v


def moba_sample(h, pool_k, pool_v, page_table, w_qkv, w_out):
    bsz, length, _ = h.shape
    q, k, v = [t.reshape(bsz, length, MOBA_HEADS, MOBA_DH) for t in jnp.split(h @ w_qkv, 3, axis=-1)]
    n_pages = page_table.shape[1]
    past_len = n_pages * PAGE_SIZE
    bp = MOBA_BLOCK // PAGE_SIZE
    nb_past = past_len // MOBA_BLOCK
    own = nb_past
    tail0 = own * bp
    n_sel = min(MOBA_TOPK, nb_past)
    h_idx = jnp.arange(MOBA_HEADS)[None, :, None, None]

    def one(xs):
        qc, kn, vn, tb = xs
        pos = past_len + jnp.arange(length)
        k_own = jnp.concatenate([pool_k[tb[tail0:]].reshape(-1, MOBA_HEADS, MOBA_DH), kn], axis=0)
        v_own = jnp.concatenate([pool_v[tb[tail0:]].reshape(-1, MOBA_HEADS, MOBA_DH), vn], axis=0)
        pos_own = own * MOBA_BLOCK + jnp.arange(k_own.shape[0])
        if n_sel == 0:
            return moba_core(qc, pos, None, None, None, k_own, v_own, pos_own)
        kmean = pool_k[tb[:tail0]].astype(jnp.float32).reshape(
            nb_past, MOBA_BLOCK, MOBA_HEADS, MOBA_DH).mean(axis=1)
        s_blk = jnp.einsum('qhd,nhd->qhn', qc.astype(jnp.float32), kmean)
        _, sel = lax.top_k(s_blk, n_sel)
        phys = tb[sel[..., None] * bp + jnp.arange(bp)]
        k_sel = pool_k[phys, :, h_idx, :].reshape(length, MOBA_HEADS, n_sel * MOBA_BLOCK, MOBA_DH)
        v_sel = pool_v[phys, :, h_idx, :].reshape(length, MOBA_HEADS, n_sel * MOBA_BLOCK, MOBA_DH)
        return moba_core(qc, pos, k_sel, v_sel, None, k_own, v_own, pos_own)

    o = lax.map(one, (q, k, v, page_table))
    return o.reshape(bsz, length, MOBA_W) @ w_out, k, v


def memory_kv(mem, g, w_kv):
    bsz, m_len, _ = mem.shape
    mk, mv = jnp.split(rmsnorm(mem, g) @ w_kv, 2, axis=-1)
    return (mk.reshape(bsz, m_len, XA_HEADS, XA_DH), mv.reshape(bsz, m_len, XA_HEADS, XA_DH))


def cross_attention(h, mk, mv, w_q, w_o):
    bsz, length, _ = h.shape
    q = (h @ w_q).reshape(bsz, length, XA_HEADS, XA_DH)
    s = jnp.einsum('blhd,bmhd->bhlm', q, mk).astype(jnp.float32) * XA_DH ** -0.5
    p = jax.nn.softmax(s, axis=-1).astype(mv.dtype)
    o = jnp.einsum('bhlm,bmhd->blhd', p, mv).reshape(bsz, length, XA_HEADS * XA_DH)
    return o @ w_o


def swiglu(h, w_up, w_down):
    gate, up = jnp.split(h @ w_up, 2, axis=-1)
    return (jax.nn.silu(gate) * up) @ w_down


def setup_inputs(seed: int = 0) -> dict:
    key = jax.random.key(seed)
    ks = jax.random.split(key, 40)
    ctr = [0]
    f32 = jnp.float32

    def nk():
        ctr[0] += 1
        return ks[ctr[0] - 1]

    def rnd(shape, scale):
        return jax.random.normal(nk(), shape, f32) * scale

    def gain(shape):
        return 1.0 + 0.02 * jax.random.normal(nk(), shape, f32)

    n_pages = PAST_LEN // PAGE_SIZE
    n_used = DEC_BATCH * n_pages
    n_pool = (5 * n_used + 3) // 4
    page_table = jax.random.permutation(nk(), n_pool)[:n_used].reshape(DEC_BATCH, n_pages).astype(jnp.int32)
    dt = jnp.exp(jax.random.uniform(nk(), (N_B, GDN_HEADS), f32, math.log(1e-3), math.log(1e-1)))
    return {
        'x_prompt': rnd((BATCH, SEQ, D_MODEL), 1.0),
        'x_sample': rnd((DEC_BATCH, DEC_SEQ, D_MODEL), 1.0),
        'state_a_conv': rnd((N_A, DEC_BATCH, CONV_A - 1, D_MODEL), 1.0),
        'state_b_conv': rnd((N_B, DEC_BATCH, GDN_CONV - 1, GDN_CONV_CH), 1.0),
        'state_b_rec': rnd((N_B, DEC_BATCH, GDN_HEADS, GDN_DK, GDN_DV), GDN_DK ** -0.5),
        'cache_c_k': rnd((N_C, n_pool, PAGE_SIZE, MOBA_HEADS, MOBA_DH), 1.0),
        'cache_c_v': rnd((N_C, n_pool, PAGE_SIZE, MOBA_HEADS, MOBA_DH), 1.0),
        'cache_mem_k': rnd((DEPTH, DEC_BATCH, MEM_LEN, XA_HEADS, XA_DH), 1.0),
        'cache_mem_v': rnd((DEPTH, DEC_BATCH, MEM_LEN, XA_HEADS, XA_DH), 1.0),
        'page_table': page_table,
        'mem_prompt': rnd((BATCH, MEM_LEN, D_MODEL), 1.0),
        'norm_mix': gain((DEPTH, D_MODEL)),
        'norm_mem': gain((DEPTH, D_MODEL)),
        'norm_xattn': gain((DEPTH, D_MODEL)),
        'norm_ffn': gain((DEPTH, D_MODEL)),
        'norm_final': gain((D_MODEL,)),
        'a_w_in': rnd((N_A, D_MODEL, 3 * D_MODEL), D_MODEL ** -0.5),
        'a_w_conv': rnd((N_A, CONV_A, D_MODEL), CONV_A ** -0.5),
        'a_w_out': rnd((N_A, D_MODEL, D_MODEL), D_MODEL ** -0.5),
        'b_w_in': rnd((N_B, D_MODEL, GDN_PROJ), D_MODEL ** -0.5),
        'b_w_conv': rnd((N_B, GDN_CONV, GDN_CONV_CH), GDN_CONV ** -0.5),
        'b_a_log': jnp.log(jax.random.uniform(nk(), (N_B, GDN_HEADS), f32, 1.0, 16.0)),
        'b_dt_bias': dt + jnp.log(-jnp.expm1(-dt)),
        'b_norm': gain((N_B, GDN_DV)),
        'b_w_out': rnd((N_B, GDN_V, D_MODEL), GDN_V ** -0.5),
        'c_w_qkv': rnd((N_C, D_MODEL, 3 * MOBA_W), D_MODEL ** -0.5),
        'c_w_out': rnd((N_C, MOBA_W, D_MODEL), MOBA_W ** -0.5),
        'x_w_q': rnd((DEPTH, D_MODEL, XA_HEADS * XA_DH), D_MODEL ** -0.5),
        'x_w_kv': rnd((DEPTH, D_MODEL, 2 * XA_HEADS * XA_DH), D_MODEL ** -0.5),
        'x_w_o': rnd((DEPTH, XA_HEADS * XA_DH, D_MODEL), (XA_HEADS * XA_DH) ** -0.5),
        'f_w_up': rnd((DEPTH, D_MODEL, 2 * D_FF), D_MODEL ** -0.5),
        'f_w_down': rnd((DEPTH, D_FF, D_MODEL), D_FF ** -0.5),
    }


def reference(x_prompt, x_sample, state_a_conv, state_b_conv, state_b_rec, cache_c_k, cache_c_v,
              cache_mem_k, cache_mem_v, page_table, mem_prompt,
              norm_mix, norm_mem, norm_xattn, norm_ffn, norm_final,
              a_w_in, a_w_conv, a_w_out,
              b_w_in, b_w_conv, b_a_log, b_dt_bias, b_norm, b_w_out,
              c_w_qkv, c_w_out, x_w_q, x_w_kv, x_w_o, f_w_up, f_w_down):
    xp, xs = x_prompt, x_sample
    bp_, bs_ = x_prompt.shape[0], x_sample.shape[0]
    a_p, a_s, bc_p, bc_s, br_p, br_s = [], [], [], [], [], []
    ck_p, cv_p, ck_s, cv_s, mk_p, mv_p = [], [], [], [], [], []
    for i in range(DEPTH):
        kind, j = i % N_MIXERS, i // N_MIXERS
        hp = rmsnorm(xp, norm_mix[i])
        hs = rmsnorm(xs, norm_mix[i])
        if kind == 0:
            zero_buf = jnp.zeros((bp_, CONV_A - 1, D_MODEL), xp.dtype)
            yp, nbp = shortconv_mixer(hp, zero_buf, a_w_in[j], a_w_conv[j], a_w_out[j])
            ys, nbs = shortconv_mixer(hs, state_a_conv[j], a_w_in[j], a_w_conv[j], a_w_out[j])
            a_p.append(nbp)
            a_s.append(nbs)
        elif kind == 1:
            zero_buf = jnp.zeros((bp_, GDN_CONV - 1, GDN_CONV_CH), xp.dtype)
            zero_s = jnp.zeros((bp_, GDN_HEADS, GDN_DK, GDN_DV), xp.dtype)
            yp, nbp, nsp = gdn_mixer(hp, zero_buf, zero_s, b_w_in[j], b_w_conv[j], b_a_log[j],
                                     b_dt_bias[j], b_norm[j], b_w_out[j])
            ys, nbs, nss = gdn_mixer(hs, state_b_conv[j], state_b_rec[j], b_w_in[j], b_w_conv[j],
                                     b_a_log[j], b_dt_bias[j], b_norm[j], b_w_out[j])
            bc_p.append(nbp)
            bc_s.append(nbs)
            br_p.append(nsp)
            br_s.append(nss)
        else:
            yp, kp, vp = moba_prompt(hp, c_w_qkv[j], c_w_out[j])
            ys, kn, vn = moba_sample(hs, cache_c_k[j], cache_c_v[j], page_table, c_w_qkv[j], c_w_out[j])
            ck_p.append(kp)
            cv_p.append(vp)
            ck_s.append(kn)
            cv_s.append(vn)
        xp = xp + yp
        xs = xs + ys
        mkp, mvp = memory_kv(mem_prompt, norm_mem[i], x_w_kv[i])
        mk_p.append(mkp)
        mv_p.append(mvp)
        xp = xp + cross_attention(rmsnorm(xp, norm_xattn[i]), mkp, mvp, x_w_q[i], x_w_o[i])
        xs = xs + cross_attention(rmsnorm(xs, norm_xattn[i]), cache_mem_k[i], cache_mem_v[i], x_w_q[i], x_w_o[i])
        xp = xp + swiglu(rmsnorm(xp, norm_ffn[i]), f_w_up[i], f_w_down[i])
        xs = xs + swiglu(rmsnorm(xs, norm_ffn[i]), f_w_up[i], f_w_down[i])
    y_prompt = rmsnorm(xp, norm_final)
    y_sample = rmsnorm(xs, norm_final)
    return (y_prompt, y_sample,
            jnp.stack(a_p), jnp.stack(a_s),
            jnp.stack(bc_p), jnp.stack(bc_s),
            jnp.stack(br_p), jnp.stack(br_s),
            jnp.stack(ck_p), jnp.stack(cv_p), jnp.stack(ck_s), jnp.stack(cv_s),
            jnp.stack(mk_p), jnp.stack(mv_p))
```

```python
from contextlib import ExitStack
import numpy as np
import concourse.bass as bass
import concourse.mybir as mybir
from concourse.bass_utils import run_bass_kernel_spmd

F32 = mybir.dt.float32
BF = mybir.dt.bfloat16
I32 = mybir.dt.int32
AF = mybir.ActivationFunctionType
ALU = mybir.AluOpType
AX = mybir.AxisListType

NCORES = 8
D = 1024
KC = 8
SEQ = 2048
NS = 16
LS = 4
NSAMP = NS * LS
NT = SEQ + NSAMP
TILES = [(0, 512), (512, 512), (1024, 512), (1536, 512), (2048, 64)]
DFF = 2816
FC = 22
MEM = 256
EPS = 1e-6
NEG = -30000.0


class Buf:
    __slots__ = ("name", "w", "r", "excl")

    def __init__(self, name, excl=False):
        self.name = name
        self.w = None
        self.r = {}
        self.excl = excl


class DSem:
    __slots__ = ("sem", "cnt")

    def __init__(self, sem):
        self.sem = sem
        self.cnt = 0


class KB:
    def __init__(self, nc, es):
        self.nc = nc
        self.es = es
        self.E = {"pe": nc.tensor, "act": nc.scalar, "dve": nc.vector, "pool": nc.gpsimd, "sp": nc.sync}
        self.sem = {}
        self.cnt = {}
        self.seen = {}
        for q in self.E:
            self.seen[q] = {}
        for e in ("pe", "act", "dve", "pool"):
            self.sem[e] = es.enter_context(nc.semaphore("s_" + e))
            self.cnt[e] = 0
        self.dsems = []
        self.n_ins = 0

    def dsem(self, name):
        d = DSem(self.es.enter_context(self.nc.semaphore("d_" + name)))
        self.dsems.append(d)
        return d

    def sb(self, name, shape, dt):
        return self.es.enter_context(self.nc.sbuf_tensor("sb_" + name, list(shape), dt))

    def _sync(self, q, reads, writes):
        need = {}
        for b in reads:
            if b.w is not None:
                k, v = b.w
                if need.get(k, 0) < v:
                    need[k] = v
        for b in writes:
            if b.w is not None:
                k, v = b.w
                if need.get(k, 0) < v:
                    need[k] = v
            for k, v in b.r.items():
                if need.get(k, 0) < v:
                    need[k] = v
        seen = self.seen[q]
        for k, v in need.items():
            if isinstance(k, DSem):
                v = k.cnt
                s = k.sem
            else:
                if k == q and q == "pe":
                    continue
                s = self.sem[k]
            if seen.get(k, 0) >= v:
                continue
            self.E[q].wait_ge(s, v)
            seen[k] = v
            self.n_ins += 1

    def op(self, eng, fn, reads=(), writes=(), inc=True):
        if eng != "pe":
            ex = [b for b in reads if b.excl]
            if ex:
                reads = [b for b in reads if not b.excl]
                writes = list(writes) + ex
        self._sync(eng, reads, writes)
        ins = fn(self.E[eng])
        self.n_ins += 1
        if inc:
            self.cnt[eng] += 1
            ins.then_inc(self.sem[eng], 1)
            v = self.cnt[eng]
        else:
            assert eng == "pe"
            v = self.cnt[eng] + 1
        for b in reads:
            b.r[eng] = v
        for b in writes:
            b.w = (eng, v)
            b.r = {}
        return ins

    def dma(self, q, out, in_, reads, writes, ds, **kw):
        self._sync(q, reads, writes)
        ins = self.E[q].dma_start(out=out, in_=in_, **kw)
        self.n_ins += 1
        ds.cnt += 16
        ins.then_inc(ds.sem, 16)
        for b in reads:
            b.r[ds] = ds.cnt
        for b in writes:
            b.w = (ds, ds.cnt)
            b.r = {}
        return ins

    def idma(self, out, in_, idx_ap, reads, writes, ds):
        self._sync("pool", reads, writes)
        ins = self.nc.gpsimd.indirect_dma_start(out=out, out_offset=None, in_=in_,
                                                in_offset=bass.IndirectOffsetOnAxis(ap=idx_ap, axis=0))
        self.n_ins += 1
        ds.cnt += 16
        ins.then_inc(ds.sem, 16)
        for b in reads:
            b.r[ds] = ds.cnt
        for b in writes:
            b.w = (ds, ds.cnt)
            b.r = {}
        return ins

    def barrier(self):
        for q in self.E:
            seen = self.seen[q]
            for e in ("pe", "act", "dve", "pool"):
                if e == q and q == "pe":
                    continue
                v = self.cnt[e]
                if v and seen.get(e, 0) < v:
                    self.E[q].wait_ge(self.sem[e], v)
                    seen[e] = v
                    self.n_ins += 1
            for d in self.dsems:
                if d.cnt and seen.get(d, 0) < d.cnt:
                    self.E[q].wait_ge(d.sem, d.cnt)
                    seen[d] = d.cnt
                    self.n_ins += 1

    def finish(self):
        for d in self.dsems:
            if d.cnt:
                self.nc.sync.wait_ge(d.sem, d.cnt)
        for e in ("pe", "act", "dve", "pool"):
            if self.cnt[e]:
                self.nc.sync.wait_ge(self.sem[e], self.cnt[e])


def build(depth=4, dbg=False, phases=('mix', 'xa', 'xa_k', 'xa_v', 'xa_p', 'xa_s', 'ffn')):
    nc = bass.Bass("TRN2", target_bir_lowering=False)
    es = ExitStack()
    kb = KB(nc, es)

    def din(name, shape, dt=F32):
        return nc.dram_tensor(name, list(shape), dt, kind="ExternalInput").ap()

    def dout(name, shape, dt=F32):
        return nc.dram_tensor(name, list(shape), dt, kind="ExternalOutput").ap()

    x0_d = din("x0", [128, KC, NT])
    memT_d = din("memT", [128, KC, MEM])
    par_d = din("par", [128, NPAR])
    cst_d = din("cst", [128, NCST])
    cstb_d = din("cstb", [128, NCSTB], BF)
    sa_d = din("sa", [128, 2, KC, NS, 2])
    cmk_d = din("cmk", [4, NS, 128, 8, MEM])
    cmv_d = din("cmv", [4, NS, 2, 128, D])
    a_w_in = din("a_w_in", [2, D, 3 * D])
    a_w_out = din("a_w_out", [2, D, D])
    x_w_q = din("x_w_q", [4, D, D])
    x_w_kv = din("x_w_kv", [4, D, 2 * D])
    x_w_o = din("x_w_o", [4, D, D])
    f_w_up = din("f_w_up", [4, D, 2 * DFF])
    f_w_down = din("f_w_down", [4, DFF, D])

    b_w_in = din("b_w_in", [1, D, 4112])
    b_w_out = din("b_w_out", [1, D, D])
    sbc_d = din("sbc", [24, 128, NS, 3])
    sbr_d = din("sbr", [NS, 8, 128, 128])
    gcst_d = din("gcst", [64, NGC])
    obc_p_d = dout("obc_p", [128, 24, 3])
    obc_s_d = dout("obc_s", [24, 128, NS, 3])
    obr_p_d = dout("obr_p", [8, 128, 128])
    obr_s_d = dout("obr_s", [NS, 8, 128, 128])
    c_w_qkv = din("c_w_qkv", [1, D, 3 * D])
    c_w_out = din("c_w_out", [1, D, D])
    pt_d = din("pt", [1, NS * 16], I32)
    poolk_d = din("poolk", [NPOOL * 128, D])
    poolv_d = din("poolv", [NPOOL * 128, D])
    ock_d = dout("ock", [128, 8, NT])
    ocv_d = dout("ocv", [NT, D])
    yT_d = dout("yT", [128, KC, NT])
    oa_d = dout("oa", [128, 2, KC, 2 + NS * 2])
    omk_d = dout("omk", [4, 128, KC, MEM])
    omv_d = dout("omv", [4, 2, 128, D])

    xT = kb.sb("xT", [128, KC, NT], F32)
    hT = kb.sb("hT", [128, KC, NT], BF)
    ARN = 16896
    Bg = kb.sb("Bg", [128, ARN], BF)
    Cg = kb.sb("Cg", [128, ARN], BF)

    def carve(arena, off, n, dt=BF):
        assert off % 4 == 0
        if dt == BF:
            assert off // 2 + n <= ARN
            return arena[:, off // 2: off // 2 + n]
        assert off // 2 + 2 * n <= ARN
        return arena[:, off // 2: off // 2 + 2 * n].bitcast(dt)

    par = kb.sb("par", [128, NPAR], F32)
    cst = kb.sb("cst", [128, NCST], F32)
    cstb = kb.sb("cstb", [128, NCSTB], BF)
    NW = 3
    wslots = [kb.sb("w%d" % i, [128, 2048], BF) for i in range(NW)]
    wbufs = [Buf("w%d" % i) for i in range(NW)]
    wds = [kb.dsem("w%d" % i) for i in range(NW)]
    wstate = {"i": 0}
    sq = kb.sb("sq", [128, KC, 512], BF)
    rstd = kb.sb("rstd", [128, 512], F32)
    t1 = kb.sb("t1", [128, 512], F32)
    t2 = kb.sb("t2", [128, 512], F32)
    t3 = kb.sb("t3", [128, 512], F32)
    ptb = kb.sb("ptb", [128, 2, 512], BF)
    pts = kb.sb("pts", [128, 2, 256], BF)
    big = Bg
    km = kb.sb("km", [128, 8, 8], F32)
    kmb = kb.sb("kmb", [128, 8, 8], BF)
    sel = kb.sb("sel", [128, 200], F32)
    selb = kb.sb("selb", [128, 64], BF)
    negT = kb.sb("negT", [64, 256], BF)
    tsel = kb.sb("tsel", [64, 256], BF)
    vsT = kb.sb("vsT", [128, 8, NSAMP], BF)
    qsamp = kb.sb("qsamp", [128, 8, NSAMP], BF)
    zbuf = carve(Cg, 0, 2 + SEQ, F32)
    zs = carve(Cg, 8200, NS * 6, F32).rearrange("p (s t) -> p s t", t=6)
    oa_sb = carve(Cg, 8584, 2 * KC * (2 + NS * 2), F32).rearrange("p (j c r) -> p j c r", j=2, c=KC)
    sa_sb = carve(Cg, 10760, 2 * KC * NS * 2, F32).rearrange("p (j c s r) -> p j c s r", j=2, c=KC, s=NS)
    mkT = carve(Cg, 0, 8 * MEM).rearrange("p (k m) -> p k m", k=8)
    mv = carve(Cg, 4096, 2 * D).rearrange("p (a n) -> p a n", a=2)
    skT = [carve(Cg, 8192 + i * 8192, 8 * MEM).rearrange("p (k m) -> p k m", k=8) for i in range(2)]
    sv = [carve(Cg, 12288 + i * 8192, 2 * D).rearrange("p (a n) -> p a n", a=2) for i in range(2)]
    memR = carve(Bg, 0, KC * MEM, F32).rearrange("p (k m) -> p k m", k=KC)
    memN = carve(Bg, 8192, KC * MEM).rearrange("p (k m) -> p k m", k=KC)
    mo32 = t3

    B = {n: Buf(n) for n in ("xT", "hT", "big", "par", "cst", "cstb", "sq", "rstd", "t1", "t2", "t3", "ptb", "pts", "memN",
                             "memR", "mkT", "mv", "mo32", "zbuf", "zs", "oa_sb", "sa_sb", "skT0", "skT1", "sv0", "sv1")}
    xB = [Buf("x%d" % i) for i in range(len(TILES))]
    hB = [Buf("h%d" % i) for i in range(len(TILES))]
    gB = [Buf("g%d" % i) for i in range(len(TILES))]

    psum = [es.enter_context(nc.psum_tensor("ps%d" % i, [128, 512], F32)) for i in range(8)]
    psB = [Buf("ps%d" % i, excl=True) for i in range(8)]
    pstate = {"i": 0}

    def bank():
        i = pstate["i"]
        pstate["i"] = (i + 1) % 6
        return psum[i], psB[i]

    def lbank(i):
        return psum[6 + i], psB[6 + i]

    d_in = kb.dsem("in")
    d_out = kb.dsem("out")
    d_kv = [kb.dsem("kv0"), kb.dsem("kv1")]
    d_kvv = [kb.dsem("kvv0"), kb.dsem("kvv1")]

    for ti, (t0, tn) in enumerate(TILES):
        kb.dma("sp", xT[:, :, t0:t0 + tn], x0_d[:, :, t0:t0 + tn], [], [xB[ti]], d_in)
    kb.dma("sp", par[:], par_d, [], [B["par"]], d_in)
    kb.dma("sp", cst[:], cst_d, [], [B["cst"]], d_in)
    kb.dma("sp", cstb[:], cstb_d, [], [B["cstb"]], d_in)

    ones_bf = cstb[:, C_ONES:C_ONES + 128]

    def mm(out, lhsT, rhs, start, stop, reads, writes, inc):
        return kb.op("pe", lambda e: e.matmul(out, lhsT=lhsT, rhs=rhs, start=start, stop=stop), reads, writes, inc)

    def wload(parts):
        i = wstate["i"]
        wstate["i"] = (i + 1) % NW
        nk = parts[0][1]
        tot = sum(p[2] for p in parts)
        assert nk * tot <= 2048
        view = wslots[i][:, 0:nk * tot].rearrange("p (k n) -> p k n", k=nk)
        c0 = 0
        for ap, nk_, ncols in parts:
            kb.dma("pool", view[:, :, c0:c0 + ncols], ap.rearrange("(k p) n -> p k n", p=128), [], [wbufs[i]], wds[i])
            c0 += ncols
        return view, wbufs[i]

    def rms_stats(src, srcB, t0, tn, nfeat_chunks=KC):
        kb.op("act", lambda e: e.activation(out=sq[:, :, :tn], in_=src[:, :, t0:t0 + tn], func=AF.Square),
              [srcB], [B["sq"]])
        ps, pb = bank()
        for k in range(KC):
            mm(ps[:, :tn], ones_bf, sq[:, k, :tn], k == 0, k == KC - 1, [B["sq"], B["cstb"]], [pb], k == KC - 1)
        kb.op("act", lambda e: e.activation(out=rstd[:, :tn], in_=ps[:, :tn], func=AF.Sqrt, bias=EPS, scale=1.0 / D),
              [pb], [B["rstd"]])
        kb.op("dve", lambda e: e.reciprocal(out=rstd[:, :tn], in_=rstd[:, :tn]), [], [B["rstd"]])

    def rmsnorm(gcol, dst=None, dstB=None, dst_dt_is_f32=False):
        dst = hT if dst is None else dst
        for ti, (t0, tn) in enumerate(TILES):
            rms_stats(xT, xB[ti], t0, tn)
            db = hB[ti] if dstB is None else dstB[ti]
            for k in range(KC):
                kb.op("dve", lambda e, k=k: e.scalar_tensor_tensor(
                    out=dst[:, k, t0:t0 + tn], in0=xT[:, k, t0:t0 + tn], scalar=par[:, gcol + k:gcol + k + 1],
                    in1=rstd[:, :tn], op0=ALU.mult, op1=ALU.mult), [xB[ti], B["rstd"], B["par"]], [db])

    def proj_fm(W, nk, n_oc, rhs_fn, evac, tiles=TILES, col0=0, row0=0):
        for og in range(0, n_oc, 2):
            ng = min(2, n_oc - og)
            view, wb = wload([(W[row0:row0 + nk * 128, col0 + og * 128: col0 + (og + ng) * 128], nk, ng * 128)])
            for j in range(ng):
                oc = og + j
                for ti, (t0, tn) in enumerate(tiles):
                    ps, pb = bank()
                    for k in range(nk):
                        rap, rb = rhs_fn(k, ti, t0, tn)
                        mm(ps[:, :tn], view[:, k, j * 128:(j + 1) * 128], rap, k == 0, k == nk - 1, [wb, rb], [pb], k == nk - 1)
                    evac(oc, ti, t0, tn, ps, pb)

    def h_rhs(k, ti, t0, tn):
        return hT[:, k, t0:t0 + tn], hB[ti]

    def add_to_x(oc, ti, t0, tn, ps, pb):
        kb.op("dve", lambda e: e.tensor_tensor(out=xT[:, oc, t0:t0 + tn], in0=xT[:, oc, t0:t0 + tn], in1=ps[:, :tn], op=ALU.add),
              [pb], [xB[ti]])

    def shortconv(li, j):
        yv = big[:, 0:KC * NT].rearrange("p (k t) -> p k t", k=KC)
        kb.barrier()
        kb.dma("sp", sa_sb[:], sa_d, [], [B["sa_sb"]], d_in)
        rmsnorm(P_NMIX + li * 8)
        kb.op("dve", lambda e: e.memset(zbuf[:, 0:2], 0.0), [], [B["zbuf"]])
        for c in range(KC):
            W = a_w_in[j]
            wv = [wload([(W[:, g * D + c * 128: g * D + (c + 1) * 128], KC, 128)]) for g in range(3)]
            wc = P_ACONV + (j * 3) * 8 + c
            kb.op("act", lambda e: e.copy(out=zs[:, :, 0:2], in_=sa_sb[:, j, c, :, :]), [B["sa_sb"]], [B["zs"]])
            for ti, (t0, tn) in enumerate(TILES):
                pss = []
                for g in range(3):
                    ps, pb = bank()
                    for k in range(KC):
                        mm(ps[:, :tn], wv[g][0][:, k, :], hT[:, k, t0:t0 + tn], k == 0, k == KC - 1,
                           [wv[g][1], hB[ti]], [pb], k == KC - 1)
                    pss.append((ps, pb))
                (pgb, bgb), (pgc, bgc), (pu, bu) = pss
                kb.op("act", lambda e: e.copy(out=t1[:, :tn], in_=pu[:, :tn]), [bu], [B["t1"]])
                kb.op("act", lambda e: e.copy(out=t2[:, :tn], in_=pgb[:, :tn]), [bgb], [B["t2"]])
                samp = t0 >= SEQ
                if not samp:
                    zw = zbuf[:, 2 + t0:2 + t0 + tn]
                    kb.op("dve", lambda e: e.tensor_tensor(out=zw, in0=pgc[:, :tn], in1=t1[:, :tn], op=ALU.mult),
                          [bgc, B["t1"]], [B["zbuf"]])
                    zsh = [zbuf[:, t0 + d:t0 + d + tn] for d in range(3)]
                    accv = t3[:, :tn]
                    t2v = t2[:, :tn]
                    yo = yv[:, c, t0:t0 + tn]
                    zB = B["zbuf"]
                else:
                    zw = zs[:, :, 2:6]
                    kb.op("dve", lambda e: e.tensor_tensor(out=zw, in0=pgc[:, :tn].rearrange("p (s t) -> p s t", t=LS),
                                                           in1=t1[:, :tn].rearrange("p (s t) -> p s t", t=LS), op=ALU.mult),
                          [bgc, B["t1"]], [B["zs"]])
                    zsh = [zs[:, :, d:d + 4] for d in range(3)]
                    accv = t3[:, :tn].rearrange("p (s t) -> p s t", t=LS)
                    t2v = t2[:, :tn].rearrange("p (s t) -> p s t", t=LS)
                    yo = yv[:, c, t0:t0 + tn].rearrange("p (s t) -> p s t", t=LS)
                    zB = B["zs"]
                kb.op("dve", lambda e: e.tensor_scalar(out=accv, in0=zsh[0], scalar1=par[:, wc:wc + 1], scalar2=None, op0=ALU.mult),
                      [zB, B["par"]], [B["t3"]])
                for d in (1, 2):
                    kb.op("dve", lambda e, d=d: e.scalar_tensor_tensor(out=accv, in0=zsh[d], scalar=par[:, wc + 8 * d:wc + 8 * d + 1],
                                                                     in1=accv, op0=ALU.mult, op1=ALU.add),
                          [zB, B["par"]], [B["t3"]])
                kb.op("dve", lambda e: e.tensor_tensor(out=yo, in0=accv, in1=t2v, op=ALU.mult), [B["t2"]], [B["t3"], gB[ti]])
            kb.op("act", lambda e: e.copy(out=oa_sb[:, j, c, 0:2], in_=zbuf[:, SEQ:SEQ + 2]), [B["zbuf"]], [B["oa_sb"]])
            kb.op("act", lambda e: e.copy(out=oa_sb[:, j, c, 2:2 + 2 * NS].rearrange("p (s r) -> p s r", r=2), in_=zs[:, :, 4:6]),
                  [B["zs"]], [B["oa_sb"]])
        kb.dma("sp", oa_d[:, j], oa_sb[:, j], [B["oa_sb"]], [], d_out)
        proj_fm(a_w_out[j], KC, KC, lambda k, ti, t0, tn: (yv[:, k, t0:t0 + tn], gB[ti]), add_to_x)


    def gdn(li):
        kb.barrier()
        W = b_w_in[0]
        BT = 256
        ident_bf = cstb[:, C_IDB:C_IDB + 128]

        def v3(ap, h=8):
            return ap.rearrange("p (h t) -> p h t", h=h)
        qT = v3(carve(Bg, 0, 8 * BT)); kT = v3(carve(Bg, 4096, 8 * BT)); vT = v3(carve(Bg, 8192, 8 * BT)); zT = v3(carve(Bg, 12288, 8 * BT))
        oraw = v3(carve(Bg, 16384, 8 * BT))
        Tb = carve(Bg, 20480, 512); bgk = carve(Bg, 21504, 1024); kdec = carve(Bg, 23552, 1024); bv = carve(Bg, 25600, 1024)
        nkdT = carve(Bg, 27648, 512); w_sb = carve(Bg, 28672, 1024); aqkT = carve(Bg, 30720, 512); qgT = carve(Bg, 31744, 512)
        ktm = ptb[:, :, :].rearrange("p a n -> p (a n)")
        egr = rstd
        raw = carve(Cg, 0, 260, F32); carry = carve(Cg, 1040, 72, F32).rearrange("p (c r) -> p c r", r=3)
        S = carve(Cg, 1328, 1024, F32); Sb = carve(Cg, 5424, 1024)
        negGU = carve(Cg, 7472, 512, F32); decLs = carve(Cg, 9520, 512, F32); decU = carve(Cg, 11568, 512, F32)
        LN = [carve(Cg, 13616 + i * 2048, 512) for i in range(4)]
        Q = carve(Cg, 21808, 512)
        gc = carve(Cg, 23856, NGC, F32)
        sm = carve(Cg, 23856 + 4 * NGC, 160, F32)
        nb = {n: Buf("g_" + n) for n in ("q", "k", "v", "z", "oraw", "Tb", "bgk", "kdec", "bv", "nkdT", "w", "aqkT", "qgT", "ktm",
                                         "raw", "carry", "S", "Sb", "negGU", "decLs", "decU", "L0", "L1", "L2", "L3", "Q", "gc",
                                         "ba", "beta", "g", "gam", "egam", "ekd", "bg", "egl", "negA", "xa")}
        LNB = [nb["L0"], nb["L1"], nb["L2"], nb["L3"]]
        kb.dma("sp", gc[0:64, :], gcst_d, [], [nb["gc"]], d_in)
        IDf = gc[:, G_ID:G_ID + 64]; IDrep = gc[:, G_IDREP:G_IDREP + 512]; Ucs = gc[:, G_U:G_U + 64]; negU = gc[:, G_NEGU:G_NEGU + 64]
        ones64 = gc[:, G_ONES:G_ONES + 128]; negones64 = gc[:, G_NEGONES:G_NEGONES + 128]
        Mlow = gc[:, G_MLOW:G_MLOW + 512]; Mup = gc[:, G_MUP:G_MUP + 512]
        ba_sb = sm[:, 0:16]; beta = sm[:, 16:24]; gg = sm[:, 24:32]; gam = sm[:, 32:40]; egam = sm[:, 40:48]; ekd = sm[:, 48:56]
        bgs = sm[:, 56:64]; egl = sm[:, 64:72]; negA = sm[:, 72:80]; xa = sm[:, 80:88]
        dtb = par[:, P_DTB:P_DTB + 8]
        kb.op("act", lambda e: e.activation(out=negA[:, :], in_=par[:, P_ALOG:P_ALOG + 8], func=AF.Exp), [B["par"]], [nb["negA"]])
        kb.op("dve", lambda e: e.tensor_scalar(out=negA[:, :], in0=negA[:, :], scalar1=-1.0, scalar2=None, op0=ALU.mult), [], [nb["negA"]])
        kb.op("dve", lambda e: e.memset(carry[:, :, :], 0.0), [], [nb["carry"]])
        kb.op("dve", lambda e: e.memset(S[:, :], 0.0), [], [nb["S"]])
        kb.op("dve", lambda e: e.memset(Sb[:, :], 0.0), [], [nb["Sb"]])

        def r3(ap, c):
            return ap

        def rep(cap, c):
            return cap.rearrange("p (h j) -> p h j", h=8)[0:c, :, 0:c]

        def fl(ap, c):
            return ap[0:c, 0:8 * c].rearrange("p (h j) -> p h j", h=8)

        def bc8(v, c, n):
            return v[0:c, :].unsqueeze(2).to_broadcast([c, 8, n])

        def chunk(c, col0, ba_ps, ba_b, hcols_unused=None):
            nlev = {64: 5, 4: 1}[c]
            kb.op("act", lambda e: e.activation(out=beta[0:c, :], in_=ba_ps[:, 0:8], func=AF.Sigmoid), [ba_b], [nb["beta"]])
            kb.op("dve", lambda e: e.tensor_tensor(out=xa[0:c, :], in0=ba_ps[:, 8:16], in1=dtb[0:c, :], op=ALU.add), [ba_b, B["par"]], [nb["xa"]])
            kb.op("act", lambda e: e.activation(out=xa[0:c, :], in_=xa[0:c, :], func=AF.Exp), [], [nb["xa"]])
            kb.op("act", lambda e: e.activation(out=xa[0:c, :], in_=xa[0:c, :], func=AF.Ln, bias=1.0, scale=1.0), [], [nb["xa"]])
            kb.op("dve", lambda e: e.tensor_tensor(out=gg[0:c, :], in0=xa[0:c, :], in1=negA[0:c, :], op=ALU.mult), [nb["xa"], nb["negA"]], [nb["g"]])
            pA, bA = bank()
            mm(pA[0:c, 0:8], Ucs[0:c, 0:c], gg[0:c, :], True, True, [nb["gc"], nb["g"]], [bA], False)
            mm(pA[0:c, 8:16], ones64[0:c, 0:c], gg[0:c, :], True, True, [nb["gc"], nb["g"]], [bA], False)
            mm(pA[:, 16:24], ones64[0:c, 0:128], gg[0:c, :], True, True, [nb["gc"], nb["g"]], [bA], True)
            kb.op("act", lambda e: e.copy(out=gam[0:c, :], in_=pA[0:c, 0:8]), [bA], [nb["gam"]])
            kb.op("act", lambda e: e.activation(out=egam[0:c, :], in_=pA[0:c, 0:8], func=AF.Exp), [bA], [nb["egam"]])
            kb.op("dve", lambda e: e.tensor_tensor(out=ekd[0:c, :], in0=pA[0:c, 8:16], in1=gam[0:c, :], op=ALU.subtract), [bA, nb["gam"]], [nb["ekd"]])
            kb.op("act", lambda e: e.activation(out=ekd[0:c, :], in_=ekd[0:c, :], func=AF.Exp), [], [nb["ekd"]])
            kb.op("dve", lambda e: e.tensor_tensor(out=bgs[0:c, :], in0=beta[0:c, :], in1=egam[0:c, :], op=ALU.mult), [nb["beta"], nb["egam"]], [nb["bg"]])
            kb.op("act", lambda e: e.activation(out=egl[:, :], in_=pA[:, 16:24], func=AF.Exp), [bA], [nb["egl"]])
            kb.op("dve", lambda e: e.tensor_tensor(out=fl(negGU, c), in0=bc8(gg, c, c),
                                                   in1=negU[0:c, 0:c].unsqueeze(1).to_broadcast([c, 8, c]), op=ALU.mult),
                  [nb["g"], nb["gc"]], [nb["negGU"]])
            pD, bD = bank()
            for h in range(8):
                mm(pD[0:c, h * c:(h + 1) * c], ones64[0:c, 0:c], negGU[0:c, h * c:(h + 1) * c], True, False, [nb["gc"], nb["negGU"]], [bD], False)
                mm(pD[0:c, h * c:(h + 1) * c], negGU[0:c, h * c:(h + 1) * c], negones64[0:c, 0:c], False, True, [nb["gc"], nb["negGU"]], [bD], h == 7)
            kb.op("dve", lambda e: e.tensor_tensor(out=fl(decLs, c), in0=fl(pD, c), in1=rep(Mlow, c), op=ALU.add), [bD, nb["gc"]], [nb["decLs"]])
            kb.op("act", lambda e: e.activation(out=decLs[0:c, 0:8 * c], in_=decLs[0:c, 0:8 * c], func=AF.Exp), [], [nb["decLs"]])
            kb.op("dve", lambda e: e.scalar_tensor_tensor(out=fl(decU, c), in0=fl(pD, c), scalar=-1.0, in1=rep(Mup, c), op0=ALU.mult, op1=ALU.add),
                  [bD, nb["gc"]], [nb["decU"]])
            kb.op("act", lambda e: e.activation(out=decU[0:c, 0:8 * c], in_=decU[0:c, 0:8 * c], func=AF.Exp), [], [nb["decU"]])
            pK, bK = bank()
            for h in range(8):
                mm(pK[0:c, h * c:(h + 1) * c], kT[:, h, col0:col0 + c], kT[:, h, col0:col0 + c], True, True, [nb["k"]], [bK], h == 7)
            L0, N0 = LN[0], LN[1]
            kb.op("dve", lambda e: e.tensor_tensor(out=t2[0:c, 0:8 * c], in0=pK[0:c, 0:8 * c], in1=decLs[0:c, 0:8 * c], op=ALU.mult),
                  [bK, nb["decLs"]], [B["t2"]])
            kb.op("dve", lambda e: e.tensor_tensor(out=fl(L0, c), in0=fl(t2, c), in1=bc8(beta, c, c), op=ALU.mult), [nb["beta"], B["t2"]], [LNB[0]])
            pN_, bN = bank()
            pN = pN_[:, :].bitcast(BF)
            for h in range(8):
                kb.op("pe", lambda e, h=h: e.transpose(pN[0:c, h * c:(h + 1) * c], L0[0:c, h * c:(h + 1) * c], ident_bf[0:c, 0:c]),
                      [LNB[0], B["cstb"]], [bN], h == 7)
            kb.op("act", lambda e: e.copy(out=N0[0:c, 0:8 * c], in_=pN[0:c, 0:8 * c]), [bN], [LNB[1]])
            kb.op("dve", lambda e: e.tensor_tensor(out=fl(Q, c), in0=rep(IDrep, c), in1=fl(pN, c), op=ALU.subtract), [bN, nb["gc"]], [nb["Q"]])
            li_, ni_ = 0, 1
            for lv in range(nlev):
                last = lv == nlev - 1
                Lp, Np = LN[li_], LN[ni_]
                free = [i for i in range(4) if i not in (li_, ni_)]
                l2, n2 = free[0], free[1]
                pL, bL = bank()
                for h in range(8):
                    mm(pL[0:c, h * c:(h + 1) * c], Np[0:c, h * c:(h + 1) * c], Lp[0:c, h * c:(h + 1) * c], True, True, [LNB[li_], LNB[ni_]], [bL], h == 7)
                if not last:
                    pN2, bN2 = bank()
                    for h in range(8):
                        mm(pN2[0:c, h * c:(h + 1) * c], Lp[0:c, h * c:(h + 1) * c], Np[0:c, h * c:(h + 1) * c], True, True, [LNB[li_], LNB[ni_]], [bN2], h == 7)
                kb.op("act", lambda e: e.copy(out=LN[l2][0:c, 0:8 * c], in_=pL[0:c, 0:8 * c]), [bL], [LNB[l2]])
                if not last:
                    kb.op("dve", lambda e: e.tensor_copy(out=LN[n2][0:c, 0:8 * c], in_=pN2[0:c, 0:8 * c]), [bN2], [LNB[n2]])
                pQ, bQ = bank()
                for h in range(8):
                    mm(pQ[0:c, h * c:(h + 1) * c], LN[l2][0:c, h * c:(h + 1) * c], Q[0:c, h * c:(h + 1) * c], True, True, [LNB[l2], nb["Q"]], [bQ], h == 7)
                kb.op("dve", lambda e: e.tensor_tensor(out=Q[0:c, 0:8 * c], in0=Q[0:c, 0:8 * c], in1=pQ[0:c, 0:8 * c], op=ALU.add), [bQ], [nb["Q"]])
                li_, ni_ = l2, n2
            Tb = Q
            nb["Tb"] = nb["Q"]
            pT, bT = bank()
            pTb = pT[:, :].bitcast(BF)
            for h in range(8):
                kb.op("pe", lambda e, h=h: e.transpose(pTb[0:c, h * 128:(h + 1) * 128], kT[:, h, col0:col0 + c], ident_bf),
                      [nb["k"], B["cstb"]], [bT], h == 7)
            kb.op("act", lambda e: e.copy(out=ktm[0:c, :], in_=pTb[0:c, :]), [bT], [nb["ktm"]])
            k3 = ktm[0:c, :].rearrange("p (h d) -> p h d", h=8)
            kb.op("dve", lambda e: e.tensor_tensor(out=bgk[0:c, :].rearrange("p (h d) -> p h d", h=8), in0=k3, in1=bc8(bgs, c, 128), op=ALU.mult),
                  [nb["ktm"], nb["bg"]], [nb["bgk"]])
            kb.op("dve", lambda e: e.tensor_tensor(out=kdec[0:c, :].rearrange("p (h d) -> p h d", h=8), in0=k3, in1=bc8(ekd, c, 128), op=ALU.mult),
                  [nb["ktm"], nb["ekd"]], [nb["kdec"]])
            pV, bV = bank()
            pVb = pV[:, :].bitcast(BF)
            for h in range(8):
                kb.op("pe", lambda e, h=h: e.transpose(pVb[0:c, h * 128:(h + 1) * 128], vT[:, h, col0:col0 + c], ident_bf),
                      [nb["v"], B["cstb"]], [bV], h == 7)
            kb.op("dve", lambda e: e.tensor_tensor(out=bv[0:c, :].rearrange("p (h d) -> p h d", h=8),
                                                   in0=pVb[0:c, :].rearrange("p (h d) -> p h d", h=8), in1=bc8(beta, c, 128), op=ALU.mult),
                  [bV, nb["beta"]], [nb["bv"]])
            pKd, bKd = bank()
            for h in range(8):
                mm(pKd[:, h * c:(h + 1) * c], bgk[0:c, h * 128:(h + 1) * 128], Tb[0:c, h * c:(h + 1) * c], True, True, [nb["bgk"], nb["Tb"]], [bKd], h == 7)
            kb.op("act", lambda e: e.mul(out=nkdT[:, 0:8 * c], in_=pKd[:, 0:8 * c], mul=-1.0), [bKd], [nb["nkdT"]])
            pW = [bank(), bank()]
            for h in range(8):
                pw, bw = pW[h // 4]
                hh = h % 4
                mm(pw[0:c, hh * 128:(hh + 1) * 128], Tb[0:c, h * c:(h + 1) * c], bv[0:c, h * 128:(h + 1) * 128], True, False, [nb["Tb"], nb["bv"]], [bw], False)
                mm(pw[0:c, hh * 128:(hh + 1) * 128], nkdT[:, h * c:(h + 1) * c], Sb[:, h * 128:(h + 1) * 128], False, True, [nb["nkdT"], nb["Sb"]], [bw], hh == 3)
            for i2 in range(2):
                kb.op("act", lambda e, i2=i2: e.copy(out=w_sb[0:c, i2 * 512:(i2 + 1) * 512], in_=pW[i2][0][0:c, :]), [pW[i2][1]], [nb["w"]])
            pA2, bA2 = bank()
            for h in range(8):
                mm(pA2[0:c, h * c:(h + 1) * c], kT[:, h, col0:col0 + c], qT[:, h, col0:col0 + c], True, True, [nb["k"], nb["q"]], [bA2], h == 7)
            kb.op("dve", lambda e: e.tensor_tensor(out=aqkT[0:c, 0:8 * c], in0=pA2[0:c, 0:8 * c], in1=decU[0:c, 0:8 * c], op=ALU.mult),
                  [bA2, nb["decU"]], [nb["aqkT"]])
            pE, bE = bank()
            mm(pE[:, 0:8 * c], negones64[0:c, 0:128], negGU[0:c, 0:8 * c], True, True, [nb["gc"], nb["negGU"]], [bE], True)
            kb.op("act", lambda e: e.activation(out=egr[:, 0:8 * c], in_=pE[:, 0:8 * c], func=AF.Exp), [bE], [B["rstd"]])
            kb.op("dve", lambda e: e.tensor_tensor(out=qgT[:, 0:8 * c].rearrange("p (h j) -> p h j", h=8), in0=qT[:, :, col0:col0 + c],
                                                   in1=egr[:, 0:8 * c].rearrange("p (h j) -> p h j", h=8), op=ALU.mult),
                  [nb["q"], B["rstd"]], [nb["qgT"]])
            pO, bO = bank()
            for h in range(8):
                mm(pO[:, h * c:(h + 1) * c], Sb[:, h * 128:(h + 1) * 128], qgT[:, h * c:(h + 1) * c], True, False, [nb["Sb"], nb["qgT"]], [bO], False)
                mm(pO[:, h * c:(h + 1) * c], w_sb[0:c, h * 128:(h + 1) * 128], aqkT[0:c, h * c:(h + 1) * c], False, True, [nb["w"], nb["aqkT"]], [bO], h == 7)
            kb.op("act", lambda e: e.copy(out=oraw[:, :, col0:col0 + c], in_=pO[:, 0:8 * c].rearrange("p (h j) -> p h j", h=8)), [bO], [nb["oraw"]])
            pS2 = [bank(), bank()]
            for h in range(8):
                p2, b2 = pS2[h // 4]
                hh = h % 4
                mm(p2[:, hh * 128:(hh + 1) * 128], kdec[0:c, h * 128:(h + 1) * 128], w_sb[0:c, h * 128:(h + 1) * 128], True, True, [nb["kdec"], nb["w"]], [b2], hh == 3)
            kb.op("dve", lambda e: e.tensor_tensor(out=S[:, :].rearrange("p (h d) -> p h d", h=8), in0=S[:, :].rearrange("p (h d) -> p h d", h=8),
                                                   in1=egl[:, :].unsqueeze(2).to_broadcast([128, 8, 128]), op=ALU.mult), [nb["egl"]], [nb["S"]])
            for i2 in range(2):
                kb.op("dve", lambda e, i2=i2: e.tensor_tensor(out=S[:, i2 * 512:(i2 + 1) * 512], in0=S[:, i2 * 512:(i2 + 1) * 512], in1=pS2[i2][0][:, :], op=ALU.add),
                      [pS2[i2][1]], [nb["S"]])
            kb.op("act", lambda e: e.copy(out=Sb[:, :], in_=S[:, :]), [nb["S"]], [nb["Sb"]])

        def in_proj(ti, t0, tn, samp):
            if samp:
                raws = raw[:, 0:NS * 7].rearrange("p (s t) -> p s t", t=7)
            for cg in range(0, 24, 2):
                view, wb = wload([(W[:, cg * 128:(cg + 2) * 128], KC, 256)])
                for j2 in range(2):
                    cc = cg + j2
                    ps, pb = bank()
                    for k in range(KC):
                        mm(ps[:, :tn], view[:, k, j2 * 128:(j2 + 1) * 128], hT[:, k, t0:t0 + tn], k == 0, k == KC - 1, [wb, hB[ti]], [pb], k == KC - 1)
                    wc = P_BCONV + cc
                    if not samp:
                        kb.op("act", lambda e: e.copy(out=raw[:, 0:3], in_=carry[:, cc, :]), [nb["carry"]], [nb["raw"]])
                        kb.op("act", lambda e: e.copy(out=raw[:, 3:3 + tn], in_=ps[:, :tn]), [pb], [nb["raw"]])
                        kb.op("act", lambda e: e.copy(out=carry[:, cc, :], in_=raw[:, tn:tn + 3]), [nb["raw"]], [nb["carry"]])
                        win = [raw[:, d:d + tn] for d in range(4)]
                        acc = t3[:, :tn]
                    else:
                        kb.dma("sp", raws[:, :, 0:3], sbc_d[cc], [], [nb["raw"]], d_in)
                        kb.op("act", lambda e: e.copy(out=raws[:, :, 3:7], in_=ps[:, :tn].rearrange("p (s t) -> p s t", t=LS)), [pb], [nb["raw"]])
                        kb.dma("sp", obc_s_d[cc], raws[:, :, 4:7], [nb["raw"]], [], d_out)
                        win = [raws[:, :, d:d + 4] for d in range(4)]
                        acc = t3[:, :tn].rearrange("p (s t) -> p s t", t=LS)
                    kb.op("dve", lambda e: e.tensor_scalar(out=acc, in0=win[0], scalar1=par[:, wc:wc + 1], scalar2=None, op0=ALU.mult),
                          [nb["raw"], B["par"]], [B["t3"]])
                    for d in (1, 2, 3):
                        kb.op("dve", lambda e, d=d: e.scalar_tensor_tensor(out=acc, in0=win[d], scalar=par[:, wc + 24 * d:wc + 24 * d + 1], in1=acc,
                                                                         op0=ALU.mult, op1=ALU.add), [nb["raw"], B["par"]], [B["t3"]])
                    h = cc % 8
                    if cc >= 16:
                        kb.op("act", lambda e: e.activation(out=vT[:, h, 0:tn], in_=t3[:, :tn], func=AF.Silu), [B["t3"]], [nb["v"]])
                    else:
                        dst, dB = (qT, nb["q"]) if cc < 8 else (kT, nb["k"])
                        kb.op("act", lambda e: e.activation(out=t2[:, :tn], in_=t3[:, :tn], func=AF.Silu), [B["t3"]], [B["t2"]])
                        kb.op("act", lambda e: e.activation(out=sq[:, 0, :tn], in_=t2[:, :tn], func=AF.Square), [B["t2"]], [B["sq"]])
                        p2, b2 = bank()
                        mm(p2[:, :tn], ones_bf, sq[:, 0, :tn], True, True, [B["sq"], B["cstb"]], [b2], True)
                        kb.op("act", lambda e: e.activation(out=t1[:, :tn], in_=p2[:, :tn], func=AF.Sqrt, bias=EPS, scale=1.0), [b2], [B["t1"]])
                        kb.op("dve", lambda e: e.reciprocal(out=t1[:, :tn], in_=t1[:, :tn]), [], [B["t1"]])
                        scl = (128.0 ** -0.5) if cc < 8 else 1.0
                        kb.op("dve", lambda e: e.scalar_tensor_tensor(out=dst[:, h, 0:tn], in0=t2[:, :tn], scalar=scl, in1=t1[:, :tn],
                                                                      op0=ALU.mult, op1=ALU.mult), [B["t2"], B["t1"]], [dB])
            for zg in range(0, 8, 2):
                view, wb = wload([(W[:, 3072 + zg * 128:3072 + (zg + 2) * 128], KC, 256)])
                for j2 in range(2):
                    ps, pb = bank()
                    for k in range(KC):
                        mm(ps[:, :tn], view[:, k, j2 * 128:(j2 + 1) * 128], hT[:, k, t0:t0 + tn], k == 0, k == KC - 1, [wb, hB[ti]], [pb], k == KC - 1)
                    kb.op("act", lambda e: e.activation(out=zT[:, zg + j2, 0:tn], in_=ps[:, :tn], func=AF.Silu), [pb], [nb["z"]])

        def out_norm(ti, t0, tn):
            kb.op("act", lambda e: e.activation(out=sq[:, :, :tn], in_=oraw[:, :, 0:tn], func=AF.Square), [nb["oraw"]], [B["sq"]])
            for h in range(8):
                p2, b2 = bank()
                mm(p2[:, :tn], ones_bf, sq[:, h, :tn], True, True, [B["sq"], B["cstb"]], [b2], True)
                kb.op("act", lambda e: e.activation(out=t1[:, :tn], in_=p2[:, :tn], func=AF.Sqrt, bias=EPS, scale=1.0 / 128.0), [b2], [B["t1"]])
                kb.op("dve", lambda e: e.reciprocal(out=t1[:, :tn], in_=t1[:, :tn]), [], [B["t1"]])
                kb.op("dve", lambda e: e.scalar_tensor_tensor(out=t2[:, :tn], in0=oraw[:, h, 0:tn], scalar=par[:, P_BNORM:P_BNORM + 1], in1=t1[:, :tn],
                                                              op0=ALU.mult, op1=ALU.mult), [nb["oraw"], B["t1"], B["par"]], [B["t2"]])
                kb.op("dve", lambda e: e.tensor_tensor(out=hT[:, h, t0:t0 + tn], in0=t2[:, :tn], in1=zT[:, h, 0:tn], op=ALU.mult),
                      [B["t2"], nb["z"]], [hB[ti]])

        rmsnorm(P_NMIX + li * 8)
        for blk in range(SEQ // BT):
            t0 = blk * BT
            ti = t0 // 512
            in_proj(ti, t0, BT, False)
            wba, wbb = wload([(W[:, 4096:4112], KC, 16)])
            pBA, bBA = bank()
            for ci in range(BT // 64):
                for k in range(KC):
                    mm(pBA[0:64, ci * 16:(ci + 1) * 16], hT[:, k, t0 + ci * 64:t0 + (ci + 1) * 64], wba[:, k, :], k == 0, k == KC - 1,
                       [wbb, hB[ti]], [bBA], k == KC - 1 and ci == BT // 64 - 1)
            kb.op("act", lambda e: e.copy(out=t1[0:64, 0:64], in_=pBA[0:64, 0:64]), [bBA], [B["t1"]])
            for ci in range(BT // 64):
                chunk(64, ci * 64, t1[0:64, ci * 16:(ci + 1) * 16], B["t1"])
            out_norm(ti, t0, BT)
        kb.dma("sp", obc_p_d, carry[:, :, :], [nb["carry"]], [], d_out)
        kb.dma("sp", obr_p_d.rearrange("h k v -> k h v"), S[:, :].rearrange("p (h d) -> p h d", h=8), [nb["S"]], [], d_out)
        in_proj(4, SEQ, NSAMP, True)
        wba, wbb = wload([(W[:, 4096:4112], KC, 16)])
        pBA, bBA = bank()
        for s_ in range(NS):
            for k in range(KC):
                mm(pBA[0:4, s_ * 16:(s_ + 1) * 16], hT[:, k, SEQ + s_ * 4:SEQ + s_ * 4 + 4], wba[:, k, :], k == 0, k == KC - 1,
                   [wbb, hB[4]], [bBA], k == KC - 1 and s_ == NS - 1)
        kb.op("act", lambda e: e.copy(out=t1[0:4, 0:256], in_=pBA[0:4, 0:256]), [bBA], [B["t1"]])
        for s_ in range(NS):
            kb.dma("sp", S[:, :].rearrange("p (h d) -> p h d", h=8), sbr_d[s_].rearrange("h k v -> k h v"), [], [nb["S"]], d_in)
            kb.op("act", lambda e: e.copy(out=Sb[:, :], in_=S[:, :]), [nb["S"]], [nb["Sb"]])
            chunk(4, s_ * 4, t1[0:4, s_ * 16:(s_ + 1) * 16], B["t1"])
            kb.dma("sp", obr_s_d[s_].rearrange("h k v -> k h v"), S[:, :].rearrange("p (h d) -> p h d", h=8), [nb["S"]], [], d_out)
        out_norm(4, SEQ, NSAMP)
        proj_fm(b_w_out[0], KC, KC, h_rhs, add_to_x)


    def moba(li):
        kb.barrier()
        W = c_w_qkv[0]
        ident_bf = cstb[:, C_IDB:C_IDB + 128]
        scm = 128.0 ** -0.5
        KT = carve(Bg, 0, 8 * SEQ).rearrange("p (h t) -> p h t", h=8)
        ksamp = carve(Bg, 32768, 512).rearrange("p (h t) -> p h t", h=8)
        V = carve(Cg, 0, 16 * D).rearrange("p (a n) -> p a n", a=16)
        qb = sq[:, :, 0:256]
        P2 = ptb[:, :, :].rearrange("p a n -> p (a n)")[:, 0:512].rearrange("p (a n) -> p a n", a=2)
        nb = {n: Buf("m_" + n) for n in ("KT", "ksamp", "V", "vsamp", "qb", "km", "kmb", "sel", "selb", "negT", "tsel", "qs", "qrep",
                                         "kst", "ksb", "vst", "vall", "ksum", "kmsT", "idx", "selS", "PT", "pown")}
        rmsnorm(P_NMIX + li * 8)
        def evac_kT(oc, ti, t0, tn, ps, pb):
            if t0 < SEQ:
                kb.op("act", lambda e: e.copy(out=KT[:, oc, t0:t0 + tn], in_=ps[:, :tn]), [pb], [nb["KT"]])
            else:
                kb.op("act", lambda e: e.copy(out=ksamp[:, oc, :], in_=ps[:, :tn]), [pb], [nb["ksamp"]])
            kb.op("dve", lambda e: e.tensor_copy(out=t3[:, :tn], in_=ps[:, :tn]), [pb], [B["t3"]])
            kb.dma("sp", ock_d[:, oc, t0:t0 + tn], t3[:, :tn], [B["t3"]], [], d_out)
        proj_fm(W, KC, 8, h_rhs, evac_kT, col0=D)
        for cg in range(0, D, 256):
            view, wb = wload([(W[:, 2 * D + cg:2 * D + cg + 256], KC, 256)])
            for tt in range(17):
                m = 128 if tt < 16 else NSAMP
                ti = min(tt // 4, 4)
                ps, pb = bank()
                for k in range(KC):
                    mm(ps[0:m, :256], hT[:, k, tt * 128:tt * 128 + m], view[:, k, :], k == 0, k == KC - 1, [wb, hB[ti]], [pb], k == KC - 1)
                if tt < 16:
                    kb.op("act", lambda e: e.copy(out=V[:, tt, cg:cg + 256], in_=ps[:, :256]), [pb], [nb["V"]])
                kb.op("dve", lambda e: e.tensor_copy(out=t3[0:m, 0:256], in_=ps[0:m, :256]), [pb], [B["t3"]])
                kb.dma("sp", ocv_d[tt * 128:tt * 128 + m, cg:cg + 256], t3[0:m, 0:256], [B["t3"]], [], d_out)
        def evac_vs(oc, ti, t0, tn, ps, pb):
            kb.op("act", lambda e: e.copy(out=vsT[:, oc, :], in_=ps[:, :tn]), [pb], [nb["vsamp"]])
        proj_fm(W, KC, 8, lambda k, ti, t0, tn: (hT[:, k, t0:t0 + tn], hB[4]), evac_vs, tiles=[(SEQ, NSAMP)], col0=2 * D)
        kb.op("dve", lambda e: e.tensor_reduce(out=km[:, :, :], in_=KT[:, :, :].rearrange("p h (n t) -> p h n t", t=256), axis=AX.X, op=ALU.add),
              [nb["KT"]], [nb["km"]])
        kb.op("act", lambda e: e.mul(out=kmb[:, :, :], in_=km[:, :, :], mul=1.0 / 256.0), [nb["km"]], [nb["kmb"]])
        for b in range(8):
            q0 = b * 256
            ti = q0 // 512
            for hg in range(0, 8, 2):
                view, wb = wload([(W[:, hg * 128:(hg + 2) * 128], KC, 256)])
                for j2 in range(2):
                    ps, pb = bank()
                    for k in range(KC):
                        mm(ps[:, :256], view[:, k, j2 * 128:(j2 + 1) * 128], hT[:, k, q0:q0 + 256], k == 0, k == KC - 1, [wb, hB[ti]], [pb], k == KC - 1)
                    kb.op("act", lambda e: e.copy(out=qb[:, hg + j2, :], in_=ps[:, :256]), [pb], [nb["qb"]])
            if b >= 4:
                kb.op("dve", lambda e: e.memset(selb[:, :], 0.0), [], [nb["selb"]])
                pT, bT = bank()
                pTb = pT[:, :].bitcast(BF)
                for half in range(2):
                    pSel, bSel = bank()
                    for h in range(8):
                        mm(pSel[:, h * 8:h * 8 + b], qb[:, h, half * 128:(half + 1) * 128], kmb[:, h, 0:b], True, True, [nb["qb"], nb["kmb"]], [bSel], h == 7)
                    s3 = sel[:, 0:8 * b].rearrange("p (h n) -> p h n", h=8)
                    s3b = sel[:, 64:64 + 8 * b].rearrange("p (h n) -> p h n", h=8)
                    e3 = sel[:, 128:128 + 8 * b].rearrange("p (h n) -> p h n", h=8)
                    mx = sel[:, 192:200]

                    def mxb():
                        return mx.unsqueeze(2).to_broadcast([128, 8, b])
                    kb.op("act", lambda e: e.copy(out=s3, in_=pSel[:, 0:64].rearrange("p (h n) -> p h n", h=8)[:, :, 0:b]), [bSel], [nb["sel"]])
                    cur = s3
                    for it in range(2):
                        kb.op("dve", lambda e: e.tensor_reduce(out=mx, in_=cur, axis=AX.X, op=ALU.max), [], [nb["sel"]])
                        kb.op("dve", lambda e: e.tensor_tensor(out=e3, in0=cur, in1=mxb(), op=ALU.is_equal), [], [nb["sel"]])
                        kb.op("dve", lambda e: e.scalar_tensor_tensor(out=s3b, in0=e3, scalar=-1e30, in1=cur, op0=ALU.mult, op1=ALU.add), [], [nb["sel"]])
                        cur = s3b
                    kb.op("dve", lambda e: e.tensor_reduce(out=mx, in_=s3b, axis=AX.X, op=ALU.max), [], [nb["sel"]])
                    kb.op("dve", lambda e: e.tensor_tensor(out=e3, in0=s3, in1=mxb(), op=ALU.is_ge), [], [nb["sel"]])
                    kb.op("dve", lambda e: e.tensor_scalar(out=selb[:, :].rearrange("p (h n) -> p h n", h=8)[:, :, 0:b], in0=e3, scalar1=-NEG, scalar2=NEG,
                                                           op0=ALU.mult, op1=ALU.add), [nb["sel"]], [nb["selb"]])
                    kb.op("pe", lambda e: e.transpose(pTb[0:64, half * 128:(half + 1) * 128], selb[:, :], ident_bf), [nb["selb"], B["cstb"]], [bT], True)
                kb.op("act", lambda e: e.copy(out=negT[:, :], in_=pTb[0:64, 0:256]), [bT], [nb["negT"]])
            for h in range(8):
                pOD, bOD = lbank(0)
                pDN, bDN = lbank(1)
                for n in range(b + 1):
                    pSc, bSc = bank()
                    extra = (n == b) or (b >= 4)
                    if n < b and b >= 4:
                        r = h * 8 + n
                        kb.op("dve", lambda e: e.tensor_scalar(out=tsel[:, :], in0=negT[:, :], scalar1=cst[0:64, r:r + 1], scalar2=None, op0=ALU.mult),
                              [nb["negT"], B["cst"]], [nb["tsel"]])
                    for kc in range(2):
                        mm(pSc[:, kc * 256:(kc + 1) * 256], KT[:, h, n * 256 + kc * 128:n * 256 + (kc + 1) * 128], qb[:, h, :], True, not extra,
                           [nb["KT"], nb["qb"]], [bSc], (not extra) and kc == 1)
                        if n == b:
                            mm(pSc[:, kc * 256:(kc + 1) * 256], ident_bf, cstb[:, C_CM + kc * 256:C_CM + (kc + 1) * 256], False, True,
                               [B["cstb"]], [bSc], kc == 1)
                        elif b >= 4:
                            mm(pSc[:, kc * 256:(kc + 1) * 256], cstb[0:64, C_ONES:C_ONES + 128], tsel[:, :], False, True,
                               [B["cstb"], nb["tsel"]], [bSc], kc == 1)
                    kb.op("act", lambda e: e.activation(out=P2, in_=pSc[:, :].rearrange("p (a n) -> p a n", a=2), func=AF.Exp, scale=scm), [bSc], [B["ptb"]])
                    for kc in range(2):
                        first = (n == 0 and kc == 0)
                        lastm = (n == b and kc == 1)
                        mm(pOD[:, 0:256], V[:, n * 2 + kc, h * 128:(h + 1) * 128], P2[:, kc, :], first, lastm, [nb["V"], B["ptb"]], [bOD], False)
                        mm(pDN[:, 0:256], ones_bf, P2[:, kc, :], first, lastm, [B["cstb"], B["ptb"]], [bDN], True)
                kb.op("dve", lambda e: e.reciprocal(out=t1[:, 0:256], in_=pDN[:, 0:256]), [bDN], [B["t1"]])
                kb.op("dve", lambda e: e.tensor_tensor(out=hT[:, h, q0:q0 + 256], in0=pOD[:, 0:256], in1=t1[:, 0:256], op=ALU.mult), [bOD, B["t1"]], [hB[ti]])
        kb.barrier()
        NSL = 3
        kst = [carve(Bg, i * 4096, 1024, F32) for i in range(NSL)]
        vst = [carve(Bg, 12288 + i * 4096, 1024, F32) for i in range(NSL)]
        ksb = [carve(Bg, 24576 + i * 2048, 1024).rearrange("p (h t) -> p h t", h=8) for i in range(2)]
        ptab_i = carve(Bg, 28672, 256, I32); ptab_f = carve(Bg, 29696, 256, F32); idx_i = carve(Bg, 30720, 256, I32)
        ksum = carve(Bg, 31744, 128, F32).rearrange("p (h g) -> p h g", h=8)
        kmsT = carve(Bg, 32256, 64, F32).rearrange("p (h n) -> p h n", h=8)
        vall = carve(Cg, 0, 16 * D).rearrange("p (a n) -> p a n", a=16)
        kbs = [Buf("kst%d" % i) for i in range(NSL)]; vbs = [Buf("vst%d" % i) for i in range(NSL)]; kbb = [Buf("ksb0"), Buf("ksb1")]
        kb.dma("sp", ptab_i[:, :], pt_d.partition_broadcast(128), [], [nb["idx"]], d_in)
        kb.op("dve", lambda e: e.tensor_copy(out=ptab_f[:, :], in_=ptab_i[:, :]), [], [nb["idx"]])
        kb.op("dve", lambda e: e.tensor_scalar(out=ptab_f[:, :], in0=ptab_f[:, :], scalar1=128.0, scalar2=cst[:, C_PID:C_PID + 1], op0=ALU.mult, op1=ALU.add),
              [B["cst"]], [nb["idx"]])
        kb.op("dve", lambda e: e.tensor_copy(out=idx_i[:, :], in_=ptab_f[:, :]), [], [nb["idx"]])
        qrb = sq[:, :, :].rearrange("p a n -> p (a n)").rearrange("p (j m) -> p j m", j=32)
        nb["qrep"] = B["sq"]
        kmsb = carve(Bg, 32512, 64).rearrange("p (h n) -> p h n", h=8)
        selS = carve(Cg, 32768, 256, F32)
        d_g = [kb.dsem("gk%d" % i) for i in range(NSL)]
        d_gv = [kb.dsem("gv%d" % i) for i in range(NSL)]
        for hg in range(0, 8, 2):
            view, wb = wload([(W[:, hg * 128:(hg + 2) * 128], KC, 256)])
            for j2 in range(2):
                ps, pb = bank()
                for k in range(KC):
                    mm(ps[:, :NSAMP], view[:, k, j2 * 128:(j2 + 1) * 128], hT[:, k, SEQ:NT], k == 0, k == KC - 1, [wb, hB[4]], [pb], k == KC - 1)
                kb.op("act", lambda e: e.copy(out=qsamp[:, hg + j2, :], in_=ps[:, :NSAMP]), [pb], [nb["qs"]])
        for s_ in range(NS):
            c0 = SEQ + s_ * 4
            pS, bS = lbank(0)
            for pg in range(16):
                i3 = (s_ * 16 + pg) % NSL
                i = pg % 2
                col = s_ * 16 + pg
                kb.idma(kst[i3][:, :], poolk_d, idx_i[:, col:col + 1], [nb["idx"]], [kbs[i3]], d_g[i3])
                kb.idma(vst[i3][:, :], poolv_d, idx_i[:, col:col + 1], [nb["idx"]], [vbs[i3]], d_gv[i3])
                kb.op("act", lambda e: e.copy(out=ksb[i][:, :, :], in_=kst[i3][:, :].rearrange("p (h t) -> p h t", h=8)), [kbs[i3]], [kbb[i]])
                kb.op("dve", lambda e: e.tensor_reduce(out=ksum[:, :, pg], in_=kst[i3][:, :].rearrange("p (h t) -> p h t", h=8), axis=AX.X, op=ALU.add),
                      [kbs[i3]], [nb["ksum"]])
                kb.op("dve", lambda e: e.tensor_copy(out=vall[:, pg, :], in_=vst[i3][:, :]), [vbs[i3]], [nb["vall"]])
                for h in range(8):
                    mm(pS[:, pg * 32 + h * 4:pg * 32 + h * 4 + 4], ksb[i][:, h, :], qsamp[:, h, s_ * 4:s_ * 4 + 4], True, True, [kbb[i], nb["qs"]], [bS], h == 7)
            kb.op("dve", lambda e: e.tensor_reduce(out=kmsT[:, :, :], in_=ksum[:, :, :].rearrange("p h (n two) -> p h n two", two=2), axis=AX.X, op=ALU.add),
                  [nb["ksum"]], [nb["kmsT"]])
            kb.op("act", lambda e: e.mul(out=kmsb[:, :, :], in_=kmsT[:, :, :], mul=1.0 / 256.0), [nb["kmsT"]], [nb["kmsT"]])
            kb.op("dve", lambda e: e.tensor_copy(out=qrb[:, :, :].rearrange("p (h q) m -> p h q m", h=8),
                                                 in_=qsamp[:, :, s_ * 4:s_ * 4 + 4].unsqueeze(3).to_broadcast([128, 8, 4, 128])), [nb["qs"]], [nb["qrep"]])
            pG, bG = bank()
            for j in range(32):
                mm(pG[:, j * 8:(j + 1) * 8], qrb[:, j, :], kmsb[:, j // 4, :], True, True, [nb["qrep"], nb["kmsT"]], [bG], j == 31)
            g3 = selS[:, 0:256].rearrange("p (j n) -> p j n", n=8)
            sA = t2[:, 0:256].rearrange("p (j n) -> p j n", n=8)
            sBv = t2[:, 256:512].rearrange("p (j n) -> p j n", n=8)
            eq = t3[:, 0:256].rearrange("p (j n) -> p j n", n=8)
            mx = t3[:, 256:288]

            def mxb():
                return mx.unsqueeze(2).to_broadcast([128, 32, 8])
            kb.op("act", lambda e: e.copy(out=sA, in_=pG[:, 0:256].rearrange("p (j n) -> p j n", n=8)), [bG], [B["t2"]])
            cur = sA
            for it in range(2):
                kb.op("dve", lambda e: e.tensor_reduce(out=mx, in_=cur, axis=AX.X, op=ALU.max), [B["t2"]], [B["t3"]])
                kb.op("dve", lambda e: e.tensor_tensor(out=eq, in0=cur, in1=mxb(), op=ALU.is_equal), [B["t2"]], [B["t3"]])
                kb.op("dve", lambda e: e.scalar_tensor_tensor(out=sBv, in0=eq, scalar=-1e30, in1=cur, op0=ALU.mult, op1=ALU.add), [B["t3"]], [B["t2"]])
                cur = sBv
            kb.op("dve", lambda e: e.tensor_reduce(out=mx, in_=sBv, axis=AX.X, op=ALU.max), [B["t2"]], [B["t3"]])
            kb.op("dve", lambda e: e.tensor_tensor(out=g3, in0=sA, in1=mxb(), op=ALU.is_ge), [B["t2"], B["t3"]], [nb["selS"]])
            PT = t1[:, :].bitcast(BF)[:, 0:512].rearrange("p (g j) -> p g j", g=16)
            kb.op("act", lambda e: e.activation(out=t2[:, :], in_=pS[:, :], func=AF.Exp, scale=scm), [bS], [B["t2"]])
            kb.op("dve", lambda e: e.tensor_tensor(out=PT.rearrange("p (n two) j -> p n two j", two=2),
                                                   in0=t2[:, :].rearrange("p (n two j) -> p n two j", two=2, j=32),
                                                   in1=g3.rearrange("p j n -> p n j").unsqueeze(2).to_broadcast([128, 8, 2, 32]), op=ALU.mult),
                  [B["t2"], nb["selS"]], [B["t1"]])
            pWb, bWb = bank()
            for j in range(32):
                mm(pWb[:, j * 4:(j + 1) * 4], qrb[:, j, :], ksamp[:, j // 4, s_ * 4:s_ * 4 + 4], True, True, [nb["qrep"], nb["ksamp"]], [bWb], j == 31)
            ob = t3[:, 0:128]
            kb.op("act", lambda e: e.activation(out=ob, in_=pWb[:, 0:128], func=AF.Exp, scale=scm), [bWb], [B["t3"]])
            kb.op("dve", lambda e: e.tensor_tensor(out=ob, in0=ob, in1=cst[:, C_CM4:C_CM4 + 128], op=ALU.mult), [B["cst"]], [B["t3"]])
            kb.op("dve", lambda e: e.tensor_reduce(out=t3[:, 128:160], in_=ob.rearrange("p (j t) -> p j t", t=4), axis=AX.X, op=ALU.add), [], [B["t3"]])
            kb.op("dve", lambda e: e.tensor_tensor(out=t3[:, 160:288].rearrange("p (h q t) -> p h q t", h=8, q=4),
                                                   in0=ob.rearrange("p (h q t) -> p h q t", h=8, q=4),
                                                   in1=vsT[:, :, s_ * 4:s_ * 4 + 4].unsqueeze(2).to_broadcast([128, 8, 4, 4]), op=ALU.mult),
                  [nb["vsamp"]], [B["t3"]])
            kb.op("dve", lambda e: e.tensor_reduce(out=t3[:, 288:320], in_=t3[:, 160:288].rearrange("p (j t) -> p j t", t=4), axis=AX.X, op=ALU.add), [], [B["t3"]])
            pO_, bO_ = lbank(1)
            for h in range(8):
                for pg in range(16):
                    mm(pO_[:, h * 4:(h + 1) * 4], vall[:, pg, h * 128:(h + 1) * 128], PT[:, pg, h * 4:(h + 1) * 4], pg == 0, pg == 15,
                       [nb["vall"], B["t1"]], [bO_], False)
            for pg in range(16):
                mm(pO_[:, 32:64], ones_bf, PT[:, pg, :], pg == 0, pg == 15, [B["cstb"], B["t1"]], [bO_], pg == 15)
            kb.op("dve", lambda e: e.tensor_tensor(out=t3[:, 128:160], in0=t3[:, 128:160], in1=pO_[:, 32:64], op=ALU.add), [bO_], [B["t3"]])
            kb.op("dve", lambda e: e.tensor_tensor(out=t3[:, 288:320], in0=t3[:, 288:320], in1=pO_[:, 0:32], op=ALU.add), [bO_], [B["t3"]])
            kb.op("dve", lambda e: e.reciprocal(out=t3[:, 128:160], in_=t3[:, 128:160]), [], [B["t3"]])
            kb.op("dve", lambda e: e.tensor_tensor(out=hT[:, :, c0:c0 + 4], in0=t3[:, 288:320].rearrange("p (h q) -> p h q", h=8),
                                                   in1=t3[:, 128:160].rearrange("p (h q) -> p h q", h=8), op=ALU.mult), [], [B["t3"], hB[4]])
        proj_fm(c_w_out[0], KC, KC, h_rhs, add_to_x)

    def xattn(li):
        qv = big[:, 0:KC * NT].rearrange("p (k t) -> p k t", k=KC)
        MT = [(0, MEM)]
        kb.barrier()
        kb.dma("sp", memR[:], memT_d, [], [B["memR"]], d_in)
        rms_stats(memR, B["memR"], 0, MEM)
        for k in range(KC):
            gc = P_NMEM + li * 8 + k
            kb.op("dve", lambda e, k=k, gc=gc: e.scalar_tensor_tensor(out=memN[:, k, :], in0=memR[:, k, :], scalar=par[:, gc:gc + 1],
                                                                   in1=rstd[:, :MEM], op0=ALU.mult, op1=ALU.mult),
                  [B["memR"], B["par"], B["rstd"]], [B["memN"]])

        def mem_rhs(k, ti, t0, tn):
            return memN[:, k, t0:t0 + tn], B["memN"]

        def evac_k(oc, ti, t0, tn, ps, pb):
            kb.op("act", lambda e: e.copy(out=mkT[:, oc, :], in_=ps[:, :MEM]), [pb], [B["mkT"]])
            kb.op("dve", lambda e: e.tensor_copy(out=mo32[:, 0:MEM], in_=ps[:, :MEM]), [pb], [B["t3"]])
            kb.dma("sp", omk_d[li, :, oc, :], mo32[:, 0:MEM], [B["t3"]], [], d_out)

        if 'xa_k' in phases:
            proj_fm(x_w_kv[li], KC, 8, mem_rhs, evac_k, tiles=MT)
        for cg in (range(0, D, 256) if 'xa_v' in phases else []):
            view, wb = wload([(x_w_kv[li][:, D + cg:D + cg + 256], KC, 256)])
            for mt in range(2):
                ps, pb = bank()
                for k in range(KC):
                    mm(ps[:, :256], memN[:, k, mt * 128:(mt + 1) * 128], view[:, k, :], k == 0, k == KC - 1, [wb, B["memN"]], [pb], k == KC - 1)
                kb.op("act", lambda e: e.copy(out=mv[:, mt, cg:cg + 256], in_=ps[:, :256]), [pb], [B["mv"]])
                kb.op("dve", lambda e: e.tensor_copy(out=mo32[:, 0:256], in_=ps[:, :256]), [pb], [B["t3"]])
                kb.dma("sp", omv_d[li, mt, :, cg:cg + 256], mo32[:, 0:256], [B["t3"]], [], d_out)
        kb.barrier()
        rmsnorm(P_NXA + li * 8)

        def evac_q(oc, ti, t0, tn, ps, pb):
            kb.op("act", lambda e: e.copy(out=qv[:, oc, t0:t0 + tn], in_=ps[:, :tn]), [pb], [gB[ti]])

        proj_fm(x_w_q[li], KC, KC, h_rhs, evac_q)
        sc = 1.0 / 16.0
        pS, bS = lbank(0)
        pO, bO = lbank(1)

        def prompt_group(ti, t0, tn, hd):
            for mt in range(2):
                ps, pb = bank()
                for dc in range(2):
                    mm(ps[:, :tn], mkT[:, hd * 2 + dc, mt * 128:(mt + 1) * 128], qv[:, hd * 2 + dc, t0:t0 + tn], dc == 0, dc == 1,
                       [B["mkT"], gB[ti]], [pb], dc == 1)
                kb.op("act", lambda e, mt=mt, ps=ps: e.activation(out=ptb[:, mt, :tn], in_=ps[:, :tn], func=AF.Exp, scale=sc),
                      [pb], [B["ptb"]])
            psd, pbd = bank()
            for mt in range(2):
                mm(psd[:, :tn], ones_bf, ptb[:, mt, :tn], mt == 0, mt == 1, [B["ptb"], B["cstb"]], [pbd], mt == 1)
            kb.op("dve", lambda e: e.reciprocal(out=rstd[:, :tn], in_=psd[:, :tn]), [pbd], [B["rstd"]])
            for dvc in range(2):
                ps, pb = bank()
                for mt in range(2):
                    mm(ps[:, :tn], mv[:, mt, hd * 256 + dvc * 128: hd * 256 + (dvc + 1) * 128], ptb[:, mt, :tn], mt == 0, mt == 1,
                       [B["mv"], B["ptb"]], [pb], mt == 1)
                kb.op("dve", lambda e, ps=ps, dvc=dvc: e.tensor_tensor(out=hT[:, hd * 2 + dvc, t0:t0 + tn], in0=ps[:, :tn],
                                                                     in1=rstd[:, :tn], op=ALU.mult), [pb, B["rstd"]], [hB[ti]])

        def sample_seq(s):
            i = s % 2
            kb.dma("pool", skT[i][:], cmk_d[li, s], [], [B["skT%d" % i]], d_kv[i])
            kb.dma("pool", sv[i][:], cmv_d[li, s].rearrange("a p n -> p a n"), [], [B["sv%d" % i]], d_kvv[i])
            for mt in range(2):
                for hd in range(4):
                    c0 = mt * 256 + s * 16 + hd * 4
                    for dc in range(2):
                        mm(pS[:, c0:c0 + 4], skT[i][:, hd * 2 + dc, mt * 128:(mt + 1) * 128], qv[:, hd * 2 + dc, SEQ + s * 4:SEQ + s * 4 + 4],
                           dc == 0, dc == 1, [B["skT%d" % i], gB[4]], [bS], True if (dc == 1 and hd == 3 and mt == 1) else False)
            kb.op("act", lambda e, s=s: e.activation(
                out=pts[:, :, s * 16:(s + 1) * 16], in_=pS[:, :].rearrange("p (a c) -> p a c", a=2)[:, :, s * 16:(s + 1) * 16],
                func=AF.Exp, scale=sc), [bS], [B["pts"]])
            for hd in range(4):
                for dvc in range(2):
                    c0 = ((s * 4 + hd) * 2 + dvc) * 4
                    for mt in range(2):
                        mm(pO[:, c0:c0 + 4], sv[i][:, mt, hd * 256 + dvc * 128:hd * 256 + (dvc + 1) * 128],
                           pts[:, mt, s * 16 + hd * 4:s * 16 + hd * 4 + 4], mt == 0, mt == 1, [B["sv%d" % i], B["pts"]], [bO],
                           True if (mt == 1 and hd == 3 and dvc == 1) else False)

        for ti, (t0, tn) in enumerate(TILES[:4]):
            for hd in range(4):
                prompt_group(ti, t0, tn, hd)
                sample_seq(ti * 4 + hd)
        pD, bD = bank()
        for mt in range(2):
            mm(pD[:, :256], ones_bf, pts[:, mt, 0:256], mt == 0, mt == 1, [B["pts"], B["cstb"]], [bD], mt == 1)
        kb.op("dve", lambda e: e.reciprocal(out=rstd[:, :256], in_=pD[:, :256]), [bD], [B["rstd"]])
        for dvc in range(2):
            kb.op("dve", lambda e, dvc=dvc: e.tensor_tensor(
                out=hT[:, :, SEQ:NT].rearrange("p (h v) (s q) -> p h v s q", v=2, q=LS)[:, :, dvc, :, :],
                in0=pO[:, :].rearrange("p (s h v q) -> p h v s q", h=4, v=2, q=LS)[:, :, dvc, :, :],
                in1=rstd[:, :256].rearrange("p (s h q) -> p h s q", h=4, q=LS), op=ALU.mult), [bO, B["rstd"]], [hB[4]])
        proj_fm(x_w_o[li], KC, KC, h_rhs, add_to_x)

    def swiglu(li):
        kb.barrier()
        rmsnorm(P_NFFN + li * 8)
        av = big[:, 0:KC * NT].rearrange("p (k t) -> p k t", k=KC)
        for f0, nf in ((0, 8), (8, 8), (16, 6)):
            for fi in range(nf):
                f = f0 + fi
                view, wb = wload([(f_w_up[li][:, f * 128:(f + 1) * 128], KC, 128),
                                  (f_w_up[li][:, DFF + f * 128:DFF + (f + 1) * 128], KC, 128)])
                for ti, (t0, tn) in enumerate(TILES):
                    pg, bg = bank()
                    for k in range(KC):
                        mm(pg[:, :tn], view[:, k, 0:128], hT[:, k, t0:t0 + tn], k == 0, k == KC - 1, [wb, hB[ti]], [bg], k == KC - 1)
                    pu, bu = bank()
                    for k in range(KC):
                        mm(pu[:, :tn], view[:, k, 128:256], hT[:, k, t0:t0 + tn], k == 0, k == KC - 1, [wb, hB[ti]], [bu], k == KC - 1)
                    kb.op("act", lambda e, pg=pg: e.activation(out=t1[:, :tn], in_=pg[:, :tn], func=AF.Silu), [bg], [B["t1"]])
                    kb.op("dve", lambda e, pu=pu, fi=fi: e.tensor_tensor(out=av[:, fi, t0:t0 + tn], in0=pu[:, :tn], in1=t1[:, :tn], op=ALU.mult),
                          [bu, B["t1"]], [gB[ti]])
            proj_fm(f_w_down[li], nf, KC, lambda k, ti, t0, tn: (av[:, k, t0:t0 + tn], gB[ti]), add_to_x, row0=f0 * 128)

    for li in range(depth):
        kind, j = li % 3, li // 3
        if kind == 0 and 'mix' in phases:
            shortconv(li, j)
        if kind == 1 and 'mix' in phases:
            gdn(li)
        if kind == 2 and 'mix' in phases:
            moba(li)
        if 'xa' in phases:
            xattn(li)
        if 'ffn' in phases:
            swiglu(li)

    for ti, (t0, tn) in enumerate(TILES):
        rms_stats(xT, xB[ti], t0, tn)
        for k in range(KC):
            kb.op("dve", lambda e, k=k: e.scalar_tensor_tensor(
                out=t1[:, :tn], in0=xT[:, k, t0:t0 + tn], scalar=par[:, P_NFIN + k:P_NFIN + k + 1],
                in1=rstd[:, :tn], op0=ALU.mult, op1=ALU.mult), [xB[ti], B["rstd"], B["par"]], [B["t1"]])
            kb.dma("sp", yT_d[:, k, t0:t0 + tn], t1[:, :tn], [B["t1"]], [], d_out)
    kb.finish()
    es.close()
    return nc, kb


P_NMIX = 0
P_NMEM = 32
P_NXA = 64
P_NFFN = 96
P_NFIN = 128
P_ACONV = 136
P_BCONV = 184
P_DTB = 280
P_ALOG = 288
P_BNORM = 296
NPAR = 297
G_ID = 0
G_IDREP = 64
G_U = 576
G_NEGU = 640
G_ONES = 704
G_NEGONES = 832
G_MLOW = 960
G_MUP = 1472
NGC = 1984
C_ONES = 0
C_IDB = 128
C_CM = 256
NCSTB = 768
C_PID = 64
C_CM4 = 72
NCST = 200
NPOOL = 2560


def _vec_pk(v):
    return np.ascontiguousarray(v.reshape(-1, 128).T)


_SHARED = {}


def shared_inputs(I):
    if "poolk" not in _SHARED:
        ck = I["cache_c_k"][0]
        _SHARED["poolk"] = np.ascontiguousarray(ck.transpose(0, 3, 2, 1)).reshape(NPOOL * 128, D)
        _SHARED["poolv"] = np.ascontiguousarray(I["cache_c_v"][0]).reshape(NPOOL * 128, D)
    return _SHARED


def make_inputs(core, I):
    b = core
    s0, s1 = core * NS, (core + 1) * NS
    xa = np.concatenate([I["x_prompt"][b], I["x_sample"][s0:s1].reshape(NSAMP, D)], axis=0)
    x0 = np.ascontiguousarray(xa.T.reshape(KC, 128, NT).transpose(1, 0, 2))
    memT = np.ascontiguousarray(I["mem_prompt"][b].T.reshape(KC, 128, MEM).transpose(1, 0, 2))
    par = np.zeros((128, NPAR), np.float32)
    for li in range(4):
        par[:, P_NMIX + li * 8:P_NMIX + li * 8 + 8] = _vec_pk(I["norm_mix"][li])
        par[:, P_NMEM + li * 8:P_NMEM + li * 8 + 8] = _vec_pk(I["norm_mem"][li])
        par[:, P_NXA + li * 8:P_NXA + li * 8 + 8] = _vec_pk(I["norm_xattn"][li])
        par[:, P_NFFN + li * 8:P_NFFN + li * 8 + 8] = _vec_pk(I["norm_ffn"][li])
    par[:, P_NFIN:P_NFIN + 8] = _vec_pk(I["norm_final"])
    for j in range(2):
        for tap in range(3):
            par[:, P_ACONV + (j * 3 + tap) * 8:P_ACONV + (j * 3 + tap) * 8 + 8] = _vec_pk(I["a_w_conv"][j, tap])
    for tap in range(4):
        par[:, P_BCONV + tap * 24:P_BCONV + tap * 24 + 24] = _vec_pk(I["b_w_conv"][0, tap])
    par[:, P_DTB:P_DTB + 8] = I["b_dt_bias"][0][None, :]
    par[:, P_ALOG:P_ALOG + 8] = I["b_a_log"][0][None, :]
    par[:, P_BNORM] = I["b_norm"][0]
    gc = np.zeros((64, NGC), np.float32)
    ii = np.arange(64)
    gc[:, G_ID:G_ID + 64] = np.eye(64)
    gc[:, G_IDREP:G_IDREP + 512] = np.tile(np.eye(64), (1, 8))
    U = (ii[:, None] <= ii[None, :]).astype(np.float32)
    gc[:, G_U:G_U + 64] = U
    gc[:, G_NEGU:G_NEGU + 64] = -U
    gc[:, G_ONES:G_ONES + 128] = 1.0
    gc[:, G_NEGONES:G_NEGONES + 128] = -1.0
    gc[:, G_MLOW:G_MLOW + 512] = np.tile(np.where(ii[:, None] > ii[None, :], 0.0, -1e30), (1, 8))
    gc[:, G_MUP:G_MUP + 512] = np.tile(np.where(ii[None, :] >= ii[:, None], 0.0, -1e30), (1, 8))
    sbc = np.ascontiguousarray(I["state_b_conv"][0, s0:s1].reshape(NS, 3, 24, 128).transpose(2, 3, 0, 1))
    sbr = np.ascontiguousarray(I["state_b_rec"][0, s0:s1])
    import ml_dtypes
    cst = np.zeros((128, NCST), np.float32)
    cstb = np.zeros((128, NCSTB), np.float32)
    cstb[:, C_ONES:C_ONES + 128] = 1.0
    cstb[:, C_IDB:C_IDB + 128] = np.eye(128, dtype=np.float32)
    rr = np.arange(128)[:, None]
    pp = np.arange(256)[None, :]
    for kc in range(2):
        cstb[:, C_CM + kc * 256:C_CM + (kc + 1) * 256] = np.where(kc * 128 + rr <= pp, 0.0, NEG)
    cst[0:64, 0:64] = np.eye(64, dtype=np.float32)
    cst[:, C_PID] = np.arange(128, dtype=np.float32)
    cc_ = np.arange(128)
    cst[:, C_CM4:C_CM4 + 128] = ((cc_ % 4) <= ((cc_ // 4) % 4)).astype(np.float32)[None, :]
    cstb = cstb.astype(ml_dtypes.bfloat16)
    sa = np.ascontiguousarray(I["state_a_conv"][:, s0:s1].reshape(2, NS, 2, KC, 128).transpose(4, 0, 3, 1, 2))
    ck = I["cache_mem_k"][:, s0:s1].reshape(4, NS, MEM, 4, 2, 128)
    cmk = np.ascontiguousarray(ck.transpose(0, 1, 5, 3, 4, 2).reshape(4, NS, 128, 8, MEM))
    cmv = np.ascontiguousarray(I["cache_mem_v"][:, s0:s1].reshape(4, NS, 2, 128, D))
    return {"x0": x0, "memT": memT, "par": par, "cst": cst, "cstb": cstb, "sa": sa, "cmk": cmk, "cmv": cmv, "sbc": sbc, "sbr": sbr, "gcst": gc,
            "b_w_in": I["b_w_in"], "b_w_out": I["b_w_out"],
            "c_w_qkv": I["c_w_qkv"], "c_w_out": I["c_w_out"], "poolk": shared_inputs(I)["poolk"], "poolv": shared_inputs(I)["poolv"],
            "pt": np.ascontiguousarray(I["page_table"][s0:s1].reshape(1, NS * 16).astype(np.int32)),
            "a_w_in": I["a_w_in"], "a_w_out": I["a_w_out"], "x_w_q": I["x_w_q"], "x_w_kv": I["x_w_kv"], "x_w_o": I["x_w_o"],
            "f_w_up": I["f_w_up"], "f_w_down": I["f_w_down"]}


def kernel(**inputs):
    I = {k: np.asarray(v) for k, v in inputs.items()}
    _SHARED.clear()
    nc, kb = build()
    in_maps = [make_inputs(c, I) for c in range(NCORES)]
    res = run_bass_kernel_spmd(nc, in_maps, core_ids=list(range(NCORES)))
    R = res.results
    f32 = np.float32
    y_p = np.zeros((8, SEQ, D), f32); y_s = np.zeros((128, LS, D), f32)
    a_p = np.zeros((2, 8, 2, D), f32); a_s = np.zeros((2, 128, 2, D), f32)
    bc_p = np.zeros((1, 8, 3, 3 * D), f32); bc_s = np.zeros((1, 128, 3, 3 * D), f32)
    br_p = np.zeros((1, 8, 8, 128, 128), f32); br_s = np.zeros((1, 128, 8, 128, 128), f32)
    ck_p = np.zeros((1, 8, SEQ, 8, 128), f32); cv_p = np.zeros((1, 8, SEQ, 8, 128), f32)
    ck_s = np.zeros((1, 128, LS, 8, 128), f32); cv_s = np.zeros((1, 128, LS, 8, 128), f32)
    mk_p = np.zeros((4, 8, MEM, 4, 256), f32); mv_p = np.zeros((4, 8, MEM, 4, 256), f32)
    for c in range(NCORES):
        r = R[c]
        s0, s1 = c * NS, (c + 1) * NS
        y = np.asarray(r["yT"]).transpose(2, 1, 0).reshape(NT, D)
        y_p[c] = y[:SEQ]
        y_s[s0:s1] = y[SEQ:].reshape(NS, LS, D)
        oa = np.asarray(r["oa"])
        for j in range(2):
            a = oa[:, j].transpose(2, 1, 0).reshape(2 + 2 * NS, D)
            a_p[j, c] = a[:2]
            a_s[j, s0:s1] = a[2:].reshape(NS, 2, D)
        bc_p[0, c] = np.asarray(r["obc_p"]).transpose(2, 1, 0).reshape(3, 3 * D)
        bc_s[0, s0:s1] = np.asarray(r["obc_s"]).transpose(2, 3, 0, 1).reshape(NS, 3, 3 * D)
        br_p[0, c] = np.asarray(r["obr_p"])
        br_s[0, s0:s1] = np.asarray(r["obr_s"])
        k = np.asarray(r["ock"]).transpose(2, 1, 0)
        v = np.asarray(r["ocv"]).reshape(NT, 8, 128)
        ck_p[0, c] = k[:SEQ]; cv_p[0, c] = v[:SEQ]
        ck_s[0, s0:s1] = k[SEQ:].reshape(NS, LS, 8, 128); cv_s[0, s0:s1] = v[SEQ:].reshape(NS, LS, 8, 128)
        omk = np.asarray(r["omk"]); omv = np.asarray(r["omv"])
        for li in range(4):
            mk_p[li, c] = omk[li].transpose(2, 1, 0).reshape(MEM, 4, 256)
            mv_p[li, c] = omv[li].reshape(MEM, 4, 256)
    return (y_p, y_s, a_p, a_s, bc_p, bc_s, br_p, br_s, ck_p, cv_p, ck_s, cv_s, mk_p, mv_p)
```

```python
from contextlib import ExitStack
import numpy as np
import concourse.bass as bass
import concourse.mybir as mybir
from concourse.bass_utils import run_bass_kernel_spmd

F32 = mybir.dt.float32
BF = mybir.dt.bfloat16
I32 = mybir.dt.int32
AF = mybir.ActivationFunctionType
ALU = mybir.AluOpType
AX = mybir.AxisListType

NCORES = 8
D = 1024
KC = 8
SEQ = 2048
NS = 16
LS = 4
NSAMP = NS * LS
NT = SEQ + NSAMP
TILES = [(0, 512), (512, 512), (1024, 512), (1536, 512), (2048, 64)]
DFF = 2816
FC = 22
MEM = 256
EPS = 1e-6
NEG = -30000.0


class Buf:
    __slots__ = ("name", "w", "r", "excl")

    def __init__(self, name, excl=False):
        self.name = name
        self.w = None
        self.r = {}
        self.excl = excl


class DSem:
    __slots__ = ("sem", "cnt")

    def __init__(self, sem):
        self.sem = sem
        self.cnt = 0


class KB:
    def __init__(self, nc, es):
        self.nc = nc
        self.es = es
        self.E = {"pe": nc.tensor, "act": nc.scalar, "dve": nc.vector, "pool": nc.gpsimd, "sp": nc.sync}
        self.sem = {}
        self.cnt = {}
        self.seen = {}
        for q in self.E:
            self.seen[q] = {}
        for e in ("pe", "act", "dve", "pool"):
            self.sem[e] = es.enter_context(nc.semaphore("s_" + e))
            self.cnt[e] = 0
        self.dsems = []
        self.n_ins = 0

    def dsem(self, name):
        d = DSem(self.es.enter_context(self.nc.semaphore("d_" + name)))
        self.dsems.append(d)
        return d

    def sb(self, name, shape, dt):
        return self.es.enter_context(self.nc.sbuf_tensor("sb_" + name, list(shape), dt))

    def _sync(self, q, reads, writes):
        need = {}
        for b in reads:
            if b.w is not None:
                k, v = b.w
                if need.get(k, 0) < v:
                    need[k] = v
        for b in writes:
            if b.w is not None:
                k, v = b.w
                if need.get(k, 0) < v:
                    need[k] = v
            for k, v in b.r.items():
                if need.get(k, 0) < v:
                    need[k] = v
        seen = self.seen[q]
        for k, v in need.items():
            if isinstance(k, DSem):
                v = k.cnt
                s = k.sem
            else:
                if k == q and q == "pe":
                    continue
                s = self.sem[k]
            if seen.get(k, 0) >= v:
                continue
            self.E[q].wait_ge(s, v)
            seen[k] = v
            self.n_ins += 1

    def op(self, eng, fn, reads=(), writes=(), inc=True):
        if eng != "pe":
            ex = [b for b in reads if b.excl]
            if ex:
                reads = [b for b in reads if not b.excl]
                writes = list(writes) + ex
        self._sync(eng, reads, writes)
        ins = fn(self.E[eng])
        self.n_ins += 1
        if inc:
            self.cnt[eng] += 1
            ins.then_inc(self.sem[eng], 1)
            v = self.cnt[eng]
        else:
            assert eng == "pe"
            v = self.cnt[eng] + 1
        for b in reads:
            b.r[eng] = v
        for b in writes:
            b.w = (eng, v)
            b.r = {}
        return ins

    def dma(self, q, out, in_, reads, writes, ds, **kw):
        self._sync(q, reads, writes)
        ins = self.E[q].dma_start(out=out, in_=in_, **kw)
        self.n_ins += 1
        ds.cnt += 16
        ins.then_inc(ds.sem, 16)
        for b in reads:
            b.r[ds] = ds.cnt
        for b in writes:
            b.w = (ds, ds.cnt)
            b.r = {}
        return ins

    def idma(self, out, in_, idx_ap, reads, writes, ds):
        self._sync("pool", reads, writes)
        ins = self.nc.gpsimd.indirect_dma_start(out=out, out_offset=None, in_=in_,
                                                in_offset=bass.IndirectOffsetOnAxis(ap=idx_ap, axis=0))
        self.n_ins += 1
        ds.cnt += 16
        ins.then_inc(ds.sem, 16)
        for b in reads:
            b.r[ds] = ds.cnt
        for b in writes:
            b.w = (ds, ds.cnt)
            b.r = {}
        return ins

    def barrier(self):
        for q in self.E:
            seen = self.seen[q]
            for e in ("pe", "act", "dve", "pool"):
                if e == q and q == "pe":
                    continue
                v = self.cnt[e]
                if v and seen.get(e, 0) < v:
                    self.E[q].wait_ge(self.sem[e], v)
                    seen[e] = v
                    self.n_ins += 1
            for d in self.dsems:
                if d.cnt and seen.get(d, 0) < d.cnt:
                    self.E[q].wait_ge(d.sem, d.cnt)
                    seen[d] = d.cnt
                    self.n_ins += 1

    def finish(self):
        for d in self.dsems:
            if d.cnt:
                self.nc.sync.wait_ge(d.sem, d.cnt)
        for e in ("pe", "act", "dve", "pool"):
            if self.cnt[e]:
                self.nc.sync.wait_ge(self.sem[e], self.cnt[e])


def build(depth=4, dbg=False, phases=('mix', 'xa', 'xa_k', 'xa_v', 'xa_p', 'xa_s', 'ffn')):
    nc = bass.Bass("TRN2", target_bir_lowering=False)
    es = ExitStack()
    kb = KB(nc, es)

    def din(name, shape, dt=F32):
        return nc.dram_tensor(name, list(shape), dt, kind="ExternalInput").ap()

    def dout(name, shape, dt=F32):
        return nc.dram_tensor(name, list(shape), dt, kind="ExternalOutput").ap()

    x0_d = din("x0", [128, KC, NT])
    memT_d = din("memT", [128, KC, MEM])
    par_d = din("par", [128, NPAR])
    cst_d = din("cst", [128, NCST])
    cstb_d = din("cstb", [128, NCSTB], BF)
    sa_d = din("sa", [128, 2, KC, NS, 2])
    cmk_d = din("cmk", [4, NS, 128, 8, MEM])
    cmv_d = din("cmv", [4, NS, 2, 128, D])
    a_w_in = din("a_w_in", [2, D, 3 * D])
    a_w_out = din("a_w_out", [2, D, D])
    x_w_q = din("x_w_q", [4, D, D])
    x_w_kv = din("x_w_kv", [4, D, 2 * D])
    x_w_o = din("x_w_o", [4, D, D])
    f_w_up = din("f_w_up", [4, D, 2 * DFF])
    f_w_down = din("f_w_down", [4, DFF, D])

    b_w_in = din("b_w_in", [1, D, 4112])
    b_w_out = din("b_w_out", [1, D, D])
    sbc_d = din("sbc", [24, 128, NS, 3])
    sbr_d = din("sbr", [NS, 8, 128, 128])
    gcst_d = din("gcst", [64, NGC])
    obc_p_d = dout("obc_p", [128, 24, 3])
    obc_s_d = dout("obc_s", [24, 128, NS, 3])
    obr_p_d = dout("obr_p", [8, 128, 128])
    obr_s_d = dout("obr_s", [NS, 8, 128, 128])
    c_w_qkv = din("c_w_qkv", [1, D, 3 * D])
    c_w_out = din("c_w_out", [1, D, D])
    pt_d = din("pt", [1, NS * 16], I32)
    poolk_d = din("poolk", [NPOOL * 128, D])
    poolv_d = din("poolv", [NPOOL * 128, D])
    ock_d = dout("ock", [128, 8, NT])
    ocv_d = dout("ocv", [NT, D])
    yT_d = dout("yT", [128, KC, NT])
    oa_d = dout("oa", [128, 2, KC, 2 + NS * 2])
    omk_d = dout("omk", [4, 128, KC, MEM])
    omv_d = dout("omv", [4, 2, 128, D])

    xT = kb.sb("xT", [128, KC, NT], F32)
    hT = kb.sb("hT", [128, KC, NT], BF)
    ARN = 16896
    Bg = kb.sb("Bg", [128, ARN], BF)
    Cg = kb.sb("Cg", [128, ARN], BF)

    def carve(arena, off, n, dt=BF):
        assert off % 4 == 0
        if dt == BF:
            assert off // 2 + n <= ARN
            return arena[:, off // 2: off // 2 + n]
        assert off // 2 + 2 * n <= ARN
        return arena[:, off // 2: off // 2 + 2 * n].bitcast(dt)

    par = kb.sb("par", [128, NPAR], F32)
    cst = kb.sb("cst", [128, NCST], F32)
    cstb = kb.sb("cstb", [128, NCSTB], BF)
    NW = 3
    wslots = [kb.sb("w%d" % i, [128, 2048], BF) for i in range(NW)]
    wbufs = [Buf("w%d" % i) for i in range(NW)]
    wds = [kb.dsem("w%d" % i) for i in range(NW)]
    wstate = {"i": 0}
    sq = kb.sb("sq", [128, KC, 512], BF)
    rstd = kb.sb("rstd", [128, 512], F32)
    t1 = kb.sb("t1", [128, 512], F32)
    t2 = kb.sb("t2", [128, 512], F32)
    t3 = kb.sb("t3", [128, 512], F32)
    ptb = kb.sb("ptb", [128, 2, 512], BF)
    pts = kb.sb("pts", [128, 2, 256], BF)
    big = Bg
    km = kb.sb("km", [128, 8, 8], F32)
    kmb = kb.sb("kmb", [128, 8, 8], BF)
    sel = kb.sb("sel", [128, 200], F32)
    selb = kb.sb("selb", [128, 64], BF)
    negT = kb.sb("negT", [64, 256], BF)
    tsel = kb.sb("tsel", [64, 256], BF)
    tsel2 = kb.sb("tsel2", [64, 256], BF)
    vsT = kb.sb("vsT", [128, 8, NSAMP], BF)
    qsamp = kb.sb("qsamp", [128, 8, NSAMP], BF)
    zbuf = carve(Cg, 0, 2 + SEQ, F32)
    zs = carve(Cg, 8200, NS * 6, F32).rearrange("p (s t) -> p s t", t=6)
    oa_sb = carve(Cg, 8584, 2 * KC * (2 + NS * 2), F32).rearrange("p (j c r) -> p j c r", j=2, c=KC)
    sa_sb = carve(Cg, 10760, 2 * KC * NS * 2, F32).rearrange("p (j c s r) -> p j c s r", j=2, c=KC, s=NS)
    mkT = carve(Cg, 0, 8 * MEM).rearrange("p (k m) -> p k m", k=8)
    mv = carve(Cg, 4096, 2 * D).rearrange("p (a n) -> p a n", a=2)
    skT = [carve(Cg, 8192 + i * 8192, 8 * MEM).rearrange("p (k m) -> p k m", k=8) for i in range(2)]
    sv = [carve(Cg, 12288 + i * 8192, 2 * D).rearrange("p (a n) -> p a n", a=2) for i in range(2)]
    memR = carve(Bg, 0, KC * MEM, F32).rearrange("p (k m) -> p k m", k=KC)
    memN = carve(Bg, 8192, KC * MEM).rearrange("p (k m) -> p k m", k=KC)
    mo32 = t3

    B = {n: Buf(n) for n in ("xT", "hT", "big", "par", "cst", "cstb", "sq", "rstd", "t1", "t2", "t3", "ptb", "pts", "memN",
                             "memR", "mkT", "mv", "mo32", "zbuf", "zs", "oa_sb", "sa_sb", "skT0", "skT1", "sv0", "sv1")}
    xB = [Buf("x%d" % i) for i in range(len(TILES))]
    hB = [Buf("h%d" % i) for i in range(len(TILES))]
    gB = [Buf("g%d" % i) for i in range(len(TILES))]

    psum = [es.enter_context(nc.psum_tensor("ps%d" % i, [128, 512], F32)) for i in range(8)]
    psB = [Buf("ps%d" % i, excl=True) for i in range(8)]
    pstate = {"i": 0, "n": 6}

    def bank():
        i = pstate["i"]
        pstate["i"] = (i + 1) % pstate["n"]
        return psum[i], psB[i]

    def lbank(i):
        return psum[pstate["n"] + i], psB[pstate["n"] + i]

    d_in = kb.dsem("in")
    d_out = kb.dsem("out")
    d_kv = [kb.dsem("kv0"), kb.dsem("kv1")]
    d_kvv = [kb.dsem("kvv0"), kb.dsem("kvv1")]

    for ti, (t0, tn) in enumerate(TILES):
        kb.dma("sp", xT[:, :, t0:t0 + tn], x0_d[:, :, t0:t0 + tn], [], [xB[ti]], d_in)
    kb.dma("sp", par[:], par_d, [], [B["par"]], d_in)
    kb.dma("sp", cst[:], cst_d, [], [B["cst"]], d_in)
    kb.dma("sp", cstb[:], cstb_d, [], [B["cstb"]], d_in)

    ones_bf = cstb[:, C_ONES:C_ONES + 128]

    def mm(out, lhsT, rhs, start, stop, reads, writes, inc):
        return kb.op("pe", lambda e: e.matmul(out, lhsT=lhsT, rhs=rhs, start=start, stop=stop), reads, writes, inc)

    def wload(parts):
        i = wstate["i"]
        wstate["i"] = (i + 1) % NW
        nk = parts[0][1]
        tot = sum(p[2] for p in parts)
        assert nk * tot <= 2048
        view = wslots[i][:, 0:nk * tot].rearrange("p (k n) -> p k n", k=nk)
        c0 = 0
        for ap, nk_, ncols in parts:
            kb.dma("pool", view[:, :, c0:c0 + ncols], ap.rearrange("(k p) n -> p k n", p=128), [], [wbufs[i]], wds[i])
            c0 += ncols
        return view, wbufs[i]

    def rms_stats(src, srcB, t0, tn, nfeat_chunks=KC):
        kb.op("act", lambda e: e.activation(out=sq[:, :, :tn], in_=src[:, :, t0:t0 + tn], func=AF.Square),
              [srcB], [B["sq"]])
        ps, pb = bank()
        for k in range(KC):
            mm(ps[:, :tn], ones_bf, sq[:, k, :tn], k == 0, k == KC - 1, [B["sq"], B["cstb"]], [pb], k == KC - 1)
        kb.op("act", lambda e: e.activation(out=rstd[:, :tn], in_=ps[:, :tn], func=AF.Sqrt, bias=EPS, scale=1.0 / D),
              [pb], [B["rstd"]])
        kb.op("dve", lambda e: e.reciprocal(out=rstd[:, :tn], in_=rstd[:, :tn]), [], [B["rstd"]])

    def rmsnorm(gcol, dst=None, dstB=None, dst_dt_is_f32=False):
        dst = hT if dst is None else dst
        for ti, (t0, tn) in enumerate(TILES):
            rms_stats(xT, xB[ti], t0, tn)
            db = hB[ti] if dstB is None else dstB[ti]
            for k in range(KC):
                kb.op("dve", lambda e, k=k: e.scalar_tensor_tensor(
                    out=dst[:, k, t0:t0 + tn], in0=xT[:, k, t0:t0 + tn], scalar=par[:, gcol + k:gcol + k + 1],
                    in1=rstd[:, :tn], op0=ALU.mult, op1=ALU.mult), [xB[ti], B["rstd"], B["par"]], [db])

    def proj_fm(W, nk, n_oc, rhs_fn, evac, tiles=TILES, col0=0, row0=0):
        for og in range(0, n_oc, 2):
            ng = min(2, n_oc - og)
            view, wb = wload([(W[row0:row0 + nk * 128, col0 + og * 128: col0 + (og + ng) * 128], nk, ng * 128)])
            for j in range(ng):
                oc = og + j
                for ti, (t0, tn) in enumerate(tiles):
                    ps, pb = bank()
                    for k in range(nk):
                        rap, rb = rhs_fn(k, ti, t0, tn)
                        mm(ps[:, :tn], view[:, k, j * 128:(j + 1) * 128], rap, k == 0, k == nk - 1, [wb, rb], [pb], k == nk - 1)
                    evac(oc, ti, t0, tn, ps, pb)

    def h_rhs(k, ti, t0, tn):
        return hT[:, k, t0:t0 + tn], hB[ti]

    def add_to_x(oc, ti, t0, tn, ps, pb):
        kb.op("dve", lambda e: e.tensor_tensor(out=xT[:, oc, t0:t0 + tn], in0=xT[:, oc, t0:t0 + tn], in1=ps[:, :tn], op=ALU.add),
              [pb], [xB[ti]])

    def shortconv(li, j):
        yv = big[:, 0:KC * NT].rearrange("p (k t) -> p k t", k=KC)
        kb.barrier()
        kb.dma("sp", sa_sb[:], sa_d, [], [B["sa_sb"]], d_in)
        rmsnorm(P_NMIX + li * 8)
        kb.op("dve", lambda e: e.memset(zbuf[:, 0:2], 0.0), [], [B["zbuf"]])
        for c in range(KC):
            W = a_w_in[j]
            wv = [wload([(W[:, g * D + c * 128: g * D + (c + 1) * 128], KC, 128)]) for g in range(3)]
            wc = P_ACONV + (j * 3) * 8 + c
            kb.op("act", lambda e: e.copy(out=zs[:, :, 0:2], in_=sa_sb[:, j, c, :, :]), [B["sa_sb"]], [B["zs"]])
            for ti, (t0, tn) in enumerate(TILES):
                pss = []
                for g in range(3):
                    ps, pb = bank()
                    for k in range(KC):
                        mm(ps[:, :tn], wv[g][0][:, k, :], hT[:, k, t0:t0 + tn], k == 0, k == KC - 1,
                           [wv[g][1], hB[ti]], [pb], k == KC - 1)
                    pss.append((ps, pb))
                (pgb, bgb), (pgc, bgc), (pu, bu) = pss
                kb.op("act", lambda e: e.copy(out=t1[:, :tn], in_=pu[:, :tn]), [bu], [B["t1"]])
                kb.op("act", lambda e: e.copy(out=t2[:, :tn], in_=pgb[:, :tn]), [bgb], [B["t2"]])
                samp = t0 >= SEQ
                if not samp:
                    zw = zbuf[:, 2 + t0:2 + t0 + tn]
                    kb.op("dve", lambda e: e.tensor_tensor(out=zw, in0=pgc[:, :tn], in1=t1[:, :tn], op=ALU.mult),
                          [bgc, B["t1"]], [B["zbuf"]])
                    zsh = [zbuf[:, t0 + d:t0 + d + tn] for d in range(3)]
                    accv = t3[:, :tn]
                    t2v = t2[:, :tn]
                    yo = yv[:, c, t0:t0 + tn]
                    zB = B["zbuf"]
                else:
                    zw = zs[:, :, 2:6]
                    kb.op("dve", lambda e: e.tensor_tensor(out=zw, in0=pgc[:, :tn].rearrange("p (s t) -> p s t", t=LS),
                                                           in1=t1[:, :tn].rearrange("p (s t) -> p s t", t=LS), op=ALU.mult),
                          [bgc, B["t1"]], [B["zs"]])
                    zsh = [zs[:, :, d:d + 4] for d in range(3)]
                    accv = t3[:, :tn].rearrange("p (s t) -> p s t", t=LS)
                    t2v = t2[:, :tn].rearrange("p (s t) -> p s t", t=LS)
                    yo = yv[:, c, t0:t0 + tn].rearrange("p (s t) -> p s t", t=LS)
                    zB = B["zs"]
                kb.op("dve", lambda e: e.tensor_scalar(out=accv, in0=zsh[0], scalar1=par[:, wc:wc + 1], scalar2=None, op0=ALU.mult),
                      [zB, B["par"]], [B["t3"]])
                for d in (1, 2):
                    kb.op("dve", lambda e, d=d: e.scalar_tensor_tensor(out=accv, in0=zsh[d], scalar=par[:, wc + 8 * d:wc + 8 * d + 1],
                                                                     in1=accv, op0=ALU.mult, op1=ALU.add),
                          [zB, B["par"]], [B["t3"]])
                kb.op("dve", lambda e: e.tensor_tensor(out=yo, in0=accv, in1=t2v, op=ALU.mult), [B["t2"]], [B["t3"], gB[ti]])
            kb.op("act", lambda e: e.copy(out=oa_sb[:, j, c, 0:2], in_=zbuf[:, SEQ:SEQ + 2]), [B["zbuf"]], [B["oa_sb"]])
            kb.op("act", lambda e: e.copy(out=oa_sb[:, j, c, 2:2 + 2 * NS].rearrange("p (s r) -> p s r", r=2), in_=zs[:, :, 4:6]),
                  [B["zs"]], [B["oa_sb"]])
        kb.dma("sp", oa_d[:, j], oa_sb[:, j], [B["oa_sb"]], [], d_out)
        proj_fm(a_w_out[j], KC, KC, lambda k, ti, t0, tn: (yv[:, k, t0:t0 + tn], gB[ti]), add_to_x)


    def gdn(li):
        kb.barrier()
        W = b_w_in[0]
        BT = 256
        ident_bf = cstb[:, C_IDB:C_IDB + 128]

        def v3(ap, h=8):
            return ap.rearrange("p (h t) -> p h t", h=h)
        qT = v3(carve(Bg, 0, 8 * BT)); kT = v3(carve(Bg, 4096, 8 * BT)); vT = v3(carve(Bg, 8192, 8 * BT)); zT = v3(carve(Bg, 12288, 8 * BT))
        oraw = v3(carve(Bg, 16384, 8 * BT))
        Tb = carve(Bg, 20480, 512); bgk = carve(Bg, 21504, 1024); kdec = carve(Bg, 23552, 1024); bv = carve(Bg, 25600, 1024)
        nkdT = carve(Bg, 27648, 512); w_sb = carve(Bg, 28672, 1024); aqkT = carve(Bg, 30720, 512); qgT = carve(Bg, 31744, 512)
        ktm = ptb[:, :, :].rearrange("p a n -> p (a n)")
        egr = rstd
        raw = carve(Cg, 0, 260, F32); carry = carve(Cg, 1040, 72, F32).rearrange("p (c r) -> p c r", r=3)
        S = carve(Cg, 1328, 1024, F32); Sb = carve(Cg, 5424, 1024)
        negGU = carve(Cg, 7472, 512, F32); decLs = carve(Cg, 9520, 512, F32); decU = carve(Cg, 11568, 512, F32)
        LN = [carve(Cg, 13616 + i * 2048, 512) for i in range(4)]
        Q = carve(Cg, 21808, 512)
        gc = carve(Cg, 23856, NGC, F32)
        sm = carve(Cg, 23856 + 4 * NGC, 160, F32)
        nb = {n: Buf("g_" + n) for n in ("q", "k", "v", "z", "oraw", "Tb", "bgk", "kdec", "bv", "nkdT", "w", "aqkT", "qgT", "ktm",
                                         "raw", "carry", "S", "Sb", "negGU", "decLs", "decU", "L0", "L1", "L2", "L3", "Q", "gc",
                                         "ba", "beta", "g", "gam", "egam", "ekd", "bg", "egl", "negA", "xa")}
        LNB = [nb["L0"], nb["L1"], nb["L2"], nb["L3"]]
        kb.dma("sp", gc[0:64, :], gcst_d, [], [nb["gc"]], d_in)
        IDf = gc[:, G_ID:G_ID + 64]; IDrep = gc[:, G_IDREP:G_IDREP + 512]; Ucs = gc[:, G_U:G_U + 64]; negU = gc[:, G_NEGU:G_NEGU + 64]
        ones64 = gc[:, G_ONES:G_ONES + 128]; negones64 = gc[:, G_NEGONES:G_NEGONES + 128]
        Mlow = gc[:, G_MLOW:G_MLOW + 512]; Mup = gc[:, G_MUP:G_MUP + 512]
        ba_sb = sm[:, 0:16]; beta = sm[:, 16:24]; gg = sm[:, 24:32]; gam = sm[:, 32:40]; egam = sm[:, 40:48]; ekd = sm[:, 48:56]
        bgs = sm[:, 56:64]; egl = sm[:, 64:72]; negA = sm[:, 72:80]; xa = sm[:, 80:88]
        dtb = par[:, P_DTB:P_DTB + 8]
        kb.op("act", lambda e: e.activation(out=negA[:, :], in_=par[:, P_ALOG:P_ALOG + 8], func=AF.Exp), [B["par"]], [nb["negA"]])
        kb.op("dve", lambda e: e.tensor_scalar(out=negA[:, :], in0=negA[:, :], scalar1=-1.0, scalar2=None, op0=ALU.mult), [], [nb["negA"]])
        kb.op("dve", lambda e: e.memset(carry[:, :, :], 0.0), [], [nb["carry"]])
        kb.op("dve", lambda e: e.memset(S[:, :], 0.0), [], [nb["S"]])
        kb.op("dve", lambda e: e.memset(Sb[:, :], 0.0), [], [nb["Sb"]])

        def r3(ap, c):
            return ap

        def rep(cap, c):
            return cap.rearrange("p (h j) -> p h j", h=8)[0:c, :, 0:c]

        def fl(ap, c):
            return ap[0:c, 0:8 * c].rearrange("p (h j) -> p h j", h=8)

        def bc8(v, c, n):
            return v[0:c, :].unsqueeze(2).to_broadcast([c, 8, n])

        def chunk(c, col0, ba_ps, ba_b, hcols_unused=None):
            nlev = {64: 5, 4: 1}[c]
            kb.op("act", lambda e: e.activation(out=beta[0:c, :], in_=ba_ps[:, 0:8], func=AF.Sigmoid), [ba_b], [nb["beta"]])
            kb.op("dve", lambda e: e.tensor_tensor(out=xa[0:c, :], in0=ba_ps[:, 8:16], in1=dtb[0:c, :], op=ALU.add), [ba_b, B["par"]], [nb["xa"]])
            kb.op("act", lambda e: e.activation(out=xa[0:c, :], in_=xa[0:c, :], func=AF.Exp), [], [nb["xa"]])
            kb.op("act", lambda e: e.activation(out=xa[0:c, :], in_=xa[0:c, :], func=AF.Ln, bias=1.0, scale=1.0), [], [nb["xa"]])
            kb.op("dve", lambda e: e.tensor_tensor(out=gg[0:c, :], in0=xa[0:c, :], in1=negA[0:c, :], op=ALU.mult), [nb["xa"], nb["negA"]], [nb["g"]])
            pA, bA = bank()
            mm(pA[0:c, 0:8], Ucs[0:c, 0:c], gg[0:c, :], True, True, [nb["gc"], nb["g"]], [bA], False)
            mm(pA[0:c, 8:16], ones64[0:c, 0:c], gg[0:c, :], True, True, [nb["gc"], nb["g"]], [bA], False)
            mm(pA[:, 16:24], ones64[0:c, 0:128], gg[0:c, :], True, True, [nb["gc"], nb["g"]], [bA], True)
            kb.op("act", lambda e: e.copy(out=gam[0:c, :], in_=pA[0:c, 0:8]), [bA], [nb["gam"]])
            kb.op("act", lambda e: e.activation(out=egam[0:c, :], in_=pA[0:c, 0:8], func=AF.Exp), [bA], [nb["egam"]])
            kb.op("dve", lambda e: e.tensor_tensor(out=ekd[0:c, :], in0=pA[0:c, 8:16], in1=gam[0:c, :], op=ALU.subtract), [bA, nb["gam"]], [nb["ekd"]])
            kb.op("act", lambda e: e.activation(out=ekd[0:c, :], in_=ekd[0:c, :], func=AF.Exp), [], [nb["ekd"]])
            kb.op("dve", lambda e: e.tensor_tensor(out=bgs[0:c, :], in0=beta[0:c, :], in1=egam[0:c, :], op=ALU.mult), [nb["beta"], nb["egam"]], [nb["bg"]])
            kb.op("act", lambda e: e.activation(out=egl[:, :], in_=pA[:, 16:24], func=AF.Exp), [bA], [nb["egl"]])
            kb.op("dve", lambda e: e.tensor_tensor(out=fl(negGU, c), in0=bc8(gg, c, c),
                                                   in1=negU[0:c, 0:c].unsqueeze(1).to_broadcast([c, 8, c]), op=ALU.mult),
                  [nb["g"], nb["gc"]], [nb["negGU"]])
            pD, bD = bank()
            for h in range(8):
                mm(pD[0:c, h * c:(h + 1) * c], ones64[0:c, 0:c], negGU[0:c, h * c:(h + 1) * c], True, False, [nb["gc"], nb["negGU"]], [bD], False)
                mm(pD[0:c, h * c:(h + 1) * c], negGU[0:c, h * c:(h + 1) * c], negones64[0:c, 0:c], False, True, [nb["gc"], nb["negGU"]], [bD], h == 7)
            kb.op("dve", lambda e: e.tensor_tensor(out=fl(decLs, c), in0=fl(pD, c), in1=rep(Mlow, c), op=ALU.add), [bD, nb["gc"]], [nb["decLs"]])
            kb.op("act", lambda e: e.activation(out=decLs[0:c, 0:8 * c], in_=decLs[0:c, 0:8 * c], func=AF.Exp), [], [nb["decLs"]])
            kb.op("dve", lambda e: e.scalar_tensor_tensor(out=fl(decU, c), in0=fl(pD, c), scalar=-1.0, in1=rep(Mup, c), op0=ALU.mult, op1=ALU.add),
                  [bD, nb["gc"]], [nb["decU"]])
            kb.op("act", lambda e: e.activation(out=decU[0:c, 0:8 * c], in_=decU[0:c, 0:8 * c], func=AF.Exp), [], [nb["decU"]])
            pK, bK = bank()
            for h in range(8):
                mm(pK[0:c, h * c:(h + 1) * c], kT[:, h, col0:col0 + c], kT[:, h, col0:col0 + c], True, True, [nb["k"]], [bK], h == 7)
            L0, N0 = LN[0], LN[1]
            kb.op("dve", lambda e: e.tensor_tensor(out=t2[0:c, 0:8 * c], in0=pK[0:c, 0:8 * c], in1=decLs[0:c, 0:8 * c], op=ALU.mult),
                  [bK, nb["decLs"]], [B["t2"]])
            kb.op("dve", lambda e: e.tensor_tensor(out=fl(L0, c), in0=fl(t2, c), in1=bc8(beta, c, c), op=ALU.mult), [nb["beta"], B["t2"]], [LNB[0]])
            pN_, bN = bank()
            pN = pN_[:, :].bitcast(BF)
            for h in range(8):
                kb.op("pe", lambda e, h=h: e.transpose(pN[0:c, h * c:(h + 1) * c], L0[0:c, h * c:(h + 1) * c], ident_bf[0:c, 0:c]),
                      [LNB[0], B["cstb"]], [bN], h == 7)
            kb.op("act", lambda e: e.copy(out=N0[0:c, 0:8 * c], in_=pN[0:c, 0:8 * c]), [bN], [LNB[1]])
            kb.op("dve", lambda e: e.tensor_tensor(out=fl(Q, c), in0=rep(IDrep, c), in1=fl(pN, c), op=ALU.subtract), [bN, nb["gc"]], [nb["Q"]])
            li_, ni_ = 0, 1
            for lv in range(nlev):
                last = lv == nlev - 1
                Lp, Np = LN[li_], LN[ni_]
                free = [i for i in range(4) if i not in (li_, ni_)]
                l2, n2 = free[0], free[1]
                pL, bL = bank()
                for h in range(8):
                    mm(pL[0:c, h * c:(h + 1) * c], Np[0:c, h * c:(h + 1) * c], Lp[0:c, h * c:(h + 1) * c], True, True, [LNB[li_], LNB[ni_]], [bL], h == 7)
                if not last:
                    pN2, bN2 = bank()
                    for h in range(8):
                        mm(pN2[0:c, h * c:(h + 1) * c], Lp[0:c, h * c:(h + 1) * c], Np[0:c, h * c:(h + 1) * c], True, True, [LNB[li_], LNB[ni_]], [bN2], h == 7)
                kb.op("act", lambda e: e.copy(out=LN[l2][0:c, 0:8 * c], in_=pL[0:c, 0:8 * c]), [bL], [LNB[l2]])
                if not last:
                    kb.op("dve", lambda e: e.tensor_copy(out=LN[n2][0:c, 0:8 * c], in_=pN2[0:c, 0:8 * c]), [bN2], [LNB[n2]])
                pQ, bQ = bank()
                for h in range(8):
                    mm(pQ[0:c, h * c:(h + 1) * c], LN[l2][0:c, h * c:(h + 1) * c], Q[0:c, h * c:(h + 1) * c], True, True, [LNB[l2], nb["Q"]], [bQ], h == 7)
                kb.op("dve", lambda e: e.tensor_tensor(out=Q[0:c, 0:8 * c], in0=Q[0:c, 0:8 * c], in1=pQ[0:c, 0:8 * c], op=ALU.add), [bQ], [nb["Q"]])
                li_, ni_ = l2, n2
            Tb = Q
            nb["Tb"] = nb["Q"]
            pT, bT = bank()
            pTb = pT[:, :].bitcast(BF)
            for h in range(8):
                kb.op("pe", lambda e, h=h: e.transpose(pTb[0:c, h * 128:(h + 1) * 128], kT[:, h, col0:col0 + c], ident_bf),
                      [nb["k"], B["cstb"]], [bT], h == 7)
            kb.op("act", lambda e: e.copy(out=ktm[0:c, :], in_=pTb[0:c, :]), [bT], [nb["ktm"]])
            k3 = ktm[0:c, :].rearrange("p (h d) -> p h d", h=8)
            kb.op("dve", lambda e: e.tensor_tensor(out=bgk[0:c, :].rearrange("p (h d) -> p h d", h=8), in0=k3, in1=bc8(bgs, c, 128), op=ALU.mult),
                  [nb["ktm"], nb["bg"]], [nb["bgk"]])
            kb.op("dve", lambda e: e.tensor_tensor(out=kdec[0:c, :].rearrange("p (h d) -> p h d", h=8), in0=k3, in1=bc8(ekd, c, 128), op=ALU.mult),
                  [nb["ktm"], nb["ekd"]], [nb["kdec"]])
            pV, bV = bank()
            pVb = pV[:, :].bitcast(BF)
            for h in range(8):
                kb.op("pe", lambda e, h=h: e.transpose(pVb[0:c, h * 128:(h + 1) * 128], vT[:, h, col0:col0 + c], ident_bf),
                      [nb["v"], B["cstb"]], [bV], h == 7)
            kb.op("dve", lambda e: e.tensor_tensor(out=bv[0:c, :].rearrange("p (h d) -> p h d", h=8),
                                                   in0=pVb[0:c, :].rearrange("p (h d) -> p h d", h=8), in1=bc8(beta, c, 128), op=ALU.mult),
                  [bV, nb["beta"]], [nb["bv"]])
            pKd, bKd = bank()
            for h in range(8):
                mm(pKd[:, h * c:(h + 1) * c], bgk[0:c, h * 128:(h + 1) * 128], Tb[0:c, h * c:(h + 1) * c], True, True, [nb["bgk"], nb["Tb"]], [bKd], h == 7)
            kb.op("act", lambda e: e.mul(out=nkdT[:, 0:8 * c], in_=pKd[:, 0:8 * c], mul=-1.0), [bKd], [nb["nkdT"]])
            pW = [bank(), bank()]
            for h in range(8):
                pw, bw = pW[h // 4]
                hh = h % 4
                mm(pw[0:c, hh * 128:(hh + 1) * 128], Tb[0:c, h * c:(h + 1) * c], bv[0:c, h * 128:(h + 1) * 128], True, False, [nb["Tb"], nb["bv"]], [bw], False)
                mm(pw[0:c, hh * 128:(hh + 1) * 128], nkdT[:, h * c:(h + 1) * c], Sb[:, h * 128:(h + 1) * 128], False, True, [nb["nkdT"], nb["Sb"]], [bw], hh == 3)
            for i2 in range(2):
                kb.op("act", lambda e, i2=i2: e.copy(out=w_sb[0:c, i2 * 512:(i2 + 1) * 512], in_=pW[i2][0][0:c, :]), [pW[i2][1]], [nb["w"]])
            pA2, bA2 = bank()
            for h in range(8):
                mm(pA2[0:c, h * c:(h + 1) * c], kT[:, h, col0:col0 + c], qT[:, h, col0:col0 + c], True, True, [nb["k"], nb["q"]], [bA2], h == 7)
            kb.op("dve", lambda e: e.tensor_tensor(out=aqkT[0:c, 0:8 * c], in0=pA2[0:c, 0:8 * c], in1=decU[0:c, 0:8 * c], op=ALU.mult),
                  [bA2, nb["decU"]], [nb["aqkT"]])
            pE, bE = bank()
            mm(pE[:, 0:8 * c], negones64[0:c, 0:128], negGU[0:c, 0:8 * c], True, True, [nb["gc"], nb["negGU"]], [bE], True)
            kb.op("act", lambda e: e.activation(out=egr[:, 0:8 * c], in_=pE[:, 0:8 * c], func=AF.Exp), [bE], [B["rstd"]])
            kb.op("dve", lambda e: e.tensor_tensor(out=qgT[:, 0:8 * c].rearrange("p (h j) -> p h j", h=8), in0=qT[:, :, col0:col0 + c],
                                                   in1=egr[:, 0:8 * c].rearrange("p (h j) -> p h j", h=8), op=ALU.mult),
                  [nb["q"], B["rstd"]], [nb["qgT"]])
            pO, bO = bank()
            for h in range(8):
                mm(pO[:, h * c:(h + 1) * c], Sb[:, h * 128:(h + 1) * 128], qgT[:, h * c:(h + 1) * c], True, False, [nb["Sb"], nb["qgT"]], [bO], False)
                mm(pO[:, h * c:(h + 1) * c], w_sb[0:c, h * 128:(h + 1) * 128], aqkT[0:c, h * c:(h + 1) * c], False, True, [nb["w"], nb["aqkT"]], [bO], h == 7)
            kb.op("act", lambda e: e.copy(out=oraw[:, :, col0:col0 + c], in_=pO[:, 0:8 * c].rearrange("p (h j) -> p h j", h=8)), [bO], [nb["oraw"]])
            pS2 = [bank(), bank()]
            for h in range(8):
                p2, b2 = pS2[h // 4]
                hh = h % 4
                mm(p2[:, hh * 128:(hh + 1) * 128], kdec[0:c, h * 128:(h + 1) * 128], w_sb[0:c, h * 128:(h + 1) * 128], True, True, [nb["kdec"], nb["w"]], [b2], hh == 3)
            kb.op("dve", lambda e: e.tensor_tensor(out=S[:, :].rearrange("p (h d) -> p h d", h=8), in0=S[:, :].rearrange("p (h d) -> p h d", h=8),
                                                   in1=egl[:, :].unsqueeze(2).to_broadcast([128, 8, 128]), op=ALU.mult), [nb["egl"]], [nb["S"]])
            for i2 in range(2):
                kb.op("dve", lambda e, i2=i2: e.tensor_tensor(out=S[:, i2 * 512:(i2 + 1) * 512], in0=S[:, i2 * 512:(i2 + 1) * 512], in1=pS2[i2][0][:, :], op=ALU.add),
                      [pS2[i2][1]], [nb["S"]])
            kb.op("act", lambda e: e.copy(out=Sb[:, :], in_=S[:, :]), [nb["S"]], [nb["Sb"]])

        def in_proj(ti, t0, tn, samp):
            if samp:
                raws = raw[:, 0:NS * 7].rearrange("p (s t) -> p s t", t=7)
            for cg in range(0, 24, 2):
                view, wb = wload([(W[:, cg * 128:(cg + 2) * 128], KC, 256)])
                for j2 in range(2):
                    cc = cg + j2
                    ps, pb = bank()
                    for k in range(KC):
                        mm(ps[:, :tn], view[:, k, j2 * 128:(j2 + 1) * 128], hT[:, k, t0:t0 + tn], k == 0, k == KC - 1, [wb, hB[ti]], [pb], k == KC - 1)
                    wc = P_BCONV + cc
                    if not samp:
                        kb.op("act", lambda e: e.copy(out=raw[:, 0:3], in_=carry[:, cc, :]), [nb["carry"]], [nb["raw"]])
                        kb.op("act", lambda e: e.copy(out=raw[:, 3:3 + tn], in_=ps[:, :tn]), [pb], [nb["raw"]])
                        kb.op("act", lambda e: e.copy(out=carry[:, cc, :], in_=raw[:, tn:tn + 3]), [nb["raw"]], [nb["carry"]])
                        win = [raw[:, d:d + tn] for d in range(4)]
                        acc = t3[:, :tn]
                    else:
                        kb.dma("sp", raws[:, :, 0:3], sbc_d[cc], [], [nb["raw"]], d_in)
                        kb.op("act", lambda e: e.copy(out=raws[:, :, 3:7], in_=ps[:, :tn].rearrange("p (s t) -> p s t", t=LS)), [pb], [nb["raw"]])
                        kb.dma("sp", obc_s_d[cc], raws[:, :, 4:7], [nb["raw"]], [], d_out)
                        win = [raws[:, :, d:d + 4] for d in range(4)]
                        acc = t3[:, :tn].rearrange("p (s t) -> p s t", t=LS)
                    kb.op("dve", lambda e: e.tensor_scalar(out=acc, in0=win[0], scalar1=par[:, wc:wc + 1], scalar2=None, op0=ALU.mult),
                          [nb["raw"], B["par"]], [B["t3"]])
                    for d in (1, 2, 3):
                        kb.op("dve", lambda e, d=d: e.scalar_tensor_tensor(out=acc, in0=win[d], scalar=par[:, wc + 24 * d:wc + 24 * d + 1], in1=acc,
                                                                         op0=ALU.mult, op1=ALU.add), [nb["raw"], B["par"]], [B["t3"]])
                    h = cc % 8
                    if cc >= 16:
                        kb.op("act", lambda e: e.activation(out=vT[:, h, 0:tn], in_=t3[:, :tn], func=AF.Silu), [B["t3"]], [nb["v"]])
                    else:
                        dst, dB = (qT, nb["q"]) if cc < 8 else (kT, nb["k"])
                        kb.op("act", lambda e: e.activation(out=t2[:, :tn], in_=t3[:, :tn], func=AF.Silu), [B["t3"]], [B["t2"]])
                        kb.op("act", lambda e: e.activation(out=sq[:, 0, :tn], in_=t2[:, :tn], func=AF.Square), [B["t2"]], [B["sq"]])
                        p2, b2 = bank()
                        mm(p2[:, :tn], ones_bf, sq[:, 0, :tn], True, True, [B["sq"], B["cstb"]], [b2], True)
                        kb.op("act", lambda e: e.activation(out=t1[:, :tn], in_=p2[:, :tn], func=AF.Sqrt, bias=EPS, scale=1.0), [b2], [B["t1"]])
                        kb.op("dve", lambda e: e.reciprocal(out=t1[:, :tn], in_=t1[:, :tn]), [], [B["t1"]])
                        scl = (128.0 ** -0.5) if cc < 8 else 1.0
                        kb.op("dve", lambda e: e.scalar_tensor_tensor(out=dst[:, h, 0:tn], in0=t2[:, :tn], scalar=scl, in1=t1[:, :tn],
                                                                      op0=ALU.mult, op1=ALU.mult), [B["t2"], B["t1"]], [dB])
            for zg in range(0, 8, 2):
                view, wb = wload([(W[:, 3072 + zg * 128:3072 + (zg + 2) * 128], KC, 256)])
                for j2 in range(2):
                    ps, pb = bank()
                    for k in range(KC):
                        mm(ps[:, :tn], view[:, k, j2 * 128:(j2 + 1) * 128], hT[:, k, t0:t0 + tn], k == 0, k == KC - 1, [wb, hB[ti]], [pb], k == KC - 1)
                    kb.op("act", lambda e: e.activation(out=zT[:, zg + j2, 0:tn], in_=ps[:, :tn], func=AF.Silu), [pb], [nb["z"]])

        def out_norm(ti, t0, tn):
            kb.op("act", lambda e: e.activation(out=sq[:, :, :tn], in_=oraw[:, :, 0:tn], func=AF.Square), [nb["oraw"]], [B["sq"]])
            for h in range(8):
                p2, b2 = bank()
                mm(p2[:, :tn], ones_bf, sq[:, h, :tn], True, True, [B["sq"], B["cstb"]], [b2], True)
                kb.op("act", lambda e: e.activation(out=t1[:, :tn], in_=p2[:, :tn], func=AF.Sqrt, bias=EPS, scale=1.0 / 128.0), [b2], [B["t1"]])
                kb.op("dve", lambda e: e.reciprocal(out=t1[:, :tn], in_=t1[:, :tn]), [], [B["t1"]])
                kb.op("dve", lambda e: e.scalar_tensor_tensor(out=t2[:, :tn], in0=oraw[:, h, 0:tn], scalar=par[:, P_BNORM:P_BNORM + 1], in1=t1[:, :tn],
                                                              op0=ALU.mult, op1=ALU.mult), [nb["oraw"], B["t1"], B["par"]], [B["t2"]])
                kb.op("dve", lambda e: e.tensor_tensor(out=hT[:, h, t0:t0 + tn], in0=t2[:, :tn], in1=zT[:, h, 0:tn], op=ALU.mult),
                      [B["t2"], nb["z"]], [hB[ti]])

        rmsnorm(P_NMIX + li * 8)
        for blk in range(SEQ // BT):
            t0 = blk * BT
            ti = t0 // 512
            in_proj(ti, t0, BT, False)
            wba, wbb = wload([(W[:, 4096:4112], KC, 16)])
            pBA, bBA = bank()
            for ci in range(BT // 64):
                for k in range(KC):
                    mm(pBA[0:64, ci * 16:(ci + 1) * 16], hT[:, k, t0 + ci * 64:t0 + (ci + 1) * 64], wba[:, k, :], k == 0, k == KC - 1,
                       [wbb, hB[ti]], [bBA], k == KC - 1 and ci == BT // 64 - 1)
            kb.op("act", lambda e: e.copy(out=t1[0:64, 0:64], in_=pBA[0:64, 0:64]), [bBA], [B["t1"]])
            for ci in range(BT // 64):
                chunk(64, ci * 64, t1[0:64, ci * 16:(ci + 1) * 16], B["t1"])
            out_norm(ti, t0, BT)
        kb.dma("sp", obc_p_d, carry[:, :, :], [nb["carry"]], [], d_out)
        kb.dma("sp", obr_p_d.rearrange("h k v -> k h v"), S[:, :].rearrange("p (h d) -> p h d", h=8), [nb["S"]], [], d_out)
        in_proj(4, SEQ, NSAMP, True)
        wba, wbb = wload([(W[:, 4096:4112], KC, 16)])
        pBA, bBA = bank()
        for s_ in range(NS):
            for k in range(KC):
                mm(pBA[0:4, s_ * 16:(s_ + 1) * 16], hT[:, k, SEQ + s_ * 4:SEQ + s_ * 4 + 4], wba[:, k, :], k == 0, k == KC - 1,
                   [wbb, hB[4]], [bBA], k == KC - 1 and s_ == NS - 1)
        kb.op("act", lambda e: e.copy(out=t1[0:4, 0:256], in_=pBA[0:4, 0:256]), [bBA], [B["t1"]])
        for s_ in range(NS):
            kb.dma("sp", S[:, :].rearrange("p (h d) -> p h d", h=8), sbr_d[s_].rearrange("h k v -> k h v"), [], [nb["S"]], d_in)
            kb.op("act", lambda e: e.copy(out=Sb[:, :], in_=S[:, :]), [nb["S"]], [nb["Sb"]])
            chunk(4, s_ * 4, t1[0:4, s_ * 16:(s_ + 1) * 16], B["t1"])
            kb.dma("sp", obr_s_d[s_].rearrange("h k v -> k h v"), S[:, :].rearrange("p (h d) -> p h d", h=8), [nb["S"]], [], d_out)
        out_norm(4, SEQ, NSAMP)
        proj_fm(b_w_out[0], KC, KC, h_rhs, add_to_x)


    def moba(li):
        kb.barrier()
        W = c_w_qkv[0]
        ident_bf = cstb[:, C_IDB:C_IDB + 128]
        scm = 128.0 ** -0.5
        KT = carve(Bg, 0, 8 * SEQ).rearrange("p (h t) -> p h t", h=8)
        ksamp = carve(Bg, 32768, 512).rearrange("p (h t) -> p h t", h=8)
        V = carve(Cg, 0, 16 * D).rearrange("p (a n) -> p a n", a=16)
        qb = sq[:, :, 0:256]
        P2 = ptb[:, :, :].rearrange("p a n -> p (a n)")[:, 0:512].rearrange("p (a n) -> p a n", a=2)
        PV2 = [ptb[:, :, :].rearrange("p a n -> p (a n)")[:, i * 512:(i + 1) * 512].rearrange("p (a n) -> p a n", a=2) for i in range(2)]
        PB2 = [B["ptb"], Buf("ptb_b")]
        tselv = [tsel, tsel2]
        tselB = [Buf("tsel0"), Buf("tsel1")]
        nb = {n: Buf("m_" + n) for n in ("KT", "ksamp", "V", "vsamp", "qb", "km", "kmb", "sel", "selb", "negT", "tsel", "qs", "qrep",
                                         "kst", "ksb", "vst", "vall", "ksum", "kmsT", "idx", "selS", "PT", "pown")}
        rmsnorm(P_NMIX + li * 8)
        def evac_kT(oc, ti, t0, tn, ps, pb):
            if t0 < SEQ:
                kb.op("act", lambda e: e.copy(out=KT[:, oc, t0:t0 + tn], in_=ps[:, :tn]), [pb], [nb["KT"]])
            else:
                kb.op("act", lambda e: e.copy(out=ksamp[:, oc, :], in_=ps[:, :tn]), [pb], [nb["ksamp"]])
            kb.op("dve", lambda e: e.tensor_copy(out=t3[:, :tn], in_=ps[:, :tn]), [pb], [B["t3"]])
            kb.dma("sp", ock_d[:, oc, t0:t0 + tn], t3[:, :tn], [B["t3"]], [], d_out)
        proj_fm(W, KC, 8, h_rhs, evac_kT, col0=D)
        for cg in range(0, D, 256):
            view, wb = wload([(W[:, 2 * D + cg:2 * D + cg + 256], KC, 256)])
            for tt in range(17):
                m = 128 if tt < 16 else NSAMP
                ti = min(tt // 4, 4)
                ps, pb = bank()
                for k in range(KC):
                    mm(ps[0:m, :256], hT[:, k, tt * 128:tt * 128 + m], view[:, k, :], k == 0, k == KC - 1, [wb, hB[ti]], [pb], k == KC - 1)
                if tt < 16:
                    kb.op("act", lambda e: e.copy(out=V[:, tt, cg:cg + 256], in_=ps[:, :256]), [pb], [nb["V"]])
                kb.op("dve", lambda e: e.tensor_copy(out=t3[0:m, 0:256], in_=ps[0:m, :256]), [pb], [B["t3"]])
                kb.dma("sp", ocv_d[tt * 128:tt * 128 + m, cg:cg + 256], t3[0:m, 0:256], [B["t3"]], [], d_out)
        def evac_vs(oc, ti, t0, tn, ps, pb):
            kb.op("act", lambda e: e.copy(out=vsT[:, oc, :], in_=ps[:, :tn]), [pb], [nb["vsamp"]])
        proj_fm(W, KC, 8, lambda k, ti, t0, tn: (hT[:, k, t0:t0 + tn], hB[4]), evac_vs, tiles=[(SEQ, NSAMP)], col0=2 * D)
        kb.op("dve", lambda e: e.tensor_reduce(out=km[:, :, :], in_=KT[:, :, :].rearrange("p h (n t) -> p h n t", t=256), axis=AX.X, op=ALU.add),
              [nb["KT"]], [nb["km"]])
        kb.op("act", lambda e: e.mul(out=kmb[:, :, :], in_=km[:, :, :], mul=1.0 / 256.0), [nb["km"]], [nb["kmb"]])
        for b in range(8):
            q0 = b * 256
            ti = q0 // 512
            for hg in range(0, 8, 2):
                view, wb = wload([(W[:, hg * 128:(hg + 2) * 128], KC, 256)])
                for j2 in range(2):
                    ps, pb = bank()
                    for k in range(KC):
                        mm(ps[:, :256], view[:, k, j2 * 128:(j2 + 1) * 128], hT[:, k, q0:q0 + 256], k == 0, k == KC - 1, [wb, hB[ti]], [pb], k == KC - 1)
                    kb.op("act", lambda e: e.copy(out=qb[:, hg + j2, :], in_=ps[:, :256]), [pb], [nb["qb"]])
            if b >= 4:
                kb.op("dve", lambda e: e.memset(selb[:, :], 0.0), [], [nb["selb"]])
                pT, bT = bank()
                pTb = pT[:, :].bitcast(BF)
                for half in range(2):
                    pSel, bSel = bank()
                    for h in range(8):
                        mm(pSel[:, h * 8:h * 8 + b], qb[:, h, half * 128:(half + 1) * 128], kmb[:, h, 0:b], True, True, [nb["qb"], nb["kmb"]], [bSel], h == 7)
                    s3 = sel[:, 0:8 * b].rearrange("p (h n) -> p h n", h=8)
                    s3b = sel[:, 64:64 + 8 * b].rearrange("p (h n) -> p h n", h=8)
                    e3 = sel[:, 128:128 + 8 * b].rearrange("p (h n) -> p h n", h=8)
                    mx = sel[:, 192:200]

                    def mxb():
                        return mx.unsqueeze(2).to_broadcast([128, 8, b])
                    kb.op("act", lambda e: e.copy(out=s3, in_=pSel[:, 0:64].rearrange("p (h n) -> p h n", h=8)[:, :, 0:b]), [bSel], [nb["sel"]])
                    cur = s3
                    for it in range(2):
                        kb.op("dve", lambda e: e.tensor_reduce(out=mx, in_=cur, axis=AX.X, op=ALU.max), [], [nb["sel"]])
                        kb.op("dve", lambda e: e.tensor_tensor(out=e3, in0=cur, in1=mxb(), op=ALU.is_equal), [], [nb["sel"]])
                        kb.op("dve", lambda e: e.scalar_tensor_tensor(out=s3b, in0=e3, scalar=-1e30, in1=cur, op0=ALU.mult, op1=ALU.add), [], [nb["sel"]])
                        cur = s3b
                    kb.op("dve", lambda e: e.tensor_reduce(out=mx, in_=s3b, axis=AX.X, op=ALU.max), [], [nb["sel"]])
                    kb.op("dve", lambda e: e.tensor_tensor(out=e3, in0=s3, in1=mxb(), op=ALU.is_ge), [], [nb["sel"]])
                    kb.op("dve", lambda e: e.tensor_scalar(out=selb[:, :].rearrange("p (h n) -> p h n", h=8)[:, :, 0:b], in0=e3, scalar1=-NEG, scalar2=NEG,
                                                           op0=ALU.mult, op1=ALU.add), [nb["sel"]], [nb["selb"]])
                    kb.op("pe", lambda e: e.transpose(pTb[0:64, half * 128:(half + 1) * 128], selb[:, :], ident_bf), [nb["selb"], B["cstb"]], [bT], True)
                kb.op("act", lambda e: e.copy(out=negT[:, :], in_=pTb[0:64, 0:256]), [bT], [nb["negT"]])
            pstate["n"] = 4
            pstate["i"] = 0
            for h in range(8):
                pOD, bOD = lbank(2 * (h % 2))
                pDN, bDN = lbank(2 * (h % 2) + 1)

                def scores(n, h=h):
                    pSc, bSc = bank()
                    j = n % 2
                    extra = (n == b) or (b >= 4)
                    if n < b and b >= 4:
                        r = h * 8 + n
                        kb.op("dve", lambda e: e.tensor_scalar(out=tselv[j][:, :], in0=negT[:, :], scalar1=cst[0:64, r:r + 1], scalar2=None, op0=ALU.mult),
                              [nb["negT"], B["cst"]], [tselB[j]])
                    for kc in range(2):
                        mm(pSc[:, kc * 256:(kc + 1) * 256], KT[:, h, n * 256 + kc * 128:n * 256 + (kc + 1) * 128], qb[:, h, :], True, not extra,
                           [nb["KT"], nb["qb"]], [bSc], (not extra) and kc == 1)
                        if n == b:
                            mm(pSc[:, kc * 256:(kc + 1) * 256], ident_bf, cstb[:, C_CM + kc * 256:C_CM + (kc + 1) * 256], False, True,
                               [B["cstb"]], [bSc], kc == 1)
                        elif b >= 4:
                            mm(pSc[:, kc * 256:(kc + 1) * 256], cstb[0:64, C_ONES:C_ONES + 128], tselv[j][:, :], False, True,
                               [B["cstb"], tselB[j]], [bSc], kc == 1)
                    kb.op("act", lambda e: e.activation(out=PV2[j], in_=pSc[:, :].rearrange("p (a n) -> p a n", a=2), func=AF.Exp, scale=scm),
                          [bSc], [PB2[j]])

                scores(0)
                for n in range(b + 1):
                    if n < b:
                        scores(n + 1)
                    j = n % 2
                    for kc in range(2):
                        first = (n == 0 and kc == 0)
                        lastm = (n == b and kc == 1)
                        mm(pOD[:, 0:256], V[:, n * 2 + kc, h * 128:(h + 1) * 128], PV2[j][:, kc, :], first, lastm, [nb["V"], PB2[j]], [bOD], False)
                        mm(pDN[:, 0:256], ones_bf, PV2[j][:, kc, :], first, lastm, [B["cstb"], PB2[j]], [bDN], True)
                kb.op("dve", lambda e: e.reciprocal(out=t1[:, 0:256], in_=pDN[:, 0:256]), [bDN], [B["t1"]])
                kb.op("dve", lambda e: e.tensor_tensor(out=hT[:, h, q0:q0 + 256], in0=pOD[:, 0:256], in1=t1[:, 0:256], op=ALU.mult), [bOD, B["t1"]], [hB[ti]])
            pstate["n"] = 6
            pstate["i"] = 0
        kb.barrier()
        NSL = 3
        kst = [carve(Bg, i * 4096, 1024, F32) for i in range(NSL)]
        vst = [carve(Bg, 12288 + i * 4096, 1024, F32) for i in range(NSL)]
        ksb = [carve(Bg, 24576 + i * 2048, 1024).rearrange("p (h t) -> p h t", h=8) for i in range(2)]
        ptab_i = carve(Bg, 28672, 256, I32); ptab_f = carve(Bg, 29696, 256, F32); idx_i = carve(Bg, 30720, 256, I32)
        ksum = carve(Bg, 31744, 128, F32).rearrange("p (h g) -> p h g", h=8)
        kmsT = carve(Bg, 32256, 64, F32).rearrange("p (h n) -> p h n", h=8)
        vall = carve(Cg, 0, 16 * D).rearrange("p (a n) -> p a n", a=16)
        kbs = [Buf("kst%d" % i) for i in range(NSL)]; vbs = [Buf("vst%d" % i) for i in range(NSL)]; kbb = [Buf("ksb0"), Buf("ksb1")]
        kb.dma("sp", ptab_i[:, :], pt_d.partition_broadcast(128), [], [nb["idx"]], d_in)
        kb.op("dve", lambda e: e.tensor_copy(out=ptab_f[:, :], in_=ptab_i[:, :]), [], [nb["idx"]])
        kb.op("dve", lambda e: e.tensor_scalar(out=ptab_f[:, :], in0=ptab_f[:, :], scalar1=128.0, scalar2=cst[:, C_PID:C_PID + 1], op0=ALU.mult, op1=ALU.add),
              [B["cst"]], [nb["idx"]])
        kb.op("dve", lambda e: e.tensor_copy(out=idx_i[:, :], in_=ptab_f[:, :]), [], [nb["idx"]])
        qrb = sq[:, :, :].rearrange("p a n -> p (a n)").rearrange("p (j m) -> p j m", j=32)
        nb["qrep"] = B["sq"]
        kmsb = carve(Bg, 32512, 64).rearrange("p (h n) -> p h n", h=8)
        selS = carve(Cg, 32768, 256, F32)
        d_g = [kb.dsem("gk%d" % i) for i in range(NSL)]
        d_gv = [kb.dsem("gv%d" % i) for i in range(NSL)]
        for hg in range(0, 8, 2):
            view, wb = wload([(W[:, hg * 128:(hg + 2) * 128], KC, 256)])
            for j2 in range(2):
                ps, pb = bank()
                for k in range(KC):
                    mm(ps[:, :NSAMP], view[:, k, j2 * 128:(j2 + 1) * 128], hT[:, k, SEQ:NT], k == 0, k == KC - 1, [wb, hB[4]], [pb], k == KC - 1)
                kb.op("act", lambda e: e.copy(out=qsamp[:, hg + j2, :], in_=ps[:, :NSAMP]), [pb], [nb["qs"]])
        for s_ in range(NS):
            c0 = SEQ + s_ * 4
            pS, bS = lbank(0)
            for pg in range(16):
                i3 = (s_ * 16 + pg) % NSL
                i = pg % 2
                col = s_ * 16 + pg
                kb.idma(kst[i3][:, :], poolk_d, idx_i[:, col:col + 1], [nb["idx"]], [kbs[i3]], d_g[i3])
                kb.idma(vst[i3][:, :], poolv_d, idx_i[:, col:col + 1], [nb["idx"]], [vbs[i3]], d_gv[i3])
                kb.op("act", lambda e: e.copy(out=ksb[i][:, :, :], in_=kst[i3][:, :].rearrange("p (h t) -> p h t", h=8)), [kbs[i3]], [kbb[i]])
                kb.op("dve", lambda e: e.tensor_reduce(out=ksum[:, :, pg], in_=kst[i3][:, :].rearrange("p (h t) -> p h t", h=8), axis=AX.X, op=ALU.add),
                      [kbs[i3]], [nb["ksum"]])
                kb.op("dve", lambda e: e.tensor_copy(out=vall[:, pg, :], in_=vst[i3][:, :]), [vbs[i3]], [nb["vall"]])
                for h in range(8):
                    mm(pS[:, pg * 32 + h * 4:pg * 32 + h * 4 + 4], ksb[i][:, h, :], qsamp[:, h, s_ * 4:s_ * 4 + 4], True, True, [kbb[i], nb["qs"]], [bS], h == 7)
            kb.op("dve", lambda e: e.tensor_reduce(out=kmsT[:, :, :], in_=ksum[:, :, :].rearrange("p h (n two) -> p h n two", two=2), axis=AX.X, op=ALU.add),
                  [nb["ksum"]], [nb["kmsT"]])
            kb.op("act", lambda e: e.mul(out=kmsb[:, :, :], in_=kmsT[:, :, :], mul=1.0 / 256.0), [nb["kmsT"]], [nb["kmsT"]])
            kb.op("dve", lambda e: e.tensor_copy(out=qrb[:, :, :].rearrange("p (h q) m -> p h q m", h=8),
                                                 in_=qsamp[:, :, s_ * 4:s_ * 4 + 4].unsqueeze(3).to_broadcast([128, 8, 4, 128])), [nb["qs"]], [nb["qrep"]])
            pG, bG = bank()
            for j in range(32):
                mm(pG[:, j * 8:(j + 1) * 8], qrb[:, j, :], kmsb[:, j // 4, :], True, True, [nb["qrep"], nb["kmsT"]], [bG], j == 31)
            g3 = selS[:, 0:256].rearrange("p (j n) -> p j n", n=8)
            sA = t2[:, 0:256].rearrange("p (j n) -> p j n", n=8)
            sBv = t2[:, 256:512].rearrange("p (j n) -> p j n", n=8)
            eq = t3[:, 0:256].rearrange("p (j n) -> p j n", n=8)
            mx = t3[:, 256:288]

            def mxb():
                return mx.unsqueeze(2).to_broadcast([128, 32, 8])
            kb.op("act", lambda e: e.copy(out=sA, in_=pG[:, 0:256].rearrange("p (j n) -> p j n", n=8)), [bG], [B["t2"]])
            cur = sA
            for it in range(2):
                kb.op("dve", lambda e: e.tensor_reduce(out=mx, in_=cur, axis=AX.X, op=ALU.max), [B["t2"]], [B["t3"]])
                kb.op("dve", lambda e: e.tensor_tensor(out=eq, in0=cur, in1=mxb(), op=ALU.is_equal), [B["t2"]], [B["t3"]])
                kb.op("dve", lambda e: e.scalar_tensor_tensor(out=sBv, in0=eq, scalar=-1e30, in1=cur, op0=ALU.mult, op1=ALU.add), [B["t3"]], [B["t2"]])
                cur = sBv
            kb.op("dve", lambda e: e.tensor_reduce(out=mx, in_=sBv, axis=AX.X, op=ALU.max), [B["t2"]], [B["t3"]])
            kb.op("dve", lambda e: e.tensor_tensor(out=g3, in0=sA, in1=mxb(), op=ALU.is_ge), [B["t2"], B["t3"]], [nb["selS"]])
            PT = t1[:, :].bitcast(BF)[:, 0:512].rearrange("p (g j) -> p g j", g=16)
            kb.op("act", lambda e: e.activation(out=t2[:, :], in_=pS[:, :], func=AF.Exp, scale=scm), [bS], [B["t2"]])
            kb.op("dve", lambda e: e.tensor_tensor(out=PT.rearrange("p (n two) j -> p n two j", two=2),
                                                   in0=t2[:, :].rearrange("p (n two j) -> p n two j", two=2, j=32),
                                                   in1=g3.rearrange("p j n -> p n j").unsqueeze(2).to_broadcast([128, 8, 2, 32]), op=ALU.mult),
                  [B["t2"], nb["selS"]], [B["t1"]])
            pWb, bWb = bank()
            for j in range(32):
                mm(pWb[:, j * 4:(j + 1) * 4], qrb[:, j, :], ksamp[:, j // 4, s_ * 4:s_ * 4 + 4], True, True, [nb["qrep"], nb["ksamp"]], [bWb], j == 31)
            ob = t3[:, 0:128]
            kb.op("act", lambda e: e.activation(out=ob, in_=pWb[:, 0:128], func=AF.Exp, scale=scm), [bWb], [B["t3"]])
            kb.op("dve", lambda e: e.tensor_tensor(out=ob, in0=ob, in1=cst[:, C_CM4:C_CM4 + 128], op=ALU.mult), [B["cst"]], [B["t3"]])
            kb.op("dve", lambda e: e.tensor_reduce(out=t3[:, 128:160], in_=ob.rearrange("p (j t) -> p j t", t=4), axis=AX.X, op=ALU.add), [], [B["t3"]])
            kb.op("dve", lambda e: e.tensor_tensor(out=t3[:, 160:288].rearrange("p (h q t) -> p h q t", h=8, q=4),
                                                   in0=ob.rearrange("p (h q t) -> p h q t", h=8, q=4),
                                                   in1=vsT[:, :, s_ * 4:s_ * 4 + 4].unsqueeze(2).to_broadcast([128, 8, 4, 4]), op=ALU.mult),
                  [nb["vsamp"]], [B["t3"]])
            kb.op("dve", lambda e: e.tensor_reduce(out=t3[:, 288:320], in_=t3[:, 160:288].rearrange("p (j t) -> p j t", t=4), axis=AX.X, op=ALU.add), [], [B["t3"]])
            pO_, bO_ = lbank(1)
            for h in range(8):
                for pg in range(16):
                    mm(pO_[:, h * 4:(h + 1) * 4], vall[:, pg, h * 128:(h + 1) * 128], PT[:, pg, h * 4:(h + 1) * 4], pg == 0, pg == 15,
                       [nb["vall"], B["t1"]], [bO_], False)
            for pg in range(16):
                mm(pO_[:, 32:64], ones_bf, PT[:, pg, :], pg == 0, pg == 15, [B["cstb"], B["t1"]], [bO_], pg == 15)
            kb.op("dve", lambda e: e.tensor_tensor(out=t3[:, 128:160], in0=t3[:, 128:160], in1=pO_[:, 32:64], op=ALU.add), [bO_], [B["t3"]])
            kb.op("dve", lambda e: e.tensor_tensor(out=t3[:, 288:320], in0=t3[:, 288:320], in1=pO_[:, 0:32], op=ALU.add), [bO_], [B["t3"]])
            kb.op("dve", lambda e: e.reciprocal(out=t3[:, 128:160], in_=t3[:, 128:160]), [], [B["t3"]])
            kb.op("dve", lambda e: e.tensor_tensor(out=hT[:, :, c0:c0 + 4], in0=t3[:, 288:320].rearrange("p (h q) -> p h q", h=8),
                                                   in1=t3[:, 128:160].rearrange("p (h q) -> p h q", h=8), op=ALU.mult), [], [B["t3"], hB[4]])
        proj_fm(c_w_out[0], KC, KC, h_rhs, add_to_x)

    def xattn(li):
        qv = big[:, 0:KC * NT].rearrange("p (k t) -> p k t", k=KC)
        MT = [(0, MEM)]
        kb.barrier()
        kb.dma("sp", memR[:], memT_d, [], [B["memR"]], d_in)
        rms_stats(memR, B["memR"], 0, MEM)
        for k in range(KC):
            gc = P_NMEM + li * 8 + k
            kb.op("dve", lambda e, k=k, gc=gc: e.scalar_tensor_tensor(out=memN[:, k, :], in0=memR[:, k, :], scalar=par[:, gc:gc + 1],
                                                                   in1=rstd[:, :MEM], op0=ALU.mult, op1=ALU.mult),
                  [B["memR"], B["par"], B["rstd"]], [B["memN"]])

        def mem_rhs(k, ti, t0, tn):
            return memN[:, k, t0:t0 + tn], B["memN"]

        def evac_k(oc, ti, t0, tn, ps, pb):
            kb.op("act", lambda e: e.copy(out=mkT[:, oc, :], in_=ps[:, :MEM]), [pb], [B["mkT"]])
            kb.op("dve", lambda e: e.tensor_copy(out=mo32[:, 0:MEM], in_=ps[:, :MEM]), [pb], [B["t3"]])
            kb.dma("sp", omk_d[li, :, oc, :], mo32[:, 0:MEM], [B["t3"]], [], d_out)

        if 'xa_k' in phases:
            proj_fm(x_w_kv[li], KC, 8, mem_rhs, evac_k, tiles=MT)
        for cg in (range(0, D, 256) if 'xa_v' in phases else []):
            view, wb = wload([(x_w_kv[li][:, D + cg:D + cg + 256], KC, 256)])
            for mt in range(2):
                ps, pb = bank()
                for k in range(KC):
                    mm(ps[:, :256], memN[:, k, mt * 128:(mt + 1) * 128], view[:, k, :], k == 0, k == KC - 1, [wb, B["memN"]], [pb], k == KC - 1)
                kb.op("act", lambda e: e.copy(out=mv[:, mt, cg:cg + 256], in_=ps[:, :256]), [pb], [B["mv"]])
                kb.op("dve", lambda e: e.tensor_copy(out=mo32[:, 0:256], in_=ps[:, :256]), [pb], [B["t3"]])
                kb.dma("sp", omv_d[li, mt, :, cg:cg + 256], mo32[:, 0:256], [B["t3"]], [], d_out)
        kb.barrier()
        rmsnorm(P_NXA + li * 8)

        def evac_q(oc, ti, t0, tn, ps, pb):
            kb.op("act", lambda e: e.copy(out=qv[:, oc, t0:t0 + tn], in_=ps[:, :tn]), [pb], [gB[ti]])

        proj_fm(x_w_q[li], KC, KC, h_rhs, evac_q)
        sc = 1.0 / 16.0
        pS, bS = lbank(0)
        pO, bO = lbank(1)

        def prompt_group(ti, t0, tn, hd):
            for mt in range(2):
                ps, pb = bank()
                for dc in range(2):
                    mm(ps[:, :tn], mkT[:, hd * 2 + dc, mt * 128:(mt + 1) * 128], qv[:, hd * 2 + dc, t0:t0 + tn], dc == 0, dc == 1,
                       [B["mkT"], gB[ti]], [pb], dc == 1)
                kb.op("act", lambda e, mt=mt, ps=ps: e.activation(out=ptb[:, mt, :tn], in_=ps[:, :tn], func=AF.Exp, scale=sc),
                      [pb], [B["ptb"]])
            psd, pbd = bank()
            for mt in range(2):
                mm(psd[:, :tn], ones_bf, ptb[:, mt, :tn], mt == 0, mt == 1, [B["ptb"], B["cstb"]], [pbd], mt == 1)
            kb.op("dve", lambda e: e.reciprocal(out=rstd[:, :tn], in_=psd[:, :tn]), [pbd], [B["rstd"]])
            for dvc in range(2):
                ps, pb = bank()
                for mt in range(2):
                    mm(ps[:, :tn], mv[:, mt, hd * 256 + dvc * 128: hd * 256 + (dvc + 1) * 128], ptb[:, mt, :tn], mt == 0, mt == 1,
                       [B["mv"], B["ptb"]], [pb], mt == 1)
                kb.op("dve", lambda e, ps=ps, dvc=dvc: e.tensor_tensor(out=hT[:, hd * 2 + dvc, t0:t0 + tn], in0=ps[:, :tn],
                                                                     in1=rstd[:, :tn], op=ALU.mult), [pb, B["rstd"]], [hB[ti]])

        def sample_seq(s):
            i = s % 2
            kb.dma("pool", skT[i][:], cmk_d[li, s], [], [B["skT%d" % i]], d_kv[i])
            kb.dma("pool", sv[i][:], cmv_d[li, s].rearrange("a p n -> p a n"), [], [B["sv%d" % i]], d_kvv[i])
            for mt in range(2):
                for hd in range(4):
                    c0 = mt * 256 + s * 16 + hd * 4
                    for dc in range(2):
                        mm(pS[:, c0:c0 + 4], skT[i][:, hd * 2 + dc, mt * 128:(mt + 1) * 128], qv[:, hd * 2 + dc, SEQ + s * 4:SEQ + s * 4 + 4],
                           dc == 0, dc == 1, [B["skT%d" % i], gB[4]], [bS], True if (dc == 1 and hd == 3 and mt == 1) else False)
            kb.op("act", lambda e, s=s: e.activation(
                out=pts[:, :, s * 16:(s + 1) * 16], in_=pS[:, :].rearrange("p (a c) -> p a c", a=2)[:, :, s * 16:(s + 1) * 16],
                func=AF.Exp, scale=sc), [bS], [B["pts"]])
            for hd in range(4):
                for dvc in range(2):
                    c0 = ((s * 4 + hd) * 2 + dvc) * 4
                    for mt in range(2):
                        mm(pO[:, c0:c0 + 4], sv[i][:, mt, hd * 256 + dvc * 128:hd * 256 + (dvc + 1) * 128],
                           pts[:, mt, s * 16 + hd * 4:s * 16 + hd * 4 + 4], mt == 0, mt == 1, [B["sv%d" % i], B["pts"]], [bO],
                           True if (mt == 1 and hd == 3 and dvc == 1) else False)

        for ti, (t0, tn) in enumerate(TILES[:4]):
            for hd in range(4):
                prompt_group(ti, t0, tn, hd)
                sample_seq(ti * 4 + hd)
        pD, bD = bank()
        for mt in range(2):
            mm(pD[:, :256], ones_bf, pts[:, mt, 0:256], mt == 0, mt == 1, [B["pts"], B["cstb"]], [bD], mt == 1)
        kb.op("dve", lambda e: e.reciprocal(out=rstd[:, :256], in_=pD[:, :256]), [bD], [B["rstd"]])
        for dvc in range(2):
            kb.op("dve", lambda e, dvc=dvc: e.tensor_tensor(
                out=hT[:, :, SEQ:NT].rearrange("p (h v) (s q) -> p h v s q", v=2, q=LS)[:, :, dvc, :, :],
                in0=pO[:, :].rearrange("p (s h v q) -> p h v s q", h=4, v=2, q=LS)[:, :, dvc, :, :],
                in1=rstd[:, :256].rearrange("p (s h q) -> p h s q", h=4, q=LS), op=ALU.mult), [bO, B["rstd"]], [hB[4]])
        proj_fm(x_w_o[li], KC, KC, h_rhs, add_to_x)

    def swiglu(li):
        kb.barrier()
        rmsnorm(P_NFFN + li * 8)
        av = big[:, 0:KC * NT].rearrange("p (k t) -> p k t", k=KC)
        for f0, nf in ((0, 8), (8, 8), (16, 6)):
            for fi in range(nf):
                f = f0 + fi
                view, wb = wload([(f_w_up[li][:, f * 128:(f + 1) * 128], KC, 128),
                                  (f_w_up[li][:, DFF + f * 128:DFF + (f + 1) * 128], KC, 128)])
                for ti, (t0, tn) in enumerate(TILES):
                    pg, bg = bank()
                    for k in range(KC):
                        mm(pg[:, :tn], view[:, k, 0:128], hT[:, k, t0:t0 + tn], k == 0, k == KC - 1, [wb, hB[ti]], [bg], k == KC - 1)
                    pu, bu = bank()
                    for k in range(KC):
                        mm(pu[:, :tn], view[:, k, 128:256], hT[:, k, t0:t0 + tn], k == 0, k == KC - 1, [wb, hB[ti]], [bu], k == KC - 1)
                    kb.op("act", lambda e, pg=pg: e.activation(out=t1[:, :tn], in_=pg[:, :tn], func=AF.Silu), [bg], [B["t1"]])
                    kb.op("dve", lambda e, pu=pu, fi=fi: e.tensor_tensor(out=av[:, fi, t0:t0 + tn], in0=pu[:, :tn], in1=t1[:, :tn], op=ALU.mult),
                          [bu, B["t1"]], [gB[ti]])
            proj_fm(f_w_down[li], nf, KC, lambda k, ti, t0, tn: (av[:, k, t0:t0 + tn], gB[ti]), add_to_x, row0=f0 * 128)

    for li in range(depth):
        kind, j = li % 3, li // 3
        if kind == 0 and 'mix' in phases:
            shortconv(li, j)
        if kind == 1 and 'mix' in phases:
            gdn(li)
        if kind == 2 and 'mix' in phases:
            moba(li)
        if 'xa' in phases:
            xattn(li)
        if 'ffn' in phases:
            swiglu(li)

    for ti, (t0, tn) in enumerate(TILES):
        rms_stats(xT, xB[ti], t0, tn)
        for k in range(KC):
            kb.op("dve", lambda e, k=k: e.scalar_tensor_tensor(
                out=t1[:, :tn], in0=xT[:, k, t0:t0 + tn], scalar=par[:, P_NFIN + k:P_NFIN + k + 1],
                in1=rstd[:, :tn], op0=ALU.mult, op1=ALU.mult), [xB[ti], B["rstd"], B["par"]], [B["t1"]])
            kb.dma("sp", yT_d[:, k, t0:t0 + tn], t1[:, :tn], [B["t1"]], [], d_out)
    kb.finish()
    es.close()
    return nc, kb


P_NMIX = 0
P_NMEM = 32
P_NXA = 64
P_NFFN = 96
P_NFIN = 128
P_ACONV = 136
P_BCONV = 184
P_DTB = 280
P_ALOG = 288
P_BNORM = 296
NPAR = 297
G_ID = 0
G_IDREP = 64
G_U = 576
G_NEGU = 640
G_ONES = 704
G_NEGONES = 832
G_MLOW = 960
G_MUP = 1472
NGC = 1984
C_ONES = 0
C_IDB = 128
C_CM = 256
NCSTB = 768
C_PID = 64
C_CM4 = 72
NCST = 200
NPOOL = 2560


def _vec_pk(v):
    return np.ascontiguousarray(v.reshape(-1, 128).T)


_SHARED = {}


def shared_inputs(I):
    if "poolk" not in _SHARED:
        ck = I["cache_c_k"][0]
        _SHARED["poolk"] = np.ascontiguousarray(ck.transpose(0, 3, 2, 1)).reshape(NPOOL * 128, D)
        _SHARED["poolv"] = np.ascontiguousarray(I["cache_c_v"][0]).reshape(NPOOL * 128, D)
    return _SHARED


def make_inputs(core, I):
    b = core
    s0, s1 = core * NS, (core + 1) * NS
    xa = np.concatenate([I["x_prompt"][b], I["x_sample"][s0:s1].reshape(NSAMP, D)], axis=0)
    x0 = np.ascontiguousarray(xa.T.reshape(KC, 128, NT).transpose(1, 0, 2))
    memT = np.ascontiguousarray(I["mem_prompt"][b].T.reshape(KC, 128, MEM).transpose(1, 0, 2))
    par = np.zeros((128, NPAR), np.float32)
    for li in range(4):
        par[:, P_NMIX + li * 8:P_NMIX + li * 8 + 8] = _vec_pk(I["norm_mix"][li])
        par[:, P_NMEM + li * 8:P_NMEM + li * 8 + 8] = _vec_pk(I["norm_mem"][li])
        par[:, P_NXA + li * 8:P_NXA + li * 8 + 8] = _vec_pk(I["norm_xattn"][li])
        par[:, P_NFFN + li * 8:P_NFFN + li * 8 + 8] = _vec_pk(I["norm_ffn"][li])
    par[:, P_NFIN:P_NFIN + 8] = _vec_pk(I["norm_final"])
    for j in range(2):
        for tap in range(3):
            par[:, P_ACONV + (j * 3 + tap) * 8:P_ACONV + (j * 3 + tap) * 8 + 8] = _vec_pk(I["a_w_conv"][j, tap])
    for tap in range(4):
        par[:, P_BCONV + tap * 24:P_BCONV + tap * 24 + 24] = _vec_pk(I["b_w_conv"][0, tap])
    par[:, P_DTB:P_DTB + 8] = I["b_dt_bias"][0][None, :]
    par[:, P_ALOG:P_ALOG + 8] = I["b_a_log"][0][None, :]
    par[:, P_BNORM] = I["b_norm"][0]
    gc = np.zeros((64, NGC), np.float32)
    ii = np.arange(64)
    gc[:, G_ID:G_ID + 64] = np.eye(64)
    gc[:, G_IDREP:G_IDREP + 512] = np.tile(np.eye(64), (1, 8))
    U = (ii[:, None] <= ii[None, :]).astype(np.float32)
    gc[:, G_U:G_U + 64] = U
    gc[:, G_NEGU:G_NEGU + 64] = -U
    gc[:, G_ONES:G_ONES + 128] = 1.0
    gc[:, G_NEGONES:G_NEGONES + 128] = -1.0
    gc[:, G_MLOW:G_MLOW + 512] = np.tile(np.where(ii[:, None] > ii[None, :], 0.0, -1e30), (1, 8))
    gc[:, G_MUP:G_MUP + 512] = np.tile(np.where(ii[None, :] >= ii[:, None], 0.0, -1e30), (1, 8))
    sbc = np.ascontiguousarray(I["state_b_conv"][0, s0:s1].reshape(NS, 3, 24, 128).transpose(2, 3, 0, 1))
    sbr = np.ascontiguousarray(I["state_b_rec"][0, s0:s1])
    import ml_dtypes
    cst = np.zeros((128, NCST), np.float32)
    cstb = np.zeros((128, NCSTB), np.float32)
    cstb[:, C_ONES:C_ONES + 128] = 1.0
    cstb[:, C_IDB:C_IDB + 128] = np.eye(128, dtype=np.float32)
    rr = np.arange(128)[:, None]
    pp = np.arange(256)[None, :]
    for kc in range(2):
        cstb[:, C_CM + kc * 256:C_CM + (kc + 1) * 256] = np.where(kc * 128 + rr <= pp, 0.0, NEG)
    cst[0:64, 0:64] = np.eye(64, dtype=np.float32)
    cst[:, C_PID] = np.arange(128, dtype=np.float32)
    cc_ = np.arange(128)
    cst[:, C_CM4:C_CM4 + 128] = ((cc_ % 4) <= ((cc_ // 4) % 4)).astype(np.float32)[None, :]
    cstb = cstb.astype(ml_dtypes.bfloat16)
    sa = np.ascontiguousarray(I["state_a_conv"][:, s0:s1].reshape(2, NS, 2, KC, 128).transpose(4, 0, 3, 1, 2))
    ck = I["cache_mem_k"][:, s0:s1].reshape(4, NS, MEM, 4, 2, 128)
    cmk = np.ascontiguousarray(ck.transpose(0, 1, 5, 3, 4, 2).reshape(4, NS, 128, 8, MEM))
    cmv = np.ascontiguousarray(I["cache_mem_v"][:, s0:s1].reshape(4, NS, 2, 128, D))
    return {"x0": x0, "memT": memT, "par": par, "cst": cst, "cstb": cstb, "sa": sa, "cmk": cmk, "cmv": cmv, "sbc": sbc, "sbr": sbr, "gcst": gc,
            "b_w_in": I["b_w_in"], "b_w_out": I["b_w_out"],
            "c_w_qkv": I["c_w_qkv"], "c_w_out": I["c_w_out"], "poolk": shared_inputs(I)["poolk"], "poolv": shared_inputs(I)["poolv"],
            "pt": np.ascontiguousarray(I["page_table"][s0:s1].reshape(1, NS * 16).astype(np.int32)),
            "a_w_in": I["a_w_in"], "a_w_out": I["a_w_out"], "x_w_q": I["x_w_q"], "x_w_kv": I["x_w_kv"], "x_w_o": I["x_w_o"],
            "f_w_up": I["f_w_up"], "f_w_down": I["f_w_down"]}


def kernel(**inputs):
    I = {k: np.asarray(v) for k, v in inputs.items()}
    _SHARED.clear()
    nc, kb = build()
    in_maps = [make_inputs(c, I) for c in range(NCORES)]
    res = run_bass_kernel_spmd(nc, in_maps, core_ids=list(range(NCORES)))
    R = res.results
    f32 = np.float32
    y_p = np.zeros((8, SEQ, D), f32); y_s = np.zeros((128, LS, D), f32)
    a_p = np.zeros((2, 8, 2, D), f32); a_s = np.zeros((2, 128, 2, D), f32)
    bc_p = np.zeros((1, 8, 3, 3 * D), f32); bc_s = np.zeros((1, 128, 3, 3 * D), f32)
    br_p = np.zeros((1, 8, 8, 128, 128), f32); br_s = np.zeros((1, 128, 8, 128, 128), f32)
    ck_p = np.zeros((1, 8, SEQ, 8, 128), f32); cv_p = np.zeros((1, 8, SEQ, 8, 128), f32)
    ck_s = np.zeros((1, 128, LS, 8, 128), f32); cv_s = np.zeros((1, 128, LS, 8, 128), f32)
    mk_p = np.zeros((4, 8, MEM, 4, 256), f32); mv_p = np.zeros((4, 8, MEM, 4, 256), f32)
    for c in range(NCORES):
        r = R[c]
        s0, s1 = c * NS, (c + 1) * NS
        y = np.asarray(r["yT"]).transpose(2, 1, 0).reshape(NT, D)
        y_p[c] = y[:SEQ]
        y_s[s0:s1] = y[SEQ:].reshape(NS, LS, D)
        oa = np.asarray(r["oa"])
        for j in range(2):
            a = oa[:, j].transpose(2, 1, 0).reshape(2 + 2 * NS, D)
            a_p[j, c] = a[:2]
            a_s[j, s0:s1] = a[2:].reshape(NS, 2, D)
        bc_p[0, c] = np.asarray(r["obc_p"]).transpose(2, 1, 0).reshape(3, 3 * D)
        bc_s[0, s0:s1] = np.asarray(r["obc_s"]).transpose(2, 3, 0, 1).reshape(NS, 3, 3 * D)
        br_p[0, c] = np.asarray(r["obr_p"])
        br_s[0, s0:s1] = np.asarray(r["obr_s"])
        k = np.asarray(r["ock"]).transpose(2, 1, 0)
        v = np.asarray(r["ocv"]).reshape(NT, 8, 128)
        ck_p[0, c] = k[:SEQ]; cv_p[0, c] = v[:SEQ]
        ck_s[0, s0:s1] = k[SEQ:].reshape(NS, LS, 8, 128); cv_s[0, s0:s1] = v[SEQ:].reshape(NS, LS, 8, 128)
        omk = np.asarray(r["omk"]); omv = np.asarray(r["omv"])
        for li in range(4):
            mk_p[li, c] = omk[li].transpose(2, 1, 0).reshape(MEM, 4, 256)
            mv_p[li, c] = omv[li].reshape(MEM, 4, 256)
    return (y_p, y_s, a_p, a_s, bc_p, bc_s, br_p, br_s, ck_p, cv_p, ck_s, cv_s, mk_p, mv_p)
```

```python
from contextlib import ExitStack
import numpy as np
import concourse.bass as bass
import concourse.mybir as mybir
from concourse.bass_utils import run_bass_kernel_spmd

F32 = mybir.dt.float32
BF = mybir.dt.bfloat16
I32 = mybir.dt.int32
AF = mybir.ActivationFunctionType
ALU = mybir.AluOpType
AX = mybir.AxisListType

NCORES = 8
D = 1024
KC = 8
SEQ = 2048
NS = 16
LS = 4
NSAMP = NS * LS
NT = SEQ + NSAMP
TILES = [(0, 512), (512, 512), (1024, 512), (1536, 512), (2048, 64)]
DFF = 2816
FC = 22
MEM = 256
EPS = 1e-6
NEG = -30000.0


class Buf:
    __slots__ = ("name", "w", "r", "excl")

    def __init__(self, name, excl=False):
        self.name = name
        self.w = None
        self.r = {}
        self.excl = excl


class DSem:
    __slots__ = ("sem", "cnt")

    def __init__(self, sem):
        self.sem = sem
        self.cnt = 0


class KB:
    def __init__(self, nc, es):
        self.nc = nc
        self.es = es
        self.E = {"pe": nc.tensor, "act": nc.scalar, "dve": nc.vector, "pool": nc.gpsimd, "sp": nc.sync}
        self.sem = {}
        self.cnt = {}
        self.seen = {}
        for q in self.E:
            self.seen[q] = {}
        for e in ("pe", "act", "dve", "pool"):
            self.sem[e] = es.enter_context(nc.semaphore("s_" + e))
            self.cnt[e] = 0
        self.dsems = []
        self.n_ins = 0

    def dsem(self, name):
        d = DSem(self.es.enter_context(self.nc.semaphore("d_" + name)))
        self.dsems.append(d)
        return d

    def sb(self, name, shape, dt):
        return self.es.enter_context(self.nc.sbuf_tensor("sb_" + name, list(shape), dt))

    def _sync(self, q, reads, writes):
        need = {}
        for b in reads:
            if b.w is not None:
                k, v = b.w
                if need.get(k, 0) < v:
                    need[k] = v
        for b in writes:
            if b.w is not None:
                k, v = b.w
                if need.get(k, 0) < v:
                    need[k] = v
            for k, v in b.r.items():
                if need.get(k, 0) < v:
                    need[k] = v
        seen = self.seen[q]
        for k, v in need.items():
            if isinstance(k, DSem):
                v = k.cnt
                s = k.sem
            else:
                if k == q and q == "pe":
                    continue
                s = self.sem[k]
            if seen.get(k, 0) >= v:
                continue
            self.E[q].wait_ge(s, v)
            seen[k] = v
            self.n_ins += 1

    def op(self, eng, fn, reads=(), writes=(), inc=True):
        if eng != "pe":
            ex = [b for b in reads if b.excl]
            if ex:
                reads = [b for b in reads if not b.excl]
                writes = list(writes) + ex
        self._sync(eng, reads, writes)
        ins = fn(self.E[eng])
        self.n_ins += 1
        if inc:
            self.cnt[eng] += 1
            ins.then_inc(self.sem[eng], 1)
            v = self.cnt[eng]
        else:
            assert eng == "pe"
            v = self.cnt[eng] + 1
        for b in reads:
            b.r[eng] = v
        for b in writes:
            b.w = (eng, v)
            b.r = {}
        return ins

    def dma(self, q, out, in_, reads, writes, ds, **kw):
        self._sync(q, reads, writes)
        ins = self.E[q].dma_start(out=out, in_=in_, **kw)
        self.n_ins += 1
        ds.cnt += 16
        ins.then_inc(ds.sem, 16)
        for b in reads:
            b.r[ds] = ds.cnt
        for b in writes:
            b.w = (ds, ds.cnt)
            b.r = {}
        return ins

    def idma(self, out, in_, idx_ap, reads, writes, ds):
        self._sync("pool", reads, writes)
        ins = self.nc.gpsimd.indirect_dma_start(out=out, out_offset=None, in_=in_,
                                                in_offset=bass.IndirectOffsetOnAxis(ap=idx_ap, axis=0))
        self.n_ins += 1
        ds.cnt += 16
        ins.then_inc(ds.sem, 16)
        for b in reads:
            b.r[ds] = ds.cnt
        for b in writes:
            b.w = (ds, ds.cnt)
            b.r = {}
        return ins

    def barrier(self):
        for q in self.E:
            seen = self.seen[q]
            for e in ("pe", "act", "dve", "pool"):
                if e == q and q == "pe":
                    continue
                v = self.cnt[e]
                if v and seen.get(e, 0) < v:
                    self.E[q].wait_ge(self.sem[e], v)
                    seen[e] = v
                    self.n_ins += 1
            for d in self.dsems:
                if d.cnt and seen.get(d, 0) < d.cnt:
                    self.E[q].wait_ge(d.sem, d.cnt)
                    seen[d] = d.cnt
                    self.n_ins += 1

    def finish(self):
        for d in self.dsems:
            if d.cnt:
                self.nc.sync.wait_ge(d.sem, d.cnt)
        for e in ("pe", "act", "dve", "pool"):
            if self.cnt[e]:
                self.nc.sync.wait_ge(self.sem[e], self.cnt[e])


def build(depth=4, dbg=False, phases=('mix', 'xa', 'xa_k', 'xa_v', 'xa_p', 'xa_s', 'ffn')):
    nc = bass.Bass("TRN2", target_bir_lowering=False)
    es = ExitStack()
    kb = KB(nc, es)

    def din(name, shape, dt=F32):
        return nc.dram_tensor(name, list(shape), dt, kind="ExternalInput").ap()

    def dout(name, shape, dt=F32):
        return nc.dram_tensor(name, list(shape), dt, kind="ExternalOutput").ap()

    x0_d = din("x0", [128, KC, NT])
    memT_d = din("memT", [128, KC, MEM])
    par_d = din("par", [128, NPAR])
    cst_d = din("cst", [128, NCST])
    cstb_d = din("cstb", [128, NCSTB], BF)
    sa_d = din("sa", [128, 2, KC, NS, 2])
    cmk_d = din("cmk", [4, NS, 128, 8, MEM])
    cmv_d = din("cmv", [4, NS, 2, 128, D])
    a_w_in = din("a_w_in", [2, D, 3 * D])
    a_w_out = din("a_w_out", [2, D, D])
    x_w_q = din("x_w_q", [4, D, D])
    x_w_kv = din("x_w_kv", [4, D, 2 * D])
    x_w_o = din("x_w_o", [4, D, D])
    f_w_up = din("f_w_up", [4, D, 2 * DFF])
    f_w_down = din("f_w_down", [4, DFF, D])

    b_w_in = din("b_w_in", [1, D, 4112])
    b_w_out = din("b_w_out", [1, D, D])
    sbc_d = din("sbc", [24, 128, NS, 3])
    sbr_d = din("sbr", [NS, 8, 128, 128])
    gcst_d = din("gcst", [64, NGC])
    obc_p_d = dout("obc_p", [128, 24, 3])
    obc_s_d = dout("obc_s", [24, 128, NS, 3])
    obr_p_d = dout("obr_p", [8, 128, 128])
    obr_s_d = dout("obr_s", [NS, 8, 128, 128])
    c_w_qkv = din("c_w_qkv", [1, D, 3 * D])
    c_w_out = din("c_w_out", [1, D, D])
    pt_d = din("pt", [1, NS * 16], I32)
    poolk_d = din("poolk", [NPOOL * 128, D])
    poolv_d = din("poolv", [NPOOL * 128, D])
    ock_d = dout("ock", [128, 8, NT])
    ocv_d = dout("ocv", [NT, D])
    yT_d = dout("yT", [128, KC, NT])
    oa_d = dout("oa", [128, 2, KC, 2 + NS * 2])
    omk_d = dout("omk", [4, 128, KC, MEM])
    omv_d = dout("omv", [4, 2, 128, D])

    xT = kb.sb("xT", [128, KC, NT], F32)
    hT = kb.sb("hT", [128, KC, NT], BF)
    ARN = 16896
    Bg = kb.sb("Bg", [128, ARN], BF)
    Cg = kb.sb("Cg", [128, ARN], BF)

    def carve(arena, off, n, dt=BF):
        assert off % 4 == 0
        if dt == BF:
            assert off // 2 + n <= ARN
            return arena[:, off // 2: off // 2 + n]
        assert off // 2 + 2 * n <= ARN
        return arena[:, off // 2: off // 2 + 2 * n].bitcast(dt)

    par = kb.sb("par", [128, NPAR], F32)
    cst = kb.sb("cst", [128, NCST], F32)
    cstb = kb.sb("cstb", [128, NCSTB], BF)
    NW = 3
    wslots = [kb.sb("w%d" % i, [128, 2048], BF) for i in range(NW)]
    wbufs = [Buf("w%d" % i) for i in range(NW)]
    wds = [kb.dsem("w%d" % i) for i in range(NW)]
    wstate = {"i": 0}
    sq = kb.sb("sq", [128, KC, 512], BF)
    rstd = kb.sb("rstd", [128, 512], F32)
    t1 = kb.sb("t1", [128, 512], F32)
    t2 = kb.sb("t2", [128, 512], F32)
    t3 = kb.sb("t3", [128, 512], F32)
    ptb = kb.sb("ptb", [128, 2, 512], BF)
    pts = kb.sb("pts", [128, 2, 256], BF)
    big = Bg
    km = kb.sb("km", [128, 8, 8], F32)
    kmb = kb.sb("kmb", [128, 8, 8], BF)
    sel = kb.sb("sel", [128, 200], F32)
    selb = kb.sb("selb", [128, 64], BF)
    negT = kb.sb("negT", [64, 256], BF)
    tsel = kb.sb("tsel", [64, 256], BF)
    tsel2 = kb.sb("tsel2", [64, 256], BF)
    vsT = kb.sb("vsT", [128, 8, NSAMP], BF)
    qsamp = kb.sb("qsamp", [128, 8, NSAMP], BF)
    zbuf = carve(Cg, 0, 2 + SEQ, F32)
    zs = carve(Cg, 8200, NS * 6, F32).rearrange("p (s t) -> p s t", t=6)
    oa_sb = carve(Cg, 8584, 2 * KC * (2 + NS * 2), F32).rearrange("p (j c r) -> p j c r", j=2, c=KC)
    sa_sb = carve(Cg, 10760, 2 * KC * NS * 2, F32).rearrange("p (j c s r) -> p j c s r", j=2, c=KC, s=NS)
    mkT = carve(Cg, 0, 8 * MEM).rearrange("p (k m) -> p k m", k=8)
    mv = carve(Cg, 4096, 2 * D).rearrange("p (a n) -> p a n", a=2)
    skT = [carve(Cg, 8192 + i * 8192, 8 * MEM).rearrange("p (k m) -> p k m", k=8) for i in range(2)]
    sv = [carve(Cg, 12288 + i * 8192, 2 * D).rearrange("p (a n) -> p a n", a=2) for i in range(2)]
    memR = carve(Bg, 0, KC * MEM, F32).rearrange("p (k m) -> p k m", k=KC)
    memN = carve(Bg, 8192, KC * MEM).rearrange("p (k m) -> p k m", k=KC)
    mo32 = t3

    B = {n: Buf(n) for n in ("xT", "hT", "big", "par", "cst", "cstb", "sq", "rstd", "t1", "t2", "t3", "ptb", "pts", "memN",
                             "memR", "mkT", "mv", "mo32", "zbuf", "zs", "oa_sb", "sa_sb", "skT0", "skT1", "sv0", "sv1")}
    xB = [Buf("x%d" % i) for i in range(len(TILES))]
    hB = [Buf("h%d" % i) for i in range(len(TILES))]
    gB = [Buf("g%d" % i) for i in range(len(TILES))]

    psum = [es.enter_context(nc.psum_tensor("ps%d" % i, [128, 512], F32)) for i in range(8)]
    psB = [Buf("ps%d" % i, excl=True) for i in range(8)]
    pstate = {"i": 0, "n": 6}

    def bank():
        i = pstate["i"]
        pstate["i"] = (i + 1) % pstate["n"]
        return psum[i], psB[i]

    def lbank(i):
        return psum[pstate["n"] + i], psB[pstate["n"] + i]

    d_in = kb.dsem("in")
    d_out = kb.dsem("out")
    d_kv = [kb.dsem("kv0"), kb.dsem("kv1")]
    d_kvv = [kb.dsem("kvv0"), kb.dsem("kvv1")]

    for ti, (t0, tn) in enumerate(TILES):
        kb.dma("sp", xT[:, :, t0:t0 + tn], x0_d[:, :, t0:t0 + tn], [], [xB[ti]], d_in)
    kb.dma("sp", par[:], par_d, [], [B["par"]], d_in)
    kb.dma("sp", cst[:], cst_d, [], [B["cst"]], d_in)
    kb.dma("sp", cstb[:], cstb_d, [], [B["cstb"]], d_in)

    ones_bf = cstb[:, C_ONES:C_ONES + 128]

    def mm(out, lhsT, rhs, start, stop, reads, writes, inc):
        return kb.op("pe", lambda e: e.matmul(out, lhsT=lhsT, rhs=rhs, start=start, stop=stop), reads, writes, inc)

    def wload(parts):
        i = wstate["i"]
        wstate["i"] = (i + 1) % NW
        nk = parts[0][1]
        tot = sum(p[2] for p in parts)
        assert nk * tot <= 2048
        view = wslots[i][:, 0:nk * tot].rearrange("p (k n) -> p k n", k=nk)
        c0 = 0
        for ap, nk_, ncols in parts:
            kb.dma("pool", view[:, :, c0:c0 + ncols], ap.rearrange("(k p) n -> p k n", p=128), [], [wbufs[i]], wds[i])
            c0 += ncols
        return view, wbufs[i]

    def rms_stats(src, srcB, t0, tn, nfeat_chunks=KC):
        kb.op("act", lambda e: e.activation(out=sq[:, :, :tn], in_=src[:, :, t0:t0 + tn], func=AF.Square),
              [srcB], [B["sq"]])
        ps, pb = bank()
        for k in range(KC):
            mm(ps[:, :tn], ones_bf, sq[:, k, :tn], k == 0, k == KC - 1, [B["sq"], B["cstb"]], [pb], k == KC - 1)
        kb.op("act", lambda e: e.activation(out=rstd[:, :tn], in_=ps[:, :tn], func=AF.Ln, bias=EPS, scale=1.0 / D),
              [pb], [B["rstd"]])
        kb.op("act", lambda e: e.activation(out=rstd[:, :tn], in_=rstd[:, :tn], func=AF.Exp, scale=-0.5), [], [B["rstd"]])

    def rmsnorm(gcol, dst=None, dstB=None, dst_dt_is_f32=False):
        dst = hT if dst is None else dst
        for ti, (t0, tn) in enumerate(TILES):
            rms_stats(xT, xB[ti], t0, tn)
            db = hB[ti] if dstB is None else dstB[ti]
            for k in range(KC):
                kb.op("dve", lambda e, k=k: e.scalar_tensor_tensor(
                    out=dst[:, k, t0:t0 + tn], in0=xT[:, k, t0:t0 + tn], scalar=par[:, gcol + k:gcol + k + 1],
                    in1=rstd[:, :tn], op0=ALU.mult, op1=ALU.mult), [xB[ti], B["rstd"], B["par"]], [db])

    def proj_fm(W, nk, n_oc, rhs_fn, evac, tiles=TILES, col0=0, row0=0):
        for og in range(0, n_oc, 2):
            ng = min(2, n_oc - og)
            view, wb = wload([(W[row0:row0 + nk * 128, col0 + og * 128: col0 + (og + ng) * 128], nk, ng * 128)])
            for j in range(ng):
                oc = og + j
                for ti, (t0, tn) in enumerate(tiles):
                    ps, pb = bank()
                    for k in range(nk):
                        rap, rb = rhs_fn(k, ti, t0, tn)
                        mm(ps[:, :tn], view[:, k, j * 128:(j + 1) * 128], rap, k == 0, k == nk - 1, [wb, rb], [pb], k == nk - 1)
                    evac(oc, ti, t0, tn, ps, pb)

    def h_rhs(k, ti, t0, tn):
        return hT[:, k, t0:t0 + tn], hB[ti]

    def add_to_x(oc, ti, t0, tn, ps, pb):
        kb.op("dve", lambda e: e.tensor_tensor(out=xT[:, oc, t0:t0 + tn], in0=xT[:, oc, t0:t0 + tn], in1=ps[:, :tn], op=ALU.add),
              [pb], [xB[ti]])

    def shortconv(li, j):
        yv = big[:, 0:KC * NT].rearrange("p (k t) -> p k t", k=KC)
        kb.barrier()
        kb.dma("sp", sa_sb[:], sa_d, [], [B["sa_sb"]], d_in)
        rmsnorm(P_NMIX + li * 8)
        kb.op("dve", lambda e: e.memset(zbuf[:, 0:2], 0.0), [], [B["zbuf"]])
        for c in range(KC):
            W = a_w_in[j]
            wv = [wload([(W[:, g * D + c * 128: g * D + (c + 1) * 128], KC, 128)]) for g in range(3)]
            wc = P_ACONV + (j * 3) * 8 + c
            kb.op("act", lambda e: e.copy(out=zs[:, :, 0:2], in_=sa_sb[:, j, c, :, :]), [B["sa_sb"]], [B["zs"]])
            for ti, (t0, tn) in enumerate(TILES):
                pss = []
                for g in range(3):
                    ps, pb = bank()
                    for k in range(KC):
                        mm(ps[:, :tn], wv[g][0][:, k, :], hT[:, k, t0:t0 + tn], k == 0, k == KC - 1,
                           [wv[g][1], hB[ti]], [pb], k == KC - 1)
                    pss.append((ps, pb))
                (pgb, bgb), (pgc, bgc), (pu, bu) = pss
                kb.op("act", lambda e: e.copy(out=t1[:, :tn], in_=pu[:, :tn]), [bu], [B["t1"]])
                kb.op("act", lambda e: e.copy(out=t2[:, :tn], in_=pgb[:, :tn]), [bgb], [B["t2"]])
                samp = t0 >= SEQ
                if not samp:
                    zw = zbuf[:, 2 + t0:2 + t0 + tn]
                    kb.op("dve", lambda e: e.tensor_tensor(out=zw, in0=pgc[:, :tn], in1=t1[:, :tn], op=ALU.mult),
                          [bgc, B["t1"]], [B["zbuf"]])
                    zsh = [zbuf[:, t0 + d:t0 + d + tn] for d in range(3)]
                    accv = t3[:, :tn]
                    t2v = t2[:, :tn]
                    yo = yv[:, c, t0:t0 + tn]
                    zB = B["zbuf"]
                else:
                    zw = zs[:, :, 2:6]
                    kb.op("dve", lambda e: e.tensor_tensor(out=zw, in0=pgc[:, :tn].rearrange("p (s t) -> p s t", t=LS),
                                                           in1=t1[:, :tn].rearrange("p (s t) -> p s t", t=LS), op=ALU.mult),
                          [bgc, B["t1"]], [B["zs"]])
                    zsh = [zs[:, :, d:d + 4] for d in range(3)]
                    accv = t3[:, :tn].rearrange("p (s t) -> p s t", t=LS)
                    t2v = t2[:, :tn].rearrange("p (s t) -> p s t", t=LS)
                    yo = yv[:, c, t0:t0 + tn].rearrange("p (s t) -> p s t", t=LS)
                    zB = B["zs"]
                kb.op("dve", lambda e: e.tensor_scalar(out=accv, in0=zsh[0], scalar1=par[:, wc:wc + 1], scalar2=None, op0=ALU.mult),
                      [zB, B["par"]], [B["t3"]])
                for d in (1, 2):
                    kb.op("dve", lambda e, d=d: e.scalar_tensor_tensor(out=accv, in0=zsh[d], scalar=par[:, wc + 8 * d:wc + 8 * d + 1],
                                                                     in1=accv, op0=ALU.mult, op1=ALU.add),
                          [zB, B["par"]], [B["t3"]])
                kb.op("dve", lambda e: e.tensor_tensor(out=yo, in0=accv, in1=t2v, op=ALU.mult), [B["t2"]], [B["t3"], gB[ti]])
            kb.op("act", lambda e: e.copy(out=oa_sb[:, j, c, 0:2], in_=zbuf[:, SEQ:SEQ + 2]), [B["zbuf"]], [B["oa_sb"]])
            kb.op("act", lambda e: e.copy(out=oa_sb[:, j, c, 2:2 + 2 * NS].rearrange("p (s r) -> p s r", r=2), in_=zs[:, :, 4:6]),
                  [B["zs"]], [B["oa_sb"]])
        kb.dma("sp", oa_d[:, j], oa_sb[:, j], [B["oa_sb"]], [], d_out)
        proj_fm(a_w_out[j], KC, KC, lambda k, ti, t0, tn: (yv[:, k, t0:t0 + tn], gB[ti]), add_to_x)


    def gdn(li):
        kb.barrier()
        W = b_w_in[0]
        BT = 256
        ident_bf = cstb[:, C_IDB:C_IDB + 128]

        def v3(ap, h=8):
            return ap.rearrange("p (h t) -> p h t", h=h)
        qT = v3(carve(Bg, 0, 8 * BT)); kT = v3(carve(Bg, 4096, 8 * BT)); vT = v3(carve(Bg, 8192, 8 * BT)); zT = v3(carve(Bg, 12288, 8 * BT))
        oraw = v3(carve(Bg, 16384, 8 * BT))
        Tb = carve(Bg, 20480, 512); bgk = carve(Bg, 21504, 1024); kdec = carve(Bg, 23552, 1024); bv = carve(Bg, 25600, 1024)
        nkdT = carve(Bg, 27648, 512); w_sb = carve(Bg, 28672, 1024); aqkT = carve(Bg, 30720, 512); qgT = carve(Bg, 31744, 512)
        ktm = ptb[:, :, :].rearrange("p a n -> p (a n)")
        egr = rstd
        raw = carve(Cg, 0, 260, F32); carry = carve(Cg, 1040, 72, F32).rearrange("p (c r) -> p c r", r=3)
        S = carve(Cg, 1328, 1024, F32); Sb = carve(Cg, 5424, 1024)
        negGU = carve(Cg, 7472, 512, F32); decLs = carve(Cg, 9520, 512, F32); decU = carve(Cg, 11568, 512, F32)
        LN = [carve(Cg, 13616 + i * 2048, 512) for i in range(4)]
        Q = carve(Cg, 21808, 512)
        gc = carve(Cg, 23856, NGC, F32)
        sm = carve(Cg, 23856 + 4 * NGC, 160, F32)
        nb = {n: Buf("g_" + n) for n in ("q", "k", "v", "z", "oraw", "Tb", "bgk", "kdec", "bv", "nkdT", "w", "aqkT", "qgT", "ktm",
                                         "raw", "carry", "S", "Sb", "negGU", "decLs", "decU", "L0", "L1", "L2", "L3", "Q", "gc",
                                         "ba", "beta", "g", "gam", "egam", "ekd", "bg", "egl", "negA", "xa")}
        LNB = [nb["L0"], nb["L1"], nb["L2"], nb["L3"]]
        kb.dma("sp", gc[0:64, :], gcst_d, [], [nb["gc"]], d_in)
        IDf = gc[:, G_ID:G_ID + 64]; IDrep = gc[:, G_IDREP:G_IDREP + 512]; Ucs = gc[:, G_U:G_U + 64]; negU = gc[:, G_NEGU:G_NEGU + 64]
        ones64 = gc[:, G_ONES:G_ONES + 128]; negones64 = gc[:, G_NEGONES:G_NEGONES + 128]
        Mlow = gc[:, G_MLOW:G_MLOW + 512]; Mup = gc[:, G_MUP:G_MUP + 512]
        ba_sb = sm[:, 0:16]; beta = sm[:, 16:24]; gg = sm[:, 24:32]; gam = sm[:, 32:40]; egam = sm[:, 40:48]; ekd = sm[:, 48:56]
        bgs = sm[:, 56:64]; egl = sm[:, 64:72]; negA = sm[:, 72:80]; xa = sm[:, 80:88]
        dtb = par[:, P_DTB:P_DTB + 8]
        kb.op("act", lambda e: e.activation(out=negA[:, :], in_=par[:, P_ALOG:P_ALOG + 8], func=AF.Exp), [B["par"]], [nb["negA"]])
        kb.op("dve", lambda e: e.tensor_scalar(out=negA[:, :], in0=negA[:, :], scalar1=-1.0, scalar2=None, op0=ALU.mult), [], [nb["negA"]])
        kb.op("dve", lambda e: e.memset(carry[:, :, :], 0.0), [], [nb["carry"]])
        kb.op("dve", lambda e: e.memset(S[:, :], 0.0), [], [nb["S"]])
        kb.op("dve", lambda e: e.memset(Sb[:, :], 0.0), [], [nb["Sb"]])

        def r3(ap, c):
            return ap

        def rep(cap, c):
            return cap.rearrange("p (h j) -> p h j", h=8)[0:c, :, 0:c]

        def fl(ap, c):
            return ap[0:c, 0:8 * c].rearrange("p (h j) -> p h j", h=8)

        def bc8(v, c, n):
            return v[0:c, :].unsqueeze(2).to_broadcast([c, 8, n])

        def chunk(c, col0, ba_ps, ba_b, hcols_unused=None):
            nlev = {64: 5, 4: 1}[c]
            kb.op("act", lambda e: e.activation(out=beta[0:c, :], in_=ba_ps[:, 0:8], func=AF.Sigmoid), [ba_b], [nb["beta"]])
            kb.op("dve", lambda e: e.tensor_tensor(out=xa[0:c, :], in0=ba_ps[:, 8:16], in1=dtb[0:c, :], op=ALU.add), [ba_b, B["par"]], [nb["xa"]])
            kb.op("act", lambda e: e.activation(out=xa[0:c, :], in_=xa[0:c, :], func=AF.Exp), [], [nb["xa"]])
            kb.op("act", lambda e: e.activation(out=xa[0:c, :], in_=xa[0:c, :], func=AF.Ln, bias=1.0, scale=1.0), [], [nb["xa"]])
            kb.op("dve", lambda e: e.tensor_tensor(out=gg[0:c, :], in0=xa[0:c, :], in1=negA[0:c, :], op=ALU.mult), [nb["xa"], nb["negA"]], [nb["g"]])
            pA, bA = bank()
            mm(pA[0:c, 0:8], Ucs[0:c, 0:c], gg[0:c, :], True, True, [nb["gc"], nb["g"]], [bA], False)
            mm(pA[0:c, 8:16], ones64[0:c, 0:c], gg[0:c, :], True, True, [nb["gc"], nb["g"]], [bA], False)
            mm(pA[:, 16:24], ones64[0:c, 0:128], gg[0:c, :], True, True, [nb["gc"], nb["g"]], [bA], True)
            kb.op("act", lambda e: e.copy(out=gam[0:c, :], in_=pA[0:c, 0:8]), [bA], [nb["gam"]])
            kb.op("act", lambda e: e.activation(out=egam[0:c, :], in_=pA[0:c, 0:8], func=AF.Exp), [bA], [nb["egam"]])
            kb.op("dve", lambda e: e.tensor_tensor(out=ekd[0:c, :], in0=pA[0:c, 8:16], in1=gam[0:c, :], op=ALU.subtract), [bA, nb["gam"]], [nb["ekd"]])
            kb.op("act", lambda e: e.activation(out=ekd[0:c, :], in_=ekd[0:c, :], func=AF.Exp), [], [nb["ekd"]])
            kb.op("dve", lambda e: e.tensor_tensor(out=bgs[0:c, :], in0=beta[0:c, :], in1=egam[0:c, :], op=ALU.mult), [nb["beta"], nb["egam"]], [nb["bg"]])
            kb.op("act", lambda e: e.activation(out=egl[:, :], in_=pA[:, 16:24], func=AF.Exp), [bA], [nb["egl"]])
            kb.op("dve", lambda e: e.tensor_tensor(out=fl(negGU, c), in0=bc8(gg, c, c),
                                                   in1=negU[0:c, 0:c].unsqueeze(1).to_broadcast([c, 8, c]), op=ALU.mult),
                  [nb["g"], nb["gc"]], [nb["negGU"]])
            pD, bD = bank()
            for h in range(8):
                mm(pD[0:c, h * c:(h + 1) * c], ones64[0:c, 0:c], negGU[0:c, h * c:(h + 1) * c], True, False, [nb["gc"], nb["negGU"]], [bD], False)
                mm(pD[0:c, h * c:(h + 1) * c], negGU[0:c, h * c:(h + 1) * c], negones64[0:c, 0:c], False, True, [nb["gc"], nb["negGU"]], [bD], h == 7)
            kb.op("dve", lambda e: e.tensor_tensor(out=fl(decLs, c), in0=fl(pD, c), in1=rep(Mlow, c), op=ALU.add), [bD, nb["gc"]], [nb["decLs"]])
            kb.op("act", lambda e: e.activation(out=decLs[0:c, 0:8 * c], in_=decLs[0:c, 0:8 * c], func=AF.Exp), [], [nb["decLs"]])
            kb.op("dve", lambda e: e.scalar_tensor_tensor(out=fl(decU, c), in0=fl(pD, c), scalar=-1.0, in1=rep(Mup, c), op0=ALU.mult, op1=ALU.add),
                  [bD, nb["gc"]], [nb["decU"]])
            kb.op("act", lambda e: e.activation(out=decU[0:c, 0:8 * c], in_=decU[0:c, 0:8 * c], func=AF.Exp), [], [nb["decU"]])
            pK, bK = bank()
            for h in range(8):
                mm(pK[0:c, h * c:(h + 1) * c], kT[:, h, col0:col0 + c], kT[:, h, col0:col0 + c], True, True, [nb["k"]], [bK], h == 7)
            L0, N0 = LN[0], LN[1]
            kb.op("dve", lambda e: e.tensor_tensor(out=t2[0:c, 0:8 * c], in0=pK[0:c, 0:8 * c], in1=decLs[0:c, 0:8 * c], op=ALU.mult),
                  [bK, nb["decLs"]], [B["t2"]])
            kb.op("dve", lambda e: e.tensor_tensor(out=fl(L0, c), in0=fl(t2, c), in1=bc8(beta, c, c), op=ALU.mult), [nb["beta"], B["t2"]], [LNB[0]])
            pN_, bN = bank()
            pN = pN_[:, :].bitcast(BF)
            for h in range(8):
                kb.op("pe", lambda e, h=h: e.transpose(pN[0:c, h * c:(h + 1) * c], L0[0:c, h * c:(h + 1) * c], ident_bf[0:c, 0:c]),
                      [LNB[0], B["cstb"]], [bN], h == 7)
            kb.op("act", lambda e: e.copy(out=N0[0:c, 0:8 * c], in_=pN[0:c, 0:8 * c]), [bN], [LNB[1]])
            kb.op("dve", lambda e: e.tensor_tensor(out=fl(Q, c), in0=rep(IDrep, c), in1=fl(pN, c), op=ALU.subtract), [bN, nb["gc"]], [nb["Q"]])
            li_, ni_ = 0, 1
            for lv in range(nlev):
                last = lv == nlev - 1
                Lp, Np = LN[li_], LN[ni_]
                free = [i for i in range(4) if i not in (li_, ni_)]
                l2, n2 = free[0], free[1]
                pL, bL = bank()
                for h in range(8):
                    mm(pL[0:c, h * c:(h + 1) * c], Np[0:c, h * c:(h + 1) * c], Lp[0:c, h * c:(h + 1) * c], True, True, [LNB[li_], LNB[ni_]], [bL], h == 7)
                if not last:
                    pN2, bN2 = bank()
                    for h in range(8):
                        mm(pN2[0:c, h * c:(h + 1) * c], Lp[0:c, h * c:(h + 1) * c], Np[0:c, h * c:(h + 1) * c], True, True, [LNB[li_], LNB[ni_]], [bN2], h == 7)
                kb.op("act", lambda e: e.copy(out=LN[l2][0:c, 0:8 * c], in_=pL[0:c, 0:8 * c]), [bL], [LNB[l2]])
                if not last:
                    kb.op("dve", lambda e: e.tensor_copy(out=LN[n2][0:c, 0:8 * c], in_=pN2[0:c, 0:8 * c]), [bN2], [LNB[n2]])
                pQ, bQ = bank()
                for h in range(8):
                    mm(pQ[0:c, h * c:(h + 1) * c], LN[l2][0:c, h * c:(h + 1) * c], Q[0:c, h * c:(h + 1) * c], True, True, [LNB[l2], nb["Q"]], [bQ], h == 7)
                kb.op("dve", lambda e: e.tensor_tensor(out=Q[0:c, 0:8 * c], in0=Q[0:c, 0:8 * c], in1=pQ[0:c, 0:8 * c], op=ALU.add), [bQ], [nb["Q"]])
                li_, ni_ = l2, n2
            Tb = Q
            nb["Tb"] = nb["Q"]
            pT, bT = bank()
            pTb = pT[:, :].bitcast(BF)
            for h in range(8):
                kb.op("pe", lambda e, h=h: e.transpose(pTb[0:c, h * 128:(h + 1) * 128], kT[:, h, col0:col0 + c], ident_bf),
                      [nb["k"], B["cstb"]], [bT], h == 7)
            kb.op("act", lambda e: e.copy(out=ktm[0:c, :], in_=pTb[0:c, :]), [bT], [nb["ktm"]])
            k3 = ktm[0:c, :].rearrange("p (h d) -> p h d", h=8)
            kb.op("dve", lambda e: e.tensor_tensor(out=bgk[0:c, :].rearrange("p (h d) -> p h d", h=8), in0=k3, in1=bc8(bgs, c, 128), op=ALU.mult),
                  [nb["ktm"], nb["bg"]], [nb["bgk"]])
            kb.op("dve", lambda e: e.tensor_tensor(out=kdec[0:c, :].rearrange("p (h d) -> p h d", h=8), in0=k3, in1=bc8(ekd, c, 128), op=ALU.mult),
                  [nb["ktm"], nb["ekd"]], [nb["kdec"]])
            pV, bV = bank()
            pVb = pV[:, :].bitcast(BF)
            for h in range(8):
                kb.op("pe", lambda e, h=h: e.transpose(pVb[0:c, h * 128:(h + 1) * 128], vT[:, h, col0:col0 + c], ident_bf),
                      [nb["v"], B["cstb"]], [bV], h == 7)
            kb.op("dve", lambda e: e.tensor_tensor(out=bv[0:c, :].rearrange("p (h d) -> p h d", h=8),
                                                   in0=pVb[0:c, :].rearrange("p (h d) -> p h d", h=8), in1=bc8(beta, c, 128), op=ALU.mult),
                  [bV, nb["beta"]], [nb["bv"]])
            pKd, bKd = bank()
            for h in range(8):
                mm(pKd[:, h * c:(h + 1) * c], bgk[0:c, h * 128:(h + 1) * 128], Tb[0:c, h * c:(h + 1) * c], True, True, [nb["bgk"], nb["Tb"]], [bKd], h == 7)
            kb.op("act", lambda e: e.mul(out=nkdT[:, 0:8 * c], in_=pKd[:, 0:8 * c], mul=-1.0), [bKd], [nb["nkdT"]])
            pW = [bank(), bank()]
            for h in range(8):
                pw, bw = pW[h // 4]
                hh = h % 4
                mm(pw[0:c, hh * 128:(hh + 1) * 128], Tb[0:c, h * c:(h + 1) * c], bv[0:c, h * 128:(h + 1) * 128], True, False, [nb["Tb"], nb["bv"]], [bw], False)
                mm(pw[0:c, hh * 128:(hh + 1) * 128], nkdT[:, h * c:(h + 1) * c], Sb[:, h * 128:(h + 1) * 128], False, True, [nb["nkdT"], nb["Sb"]], [bw], hh == 3)
            for i2 in range(2):
                kb.op("act", lambda e, i2=i2: e.copy(out=w_sb[0:c, i2 * 512:(i2 + 1) * 512], in_=pW[i2][0][0:c, :]), [pW[i2][1]], [nb["w"]])
            pA2, bA2 = bank()
            for h in range(8):
                mm(pA2[0:c, h * c:(h + 1) * c], kT[:, h, col0:col0 + c], qT[:, h, col0:col0 + c], True, True, [nb["k"], nb["q"]], [bA2], h == 7)
            kb.op("dve", lambda e: e.tensor_tensor(out=aqkT[0:c, 0:8 * c], in0=pA2[0:c, 0:8 * c], in1=decU[0:c, 0:8 * c], op=ALU.mult),
                  [bA2, nb["decU"]], [nb["aqkT"]])
            pE, bE = bank()
            mm(pE[:, 0:8 * c], negones64[0:c, 0:128], negGU[0:c, 0:8 * c], True, True, [nb["gc"], nb["negGU"]], [bE], True)
            kb.op("act", lambda e: e.activation(out=egr[:, 0:8 * c], in_=pE[:, 0:8 * c], func=AF.Exp), [bE], [B["rstd"]])
            kb.op("dve", lambda e: e.tensor_tensor(out=qgT[:, 0:8 * c].rearrange("p (h j) -> p h j", h=8), in0=qT[:, :, col0:col0 + c],
                                                   in1=egr[:, 0:8 * c].rearrange("p (h j) -> p h j", h=8), op=ALU.mult),
                  [nb["q"], B["rstd"]], [nb["qgT"]])
            pO, bO = bank()
            for h in range(8):
                mm(pO[:, h * c:(h + 1) * c], Sb[:, h * 128:(h + 1) * 128], qgT[:, h * c:(h + 1) * c], True, False, [nb["Sb"], nb["qgT"]], [bO], False)
                mm(pO[:, h * c:(h + 1) * c], w_sb[0:c, h * 128:(h + 1) * 128], aqkT[0:c, h * c:(h + 1) * c], False, True, [nb["w"], nb["aqkT"]], [bO], h == 7)
            kb.op("act", lambda e: e.copy(out=oraw[:, :, col0:col0 + c], in_=pO[:, 0:8 * c].rearrange("p (h j) -> p h j", h=8)), [bO], [nb["oraw"]])
            pS2 = [bank(), bank()]
            for h in range(8):
                p2, b2 = pS2[h // 4]
                hh = h % 4
                mm(p2[:, hh * 128:(hh + 1) * 128], kdec[0:c, h * 128:(h + 1) * 128], w_sb[0:c, h * 128:(h + 1) * 128], True, True, [nb["kdec"], nb["w"]], [b2], hh == 3)
            kb.op("dve", lambda e: e.tensor_tensor(out=S[:, :].rearrange("p (h d) -> p h d", h=8), in0=S[:, :].rearrange("p (h d) -> p h d", h=8),
                                                   in1=egl[:, :].unsqueeze(2).to_broadcast([128, 8, 128]), op=ALU.mult), [nb["egl"]], [nb["S"]])
            for i2 in range(2):
                kb.op("dve", lambda e, i2=i2: e.tensor_tensor(out=S[:, i2 * 512:(i2 + 1) * 512], in0=S[:, i2 * 512:(i2 + 1) * 512], in1=pS2[i2][0][:, :], op=ALU.add),
                      [pS2[i2][1]], [nb["S"]])
            kb.op("act", lambda e: e.copy(out=Sb[:, :], in_=S[:, :]), [nb["S"]], [nb["Sb"]])

        def in_proj(ti, t0, tn, samp):
            if samp:
                raws = raw[:, 0:NS * 7].rearrange("p (s t) -> p s t", t=7)
            for cg in range(0, 24, 2):
                view, wb = wload([(W[:, cg * 128:(cg + 2) * 128], KC, 256)])
                for j2 in range(2):
                    cc = cg + j2
                    ps, pb = bank()
                    for k in range(KC):
                        mm(ps[:, :tn], view[:, k, j2 * 128:(j2 + 1) * 128], hT[:, k, t0:t0 + tn], k == 0, k == KC - 1, [wb, hB[ti]], [pb], k == KC - 1)
                    wc = P_BCONV + cc
                    if not samp:
                        kb.op("act", lambda e: e.copy(out=raw[:, 0:3], in_=carry[:, cc, :]), [nb["carry"]], [nb["raw"]])
                        kb.op("act", lambda e: e.copy(out=raw[:, 3:3 + tn], in_=ps[:, :tn]), [pb], [nb["raw"]])
                        kb.op("act", lambda e: e.copy(out=carry[:, cc, :], in_=raw[:, tn:tn + 3]), [nb["raw"]], [nb["carry"]])
                        win = [raw[:, d:d + tn] for d in range(4)]
                        acc = t3[:, :tn]
                    else:
                        kb.dma("sp", raws[:, :, 0:3], sbc_d[cc], [], [nb["raw"]], d_in)
                        kb.op("act", lambda e: e.copy(out=raws[:, :, 3:7], in_=ps[:, :tn].rearrange("p (s t) -> p s t", t=LS)), [pb], [nb["raw"]])
                        kb.dma("sp", obc_s_d[cc], raws[:, :, 4:7], [nb["raw"]], [], d_out)
                        win = [raws[:, :, d:d + 4] for d in range(4)]
                        acc = t3[:, :tn].rearrange("p (s t) -> p s t", t=LS)
                    kb.op("dve", lambda e: e.tensor_scalar(out=acc, in0=win[0], scalar1=par[:, wc:wc + 1], scalar2=None, op0=ALU.mult),
                          [nb["raw"], B["par"]], [B["t3"]])
                    for d in (1, 2, 3):
                        kb.op("dve", lambda e, d=d: e.scalar_tensor_tensor(out=acc, in0=win[d], scalar=par[:, wc + 24 * d:wc + 24 * d + 1], in1=acc,
                                                                         op0=ALU.mult, op1=ALU.add), [nb["raw"], B["par"]], [B["t3"]])
                    h = cc % 8
                    if cc >= 16:
                        kb.op("act", lambda e: e.activation(out=vT[:, h, 0:tn], in_=t3[:, :tn], func=AF.Silu), [B["t3"]], [nb["v"]])
                    else:
                        dst, dB = (qT, nb["q"]) if cc < 8 else (kT, nb["k"])
                        kb.op("act", lambda e: e.activation(out=t2[:, :tn], in_=t3[:, :tn], func=AF.Silu), [B["t3"]], [B["t2"]])
                        kb.op("act", lambda e: e.activation(out=sq[:, 0, :tn], in_=t2[:, :tn], func=AF.Square), [B["t2"]], [B["sq"]])
                        p2, b2 = bank()
                        mm(p2[:, :tn], ones_bf, sq[:, 0, :tn], True, True, [B["sq"], B["cstb"]], [b2], True)
                        kb.op("act", lambda e: e.activation(out=t1[:, :tn], in_=p2[:, :tn], func=AF.Ln, bias=EPS, scale=1.0), [b2], [B["t1"]])
                        kb.op("act", lambda e: e.activation(out=t1[:, :tn], in_=t1[:, :tn], func=AF.Exp, scale=-0.5), [], [B["t1"]])
                        scl = (128.0 ** -0.5) if cc < 8 else 1.0
                        kb.op("dve", lambda e: e.scalar_tensor_tensor(out=dst[:, h, 0:tn], in0=t2[:, :tn], scalar=scl, in1=t1[:, :tn],
                                                                      op0=ALU.mult, op1=ALU.mult), [B["t2"], B["t1"]], [dB])
            for zg in range(0, 8, 2):
                view, wb = wload([(W[:, 3072 + zg * 128:3072 + (zg + 2) * 128], KC, 256)])
                for j2 in range(2):
                    ps, pb = bank()
                    for k in range(KC):
                        mm(ps[:, :tn], view[:, k, j2 * 128:(j2 + 1) * 128], hT[:, k, t0:t0 + tn], k == 0, k == KC - 1, [wb, hB[ti]], [pb], k == KC - 1)
                    kb.op("act", lambda e: e.activation(out=zT[:, zg + j2, 0:tn], in_=ps[:, :tn], func=AF.Silu), [pb], [nb["z"]])

        def out_norm(ti, t0, tn):
            kb.op("act", lambda e: e.activation(out=sq[:, :, :tn], in_=oraw[:, :, 0:tn], func=AF.Square), [nb["oraw"]], [B["sq"]])
            for h in range(8):
                p2, b2 = bank()
                mm(p2[:, :tn], ones_bf, sq[:, h, :tn], True, True, [B["sq"], B["cstb"]], [b2], True)
                kb.op("act", lambda e: e.activation(out=t1[:, :tn], in_=p2[:, :tn], func=AF.Ln, bias=EPS, scale=1.0 / 128.0), [b2], [B["t1"]])
                kb.op("act", lambda e: e.activation(out=t1[:, :tn], in_=t1[:, :tn], func=AF.Exp, scale=-0.5), [], [B["t1"]])
                kb.op("dve", lambda e: e.scalar_tensor_tensor(out=t2[:, :tn], in0=oraw[:, h, 0:tn], scalar=par[:, P_BNORM:P_BNORM + 1], in1=t1[:, :tn],
                                                              op0=ALU.mult, op1=ALU.mult), [nb["oraw"], B["t1"], B["par"]], [B["t2"]])
                kb.op("dve", lambda e: e.tensor_tensor(out=hT[:, h, t0:t0 + tn], in0=t2[:, :tn], in1=zT[:, h, 0:tn], op=ALU.mult),
                      [B["t2"], nb["z"]], [hB[ti]])

        rmsnorm(P_NMIX + li * 8)
        for blk in range(SEQ // BT):
            t0 = blk * BT
            ti = t0 // 512
            in_proj(ti, t0, BT, False)
            wba, wbb = wload([(W[:, 4096:4112], KC, 16)])
            pBA, bBA = bank()
            for ci in range(BT // 64):
                for k in range(KC):
                    mm(pBA[0:64, ci * 16:(ci + 1) * 16], hT[:, k, t0 + ci * 64:t0 + (ci + 1) * 64], wba[:, k, :], k == 0, k == KC - 1,
                       [wbb, hB[ti]], [bBA], k == KC - 1 and ci == BT // 64 - 1)
            kb.op("act", lambda e: e.copy(out=t1[0:64, 0:64], in_=pBA[0:64, 0:64]), [bBA], [B["t1"]])
            for ci in range(BT // 64):
                chunk(64, ci * 64, t1[0:64, ci * 16:(ci + 1) * 16], B["t1"])
            out_norm(ti, t0, BT)
        kb.dma("sp", obc_p_d, carry[:, :, :], [nb["carry"]], [], d_out)
        kb.dma("sp", obr_p_d.rearrange("h k v -> k h v"), S[:, :].rearrange("p (h d) -> p h d", h=8), [nb["S"]], [], d_out)
        in_proj(4, SEQ, NSAMP, True)
        wba, wbb = wload([(W[:, 4096:4112], KC, 16)])
        pBA, bBA = bank()
        for s_ in range(NS):
            for k in range(KC):
                mm(pBA[0:4, s_ * 16:(s_ + 1) * 16], hT[:, k, SEQ + s_ * 4:SEQ + s_ * 4 + 4], wba[:, k, :], k == 0, k == KC - 1,
                   [wbb, hB[4]], [bBA], k == KC - 1 and s_ == NS - 1)
        kb.op("act", lambda e: e.copy(out=t1[0:4, 0:256], in_=pBA[0:4, 0:256]), [bBA], [B["t1"]])
        for s_ in range(NS):
            kb.dma("sp", S[:, :].rearrange("p (h d) -> p h d", h=8), sbr_d[s_].rearrange("h k v -> k h v"), [], [nb["S"]], d_in)
            kb.op("act", lambda e: e.copy(out=Sb[:, :], in_=S[:, :]), [nb["S"]], [nb["Sb"]])
            chunk(4, s_ * 4, t1[0:4, s_ * 16:(s_ + 1) * 16], B["t1"])
            kb.dma("sp", obr_s_d[s_].rearrange("h k v -> k h v"), S[:, :].rearrange("p (h d) -> p h d", h=8), [nb["S"]], [], d_out)
        out_norm(4, SEQ, NSAMP)
        proj_fm(b_w_out[0], KC, KC, h_rhs, add_to_x)


    def moba(li):
        kb.barrier()
        W = c_w_qkv[0]
        ident_bf = cstb[:, C_IDB:C_IDB + 128]
        scm = 128.0 ** -0.5
        KT = carve(Bg, 0, 8 * SEQ).rearrange("p (h t) -> p h t", h=8)
        ksamp = carve(Bg, 32768, 512).rearrange("p (h t) -> p h t", h=8)
        V = carve(Cg, 0, 16 * D).rearrange("p (a n) -> p a n", a=16)
        qb = sq[:, :, 0:256]
        P2 = ptb[:, :, :].rearrange("p a n -> p (a n)")[:, 0:512].rearrange("p (a n) -> p a n", a=2)
        PV2 = [ptb[:, :, :].rearrange("p a n -> p (a n)")[:, i * 512:(i + 1) * 512].rearrange("p (a n) -> p a n", a=2) for i in range(2)]
        PB2 = [B["ptb"], Buf("ptb_b")]
        tselv = [tsel, tsel2]
        tselB = [Buf("tsel0"), Buf("tsel1")]
        nb = {n: Buf("m_" + n) for n in ("KT", "ksamp", "V", "vsamp", "qb", "km", "kmb", "sel", "selb", "negT", "tsel", "qs", "qrep",
                                         "kst", "ksb", "vst", "vall", "ksum", "kmsT", "idx", "selS", "PT", "pown")}
        rmsnorm(P_NMIX + li * 8)
        def evac_kT(oc, ti, t0, tn, ps, pb):
            if t0 < SEQ:
                kb.op("act", lambda e: e.copy(out=KT[:, oc, t0:t0 + tn], in_=ps[:, :tn]), [pb], [nb["KT"]])
            else:
                kb.op("act", lambda e: e.copy(out=ksamp[:, oc, :], in_=ps[:, :tn]), [pb], [nb["ksamp"]])
            kb.op("dve", lambda e: e.tensor_copy(out=t3[:, :tn], in_=ps[:, :tn]), [pb], [B["t3"]])
            kb.dma("sp", ock_d[:, oc, t0:t0 + tn], t3[:, :tn], [B["t3"]], [], d_out)
        proj_fm(W, KC, 8, h_rhs, evac_kT, col0=D)
        for cg in range(0, D, 256):
            view, wb = wload([(W[:, 2 * D + cg:2 * D + cg + 256], KC, 256)])
            for tt in range(17):
                m = 128 if tt < 16 else NSAMP
                ti = min(tt // 4, 4)
                ps, pb = bank()
                for k in range(KC):
                    mm(ps[0:m, :256], hT[:, k, tt * 128:tt * 128 + m], view[:, k, :], k == 0, k == KC - 1, [wb, hB[ti]], [pb], k == KC - 1)
                if tt < 16:
                    kb.op("act", lambda e: e.copy(out=V[:, tt, cg:cg + 256], in_=ps[:, :256]), [pb], [nb["V"]])
                kb.op("dve", lambda e: e.tensor_copy(out=t3[0:m, 0:256], in_=ps[0:m, :256]), [pb], [B["t3"]])
                kb.dma("sp", ocv_d[tt * 128:tt * 128 + m, cg:cg + 256], t3[0:m, 0:256], [B["t3"]], [], d_out)
        def evac_vs(oc, ti, t0, tn, ps, pb):
            kb.op("act", lambda e: e.copy(out=vsT[:, oc, :], in_=ps[:, :tn]), [pb], [nb["vsamp"]])
        proj_fm(W, KC, 8, lambda k, ti, t0, tn: (hT[:, k, t0:t0 + tn], hB[4]), evac_vs, tiles=[(SEQ, NSAMP)], col0=2 * D)
        kb.op("dve", lambda e: e.tensor_reduce(out=km[:, :, :], in_=KT[:, :, :].rearrange("p h (n t) -> p h n t", t=256), axis=AX.X, op=ALU.add),
              [nb["KT"]], [nb["km"]])
        kb.op("act", lambda e: e.mul(out=kmb[:, :, :], in_=km[:, :, :], mul=1.0 / 256.0), [nb["km"]], [nb["kmb"]])
        for b in range(8):
            q0 = b * 256
            ti = q0 // 512
            for hg in range(0, 8, 2):
                view, wb = wload([(W[:, hg * 128:(hg + 2) * 128], KC, 256)])
                for j2 in range(2):
                    ps, pb = bank()
                    for k in range(KC):
                        mm(ps[:, :256], view[:, k, j2 * 128:(j2 + 1) * 128], hT[:, k, q0:q0 + 256], k == 0, k == KC - 1, [wb, hB[ti]], [pb], k == KC - 1)
                    kb.op("act", lambda e: e.copy(out=qb[:, hg + j2, :], in_=ps[:, :256]), [pb], [nb["qb"]])
            if b >= 4:
                kb.op("dve", lambda e: e.memset(selb[:, :], 0.0), [], [nb["selb"]])
                pT, bT = bank()
                pTb = pT[:, :].bitcast(BF)
                for half in range(2):
                    pSel, bSel = bank()
                    for h in range(8):
                        mm(pSel[:, h * 8:h * 8 + b], qb[:, h, half * 128:(half + 1) * 128], kmb[:, h, 0:b], True, True, [nb["qb"], nb["kmb"]], [bSel], h == 7)
                    s3 = sel[:, 0:8 * b].rearrange("p (h n) -> p h n", h=8)
                    s3b = sel[:, 64:64 + 8 * b].rearrange("p (h n) -> p h n", h=8)
                    e3 = sel[:, 128:128 + 8 * b].rearrange("p (h n) -> p h n", h=8)
                    mx = sel[:, 192:200]

                    def mxb():
                        return mx.unsqueeze(2).to_broadcast([128, 8, b])
                    kb.op("act", lambda e: e.copy(out=s3, in_=pSel[:, 0:64].rearrange("p (h n) -> p h n", h=8)[:, :, 0:b]), [bSel], [nb["sel"]])
                    cur = s3
                    for it in range(2):
                        kb.op("dve", lambda e: e.tensor_reduce(out=mx, in_=cur, axis=AX.X, op=ALU.max), [], [nb["sel"]])
                        kb.op("dve", lambda e: e.tensor_tensor(out=e3, in0=cur, in1=mxb(), op=ALU.is_equal), [], [nb["sel"]])
                        kb.op("dve", lambda e: e.scalar_tensor_tensor(out=s3b, in0=e3, scalar=-1e30, in1=cur, op0=ALU.mult, op1=ALU.add), [], [nb["sel"]])
                        cur = s3b
                    kb.op("dve", lambda e: e.tensor_reduce(out=mx, in_=s3b, axis=AX.X, op=ALU.max), [], [nb["sel"]])
                    kb.op("dve", lambda e: e.tensor_tensor(out=e3, in0=s3, in1=mxb(), op=ALU.is_ge), [], [nb["sel"]])
                    kb.op("dve", lambda e: e.tensor_scalar(out=selb[:, :].rearrange("p (h n) -> p h n", h=8)[:, :, 0:b], in0=e3, scalar1=-NEG, scalar2=NEG,
                                                           op0=ALU.mult, op1=ALU.add), [nb["sel"]], [nb["selb"]])
                    kb.op("pe", lambda e: e.transpose(pTb[0:64, half * 128:(half + 1) * 128], selb[:, :], ident_bf), [nb["selb"], B["cstb"]], [bT], True)
                kb.op("act", lambda e: e.copy(out=negT[:, :], in_=pTb[0:64, 0:256]), [bT], [nb["negT"]])
            pstate["n"] = 4
            pstate["i"] = 0
            for h in range(8):
                pOD, bOD = lbank(2 * (h % 2))
                pDN, bDN = lbank(2 * (h % 2) + 1)

                def scores(n, h=h):
                    pSc, bSc = bank()
                    j = n % 2
                    extra = (n == b) or (b >= 4)
                    if n < b and b >= 4:
                        r = h * 8 + n
                        kb.op("dve", lambda e: e.tensor_scalar(out=tselv[j][:, :], in0=negT[:, :], scalar1=cst[0:64, r:r + 1], scalar2=None, op0=ALU.mult),
                              [nb["negT"], B["cst"]], [tselB[j]])
                    for kc in range(2):
                        mm(pSc[:, kc * 256:(kc + 1) * 256], KT[:, h, n * 256 + kc * 128:n * 256 + (kc + 1) * 128], qb[:, h, :], True, not extra,
                           [nb["KT"], nb["qb"]], [bSc], (not extra) and kc == 1)
                        if n == b:
                            mm(pSc[:, kc * 256:(kc + 1) * 256], ident_bf, cstb[:, C_CM + kc * 256:C_CM + (kc + 1) * 256], False, True,
                               [B["cstb"]], [bSc], kc == 1)
                        elif b >= 4:
                            mm(pSc[:, kc * 256:(kc + 1) * 256], cstb[0:64, C_ONES:C_ONES + 128], tselv[j][:, :], False, True,
                               [B["cstb"], tselB[j]], [bSc], kc == 1)
                    kb.op("act", lambda e: e.activation(out=PV2[j], in_=pSc[:, :].rearrange("p (a n) -> p a n", a=2), func=AF.Exp, scale=scm),
                          [bSc], [PB2[j]])

                scores(0)
                for n in range(b + 1):
                    if n < b:
                        scores(n + 1)
                    j = n % 2
                    for kc in range(2):
                        first = (n == 0 and kc == 0)
                        lastm = (n == b and kc == 1)
                        mm(pOD[:, 0:256], V[:, n * 2 + kc, h * 128:(h + 1) * 128], PV2[j][:, kc, :], first, lastm, [nb["V"], PB2[j]], [bOD], False)
                        mm(pDN[:, 0:256], ones_bf, PV2[j][:, kc, :], first, lastm, [B["cstb"], PB2[j]], [bDN], True)
                kb.op("act", lambda e: e.activation(out=t1[:, 0:256], in_=pDN[:, 0:256], func=AF.Ln), [bDN], [B["t1"]])
                kb.op("act", lambda e: e.activation(out=t1[:, 0:256], in_=t1[:, 0:256], func=AF.Exp, scale=-1.0), [], [B["t1"]])
                kb.op("dve", lambda e: e.tensor_tensor(out=hT[:, h, q0:q0 + 256], in0=pOD[:, 0:256], in1=t1[:, 0:256], op=ALU.mult), [bOD, B["t1"]], [hB[ti]])
            pstate["n"] = 6
            pstate["i"] = 0
        kb.barrier()
        NSL = 3
        kst = [carve(Bg, i * 4096, 1024, F32) for i in range(NSL)]
        vst = [carve(Bg, 12288 + i * 4096, 1024, F32) for i in range(NSL)]
        ksb = [carve(Bg, 24576 + i * 2048, 1024).rearrange("p (h t) -> p h t", h=8) for i in range(2)]
        ptab_i = carve(Bg, 28672, 256, I32); ptab_f = carve(Bg, 29696, 256, F32); idx_i = carve(Bg, 30720, 256, I32)
        ksum = carve(Bg, 31744, 128, F32).rearrange("p (h g) -> p h g", h=8)
        kmsT = carve(Bg, 32256, 64, F32).rearrange("p (h n) -> p h n", h=8)
        vall = carve(Cg, 0, 16 * D).rearrange("p (a n) -> p a n", a=16)
        kbs = [Buf("kst%d" % i) for i in range(NSL)]; vbs = [Buf("vst%d" % i) for i in range(NSL)]; kbb = [Buf("ksb0"), Buf("ksb1")]
        kb.dma("sp", ptab_i[:, :], pt_d.partition_broadcast(128), [], [nb["idx"]], d_in)
        kb.op("dve", lambda e: e.tensor_copy(out=ptab_f[:, :], in_=ptab_i[:, :]), [], [nb["idx"]])
        kb.op("dve", lambda e: e.tensor_scalar(out=ptab_f[:, :], in0=ptab_f[:, :], scalar1=128.0, scalar2=cst[:, C_PID:C_PID + 1], op0=ALU.mult, op1=ALU.add),
              [B["cst"]], [nb["idx"]])
        kb.op("dve", lambda e: e.tensor_copy(out=idx_i[:, :], in_=ptab_f[:, :]), [], [nb["idx"]])
        qrb = sq[:, :, :].rearrange("p a n -> p (a n)").rearrange("p (j m) -> p j m", j=32)
        nb["qrep"] = B["sq"]
        kmsb = carve(Bg, 32512, 64).rearrange("p (h n) -> p h n", h=8)
        selS = carve(Cg, 32768, 256, F32)
        d_g = [kb.dsem("gk%d" % i) for i in range(NSL)]
        d_gv = [kb.dsem("gv%d" % i) for i in range(NSL)]
        for hg in range(0, 8, 2):
            view, wb = wload([(W[:, hg * 128:(hg + 2) * 128], KC, 256)])
            for j2 in range(2):
                ps, pb = bank()
                for k in range(KC):
                    mm(ps[:, :NSAMP], view[:, k, j2 * 128:(j2 + 1) * 128], hT[:, k, SEQ:NT], k == 0, k == KC - 1, [wb, hB[4]], [pb], k == KC - 1)
                kb.op("act", lambda e: e.copy(out=qsamp[:, hg + j2, :], in_=ps[:, :NSAMP]), [pb], [nb["qs"]])
        for s_ in range(NS):
            c0 = SEQ + s_ * 4
            pS, bS = lbank(0)
            for pg in range(16):
                i3 = (s_ * 16 + pg) % NSL
                i = pg % 2
                col = s_ * 16 + pg
                kb.idma(kst[i3][:, :], poolk_d, idx_i[:, col:col + 1], [nb["idx"]], [kbs[i3]], d_g[i3])
                kb.idma(vst[i3][:, :], poolv_d, idx_i[:, col:col + 1], [nb["idx"]], [vbs[i3]], d_gv[i3])
                kb.op("act", lambda e: e.copy(out=ksb[i][:, :, :], in_=kst[i3][:, :].rearrange("p (h t) -> p h t", h=8)), [kbs[i3]], [kbb[i]])
                kb.op("dve", lambda e: e.tensor_reduce(out=ksum[:, :, pg], in_=kst[i3][:, :].rearrange("p (h t) -> p h t", h=8), axis=AX.X, op=ALU.add),
                      [kbs[i3]], [nb["ksum"]])
                kb.op("dve", lambda e: e.tensor_copy(out=vall[:, pg, :], in_=vst[i3][:, :]), [vbs[i3]], [nb["vall"]])
                for h in range(8):
                    mm(pS[:, pg * 32 + h * 4:pg * 32 + h * 4 + 4], ksb[i][:, h, :], qsamp[:, h, s_ * 4:s_ * 4 + 4], True, True, [kbb[i], nb["qs"]], [bS], h == 7)
            kb.op("dve", lambda e: e.tensor_reduce(out=kmsT[:, :, :], in_=ksum[:, :, :].rearrange("p h (n two) -> p h n two", two=2), axis=AX.X, op=ALU.add),
                  [nb["ksum"]], [nb["kmsT"]])
            kb.op("act", lambda e: e.mul(out=kmsb[:, :, :], in_=kmsT[:, :, :], mul=1.0 / 256.0), [nb["kmsT"]], [nb["kmsT"]])
            kb.op("dve", lambda e: e.tensor_copy(out=qrb[:, :, :].rearrange("p (h q) m -> p h q m", h=8),
                                                 in_=qsamp[:, :, s_ * 4:s_ * 4 + 4].unsqueeze(3).to_broadcast([128, 8, 4, 128])), [nb["qs"]], [nb["qrep"]])
            pG, bG = bank()
            for j in range(32):
                mm(pG[:, j * 8:(j + 1) * 8], qrb[:, j, :], kmsb[:, j // 4, :], True, True, [nb["qrep"], nb["kmsT"]], [bG], j == 31)
            g3 = selS[:, 0:256].rearrange("p (j n) -> p j n", n=8)
            sA = t2[:, 0:256].rearrange("p (j n) -> p j n", n=8)
            sBv = t2[:, 256:512].rearrange("p (j n) -> p j n", n=8)
            eq = t3[:, 0:256].rearrange("p (j n) -> p j n", n=8)
            mx = t3[:, 256:288]

            def mxb():
                return mx.unsqueeze(2).to_broadcast([128, 32, 8])
            kb.op("act", lambda e: e.copy(out=sA, in_=pG[:, 0:256].rearrange("p (j n) -> p j n", n=8)), [bG], [B["t2"]])
            cur = sA
            for it in range(2):
                kb.op("dve", lambda e: e.tensor_reduce(out=mx, in_=cur, axis=AX.X, op=ALU.max), [B["t2"]], [B["t3"]])
                kb.op("dve", lambda e: e.tensor_tensor(out=eq, in0=cur, in1=mxb(), op=ALU.is_equal), [B["t2"]], [B["t3"]])
                kb.op("dve", lambda e: e.scalar_tensor_tensor(out=sBv, in0=eq, scalar=-1e30, in1=cur, op0=ALU.mult, op1=ALU.add), [B["t3"]], [B["t2"]])
                cur = sBv
            kb.op("dve", lambda e: e.tensor_reduce(out=mx, in_=sBv, axis=AX.X, op=ALU.max), [B["t2"]], [B["t3"]])
            kb.op("dve", lambda e: e.tensor_tensor(out=g3, in0=sA, in1=mxb(), op=ALU.is_ge), [B["t2"], B["t3"]], [nb["selS"]])
            PT = t1[:, :].bitcast(BF)[:, 0:512].rearrange("p (g j) -> p g j", g=16)
            kb.op("act", lambda e: e.activation(out=t2[:, :], in_=pS[:, :], func=AF.Exp, scale=scm), [bS], [B["t2"]])
            kb.op("dve", lambda e: e.tensor_tensor(out=PT.rearrange("p (n two) j -> p n two j", two=2),
                                                   in0=t2[:, :].rearrange("p (n two j) -> p n two j", two=2, j=32),
                                                   in1=g3.rearrange("p j n -> p n j").unsqueeze(2).to_broadcast([128, 8, 2, 32]), op=ALU.mult),
                  [B["t2"], nb["selS"]], [B["t1"]])
            pWb, bWb = bank()
            for j in range(32):
                mm(pWb[:, j * 4:(j + 1) * 4], qrb[:, j, :], ksamp[:, j // 4, s_ * 4:s_ * 4 + 4], True, True, [nb["qrep"], nb["ksamp"]], [bWb], j == 31)
            ob = t3[:, 0:128]
            kb.op("act", lambda e: e.activation(out=ob, in_=pWb[:, 0:128], func=AF.Exp, scale=scm), [bWb], [B["t3"]])
            kb.op("dve", lambda e: e.tensor_tensor(out=ob, in0=ob, in1=cst[:, C_CM4:C_CM4 + 128], op=ALU.mult), [B["cst"]], [B["t3"]])
            kb.op("dve", lambda e: e.tensor_reduce(out=t3[:, 128:160], in_=ob.rearrange("p (j t) -> p j t", t=4), axis=AX.X, op=ALU.add), [], [B["t3"]])
            kb.op("dve", lambda e: e.tensor_tensor(out=t3[:, 160:288].rearrange("p (h q t) -> p h q t", h=8, q=4),
                                                   in0=ob.rearrange("p (h q t) -> p h q t", h=8, q=4),
                                                   in1=vsT[:, :, s_ * 4:s_ * 4 + 4].unsqueeze(2).to_broadcast([128, 8, 4, 4]), op=ALU.mult),
                  [nb["vsamp"]], [B["t3"]])
            kb.op("dve", lambda e: e.tensor_reduce(out=t3[:, 288:320], in_=t3[:, 160:288].rearrange("p (j t) -> p j t", t=4), axis=AX.X, op=ALU.add), [], [B["t3"]])
            pO_, bO_ = lbank(1)
            for h in range(8):
                for pg in range(16):
                    mm(pO_[:, h * 4:(h + 1) * 4], vall[:, pg, h * 128:(h + 1) * 128], PT[:, pg, h * 4:(h + 1) * 4], pg == 0, pg == 15,
                       [nb["vall"], B["t1"]], [bO_], False)
            for pg in range(16):
                mm(pO_[:, 32:64], ones_bf, PT[:, pg, :], pg == 0, pg == 15, [B["cstb"], B["t1"]], [bO_], pg == 15)
            kb.op("dve", lambda e: e.tensor_tensor(out=t3[:, 128:160], in0=t3[:, 128:160], in1=pO_[:, 32:64], op=ALU.add), [bO_], [B["t3"]])
            kb.op("dve", lambda e: e.tensor_tensor(out=t3[:, 288:320], in0=t3[:, 288:320], in1=pO_[:, 0:32], op=ALU.add), [bO_], [B["t3"]])
            kb.op("dve", lambda e: e.reciprocal(out=t3[:, 128:160], in_=t3[:, 128:160]), [], [B["t3"]])
            kb.op("dve", lambda e: e.tensor_tensor(out=hT[:, :, c0:c0 + 4], in0=t3[:, 288:320].rearrange("p (h q) -> p h q", h=8),
                                                   in1=t3[:, 128:160].rearrange("p (h q) -> p h q", h=8), op=ALU.mult), [], [B["t3"], hB[4]])
        proj_fm(c_w_out[0], KC, KC, h_rhs, add_to_x)

    def xattn(li):
        qv = big[:, 0:KC * NT].rearrange("p (k t) -> p k t", k=KC)
        MT = [(0, MEM)]
        kb.barrier()
        kb.dma("sp", memR[:], memT_d, [], [B["memR"]], d_in)
        rms_stats(memR, B["memR"], 0, MEM)
        for k in range(KC):
            gc = P_NMEM + li * 8 + k
            kb.op("dve", lambda e, k=k, gc=gc: e.scalar_tensor_tensor(out=memN[:, k, :], in0=memR[:, k, :], scalar=par[:, gc:gc + 1],
                                                                   in1=rstd[:, :MEM], op0=ALU.mult, op1=ALU.mult),
                  [B["memR"], B["par"], B["rstd"]], [B["memN"]])

        def mem_rhs(k, ti, t0, tn):
            return memN[:, k, t0:t0 + tn], B["memN"]

        def evac_k(oc, ti, t0, tn, ps, pb):
            kb.op("act", lambda e: e.copy(out=mkT[:, oc, :], in_=ps[:, :MEM]), [pb], [B["mkT"]])
            kb.op("dve", lambda e: e.tensor_copy(out=mo32[:, 0:MEM], in_=ps[:, :MEM]), [pb], [B["t3"]])
            kb.dma("sp", omk_d[li, :, oc, :], mo32[:, 0:MEM], [B["t3"]], [], d_out)

        if 'xa_k' in phases:
            proj_fm(x_w_kv[li], KC, 8, mem_rhs, evac_k, tiles=MT)
        for cg in (range(0, D, 256) if 'xa_v' in phases else []):
            view, wb = wload([(x_w_kv[li][:, D + cg:D + cg + 256], KC, 256)])
            for mt in range(2):
                ps, pb = bank()
                for k in range(KC):
                    mm(ps[:, :256], memN[:, k, mt * 128:(mt + 1) * 128], view[:, k, :], k == 0, k == KC - 1, [wb, B["memN"]], [pb], k == KC - 1)
                kb.op("act", lambda e: e.copy(out=mv[:, mt, cg:cg + 256], in_=ps[:, :256]), [pb], [B["mv"]])
                kb.op("dve", lambda e: e.tensor_copy(out=mo32[:, 0:256], in_=ps[:, :256]), [pb], [B["t3"]])
                kb.dma("sp", omv_d[li, mt, :, cg:cg + 256], mo32[:, 0:256], [B["t3"]], [], d_out)
        kb.barrier()
        rmsnorm(P_NXA + li * 8)

        def evac_q(oc, ti, t0, tn, ps, pb):
            kb.op("act", lambda e: e.copy(out=qv[:, oc, t0:t0 + tn], in_=ps[:, :tn]), [pb], [gB[ti]])

        proj_fm(x_w_q[li], KC, KC, h_rhs, evac_q)
        sc = 1.0 / 16.0
        pS, bS = lbank(0)
        pO, bO = lbank(1)

        def prompt_group(ti, t0, tn, hd):
            for mt in range(2):
                ps, pb = bank()
                for dc in range(2):
                    mm(ps[:, :tn], mkT[:, hd * 2 + dc, mt * 128:(mt + 1) * 128], qv[:, hd * 2 + dc, t0:t0 + tn], dc == 0, dc == 1,
                       [B["mkT"], gB[ti]], [pb], dc == 1)
                kb.op("act", lambda e, mt=mt, ps=ps: e.activation(out=ptb[:, mt, :tn], in_=ps[:, :tn], func=AF.Exp, scale=sc),
                      [pb], [B["ptb"]])
            psd, pbd = bank()
            for mt in range(2):
                mm(psd[:, :tn], ones_bf, ptb[:, mt, :tn], mt == 0, mt == 1, [B["ptb"], B["cstb"]], [pbd], mt == 1)
            kb.op("act", lambda e: e.activation(out=rstd[:, :tn], in_=psd[:, :tn], func=AF.Ln), [pbd], [B["rstd"]])
            kb.op("act", lambda e: e.activation(out=rstd[:, :tn], in_=rstd[:, :tn], func=AF.Exp, scale=-1.0), [], [B["rstd"]])
            for dvc in range(2):
                ps, pb = bank()
                for mt in range(2):
                    mm(ps[:, :tn], mv[:, mt, hd * 256 + dvc * 128: hd * 256 + (dvc + 1) * 128], ptb[:, mt, :tn], mt == 0, mt == 1,
                       [B["mv"], B["ptb"]], [pb], mt == 1)
                kb.op("dve", lambda e, ps=ps, dvc=dvc: e.tensor_tensor(out=hT[:, hd * 2 + dvc, t0:t0 + tn], in0=ps[:, :tn],
                                                                     in1=rstd[:, :tn], op=ALU.mult), [pb, B["rstd"]], [hB[ti]])

        def sample_seq(s):
            i = s % 2
            kb.dma("pool", skT[i][:], cmk_d[li, s], [], [B["skT%d" % i]], d_kv[i])
            kb.dma("pool", sv[i][:], cmv_d[li, s].rearrange("a p n -> p a n"), [], [B["sv%d" % i]], d_kvv[i])
            for mt in range(2):
                for hd in range(4):
                    c0 = mt * 256 + s * 16 + hd * 4
                    for dc in range(2):
                        mm(pS[:, c0:c0 + 4], skT[i][:, hd * 2 + dc, mt * 128:(mt + 1) * 128], qv[:, hd * 2 + dc, SEQ + s * 4:SEQ + s * 4 + 4],
                           dc == 0, dc == 1, [B["skT%d" % i], gB[4]], [bS], True if (dc == 1 and hd == 3 and mt == 1) else False)
            kb.op("act", lambda e, s=s: e.activation(
                out=pts[:, :, s * 16:(s + 1) * 16], in_=pS[:, :].rearrange("p (a c) -> p a c", a=2)[:, :, s * 16:(s + 1) * 16],
                func=AF.Exp, scale=sc), [bS], [B["pts"]])
            for hd in range(4):
                for dvc in range(2):
                    c0 = ((s * 4 + hd) * 2 + dvc) * 4
                    for mt in range(2):
                        mm(pO[:, c0:c0 + 4], sv[i][:, mt, hd * 256 + dvc * 128:hd * 256 + (dvc + 1) * 128],
                           pts[:, mt, s * 16 + hd * 4:s * 16 + hd * 4 + 4], mt == 0, mt == 1, [B["sv%d" % i], B["pts"]], [bO],
                           True if (mt == 1 and hd == 3 and dvc == 1) else False)

        for ti, (t0, tn) in enumerate(TILES[:4]):
            for hd in range(4):
                prompt_group(ti, t0, tn, hd)
                sample_seq(ti * 4 + hd)
        pD, bD = bank()
        for mt in range(2):
            mm(pD[:, :256], ones_bf, pts[:, mt, 0:256], mt == 0, mt == 1, [B["pts"], B["cstb"]], [bD], mt == 1)
        kb.op("act", lambda e: e.activation(out=rstd[:, :256], in_=pD[:, :256], func=AF.Ln), [bD], [B["rstd"]])
        kb.op("act", lambda e: e.activation(out=rstd[:, :256], in_=rstd[:, :256], func=AF.Exp, scale=-1.0), [], [B["rstd"]])
        for dvc in range(2):
            kb.op("dve", lambda e, dvc=dvc: e.tensor_tensor(
                out=hT[:, :, SEQ:NT].rearrange("p (h v) (s q) -> p h v s q", v=2, q=LS)[:, :, dvc, :, :],
                in0=pO[:, :].rearrange("p (s h v q) -> p h v s q", h=4, v=2, q=LS)[:, :, dvc, :, :],
                in1=rstd[:, :256].rearrange("p (s h q) -> p h s q", h=4, q=LS), op=ALU.mult), [bO, B["rstd"]], [hB[4]])
        proj_fm(x_w_o[li], KC, KC, h_rhs, add_to_x)

    def swiglu(li):
        kb.barrier()
        rmsnorm(P_NFFN + li * 8)
        av = big[:, 0:KC * NT].rearrange("p (k t) -> p k t", k=KC)
        for f0, nf in ((0, 8), (8, 8), (16, 6)):
            for fi in range(nf):
                f = f0 + fi
                view, wb = wload([(f_w_up[li][:, f * 128:(f + 1) * 128], KC, 128),
                                  (f_w_up[li][:, DFF + f * 128:DFF + (f + 1) * 128], KC, 128)])
                for ti, (t0, tn) in enumerate(TILES):
                    pg, bg = bank()
                    for k in range(KC):
                        mm(pg[:, :tn], view[:, k, 0:128], hT[:, k, t0:t0 + tn], k == 0, k == KC - 1, [wb, hB[ti]], [bg], k == KC - 1)
                    pu, bu = bank()
                    for k in range(KC):
                        mm(pu[:, :tn], view[:, k, 128:256], hT[:, k, t0:t0 + tn], k == 0, k == KC - 1, [wb, hB[ti]], [bu], k == KC - 1)
                    kb.op("act", lambda e, pg=pg: e.activation(out=t1[:, :tn], in_=pg[:, :tn], func=AF.Silu), [bg], [B["t1"]])
                    kb.op("dve", lambda e, pu=pu, fi=fi: e.tensor_tensor(out=av[:, fi, t0:t0 + tn], in0=pu[:, :tn], in1=t1[:, :tn], op=ALU.mult),
                          [bu, B["t1"]], [gB[ti]])
            proj_fm(f_w_down[li], nf, KC, lambda k, ti, t0, tn: (av[:, k, t0:t0 + tn], gB[ti]), add_to_x, row0=f0 * 128)

    for li in range(depth):
        kind, j = li % 3, li // 3
        if kind == 0 and 'mix' in phases:
            shortconv(li, j)
        if kind == 1 and 'mix' in phases:
            gdn(li)
        if kind == 2 and 'mix' in phases:
            moba(li)
        if 'xa' in phases:
            xattn(li)
        if 'ffn' in phases:
            swiglu(li)

    for ti, (t0, tn) in enumerate(TILES):
        rms_stats(xT, xB[ti], t0, tn)
        for k in range(KC):
            kb.op("dve", lambda e, k=k: e.scalar_tensor_tensor(
                out=t1[:, :tn], in0=xT[:, k, t0:t0 + tn], scalar=par[:, P_NFIN + k:P_NFIN + k + 1],
                in1=rstd[:, :tn], op0=ALU.mult, op1=ALU.mult), [xB[ti], B["rstd"], B["par"]], [B["t1"]])
            kb.dma("sp", yT_d[:, k, t0:t0 + tn], t1[:, :tn], [B["t1"]], [], d_out)
    kb.finish()
    es.close()
    return nc, kb


P_NMIX = 0
P_NMEM = 32
P_NXA = 64
P_NFFN = 96
P_NFIN = 128
P_ACONV = 136
P_BCONV = 184
P_DTB = 280
P_ALOG = 288
P_BNORM = 296
NPAR = 297
G_ID = 0
G_IDREP = 64
G_U = 576
G_NEGU = 640
G_ONES = 704
G_NEGONES = 832
G_MLOW = 960
G_MUP = 1472
NGC = 1984
C_ONES = 0
C_IDB = 128
C_CM = 256
NCSTB = 768
C_PID = 64
C_CM4 = 72
NCST = 200
NPOOL = 2560


def _vec_pk(v):
    return np.ascontiguousarray(v.reshape(-1, 128).T)


_SHARED = {}


def shared_inputs(I):
    if "poolk" not in _SHARED:
        ck = I["cache_c_k"][0]
        _SHARED["poolk"] = np.ascontiguousarray(ck.transpose(0, 3, 2, 1)).reshape(NPOOL * 128, D)
        _SHARED["poolv"] = np.ascontiguousarray(I["cache_c_v"][0]).reshape(NPOOL * 128, D)
    return _SHARED


def make_inputs(core, I):
    b = core
    s0, s1 = core * NS, (core + 1) * NS
    xa = np.concatenate([I["x_prompt"][b], I["x_sample"][s0:s1].reshape(NSAMP, D)], axis=0)
    x0 = np.ascontiguousarray(xa.T.reshape(KC, 128, NT).transpose(1, 0, 2))
    memT = np.ascontiguousarray(I["mem_prompt"][b].T.reshape(KC, 128, MEM).transpose(1, 0, 2))
    par = np.zeros((128, NPAR), np.float32)
    for li in range(4):
        par[:, P_NMIX + li * 8:P_NMIX + li * 8 + 8] = _vec_pk(I["norm_mix"][li])
        par[:, P_NMEM + li * 8:P_NMEM + li * 8 + 8] = _vec_pk(I["norm_mem"][li])
        par[:, P_NXA + li * 8:P_NXA + li * 8 + 8] = _vec_pk(I["norm_xattn"][li])
        par[:, P_NFFN + li * 8:P_NFFN + li * 8 + 8] = _vec_pk(I["norm_ffn"][li])
    par[:, P_NFIN:P_NFIN + 8] = _vec_pk(I["norm_final"])
    for j in range(2):
        for tap in range(3):
            par[:, P_ACONV + (j * 3 + tap) * 8:P_ACONV + (j * 3 + tap) * 8 + 8] = _vec_pk(I["a_w_conv"][j, tap])
    for tap in range(4):
        par[:, P_BCONV + tap * 24:P_BCONV + tap * 24 + 24] = _vec_pk(I["b_w_conv"][0, tap])
    par[:, P_DTB:P_DTB + 8] = I["b_dt_bias"][0][None, :]
    par[:, P_ALOG:P_ALOG + 8] = I["b_a_log"][0][None, :]
    par[:, P_BNORM] = I["b_norm"][0]
    gc = np.zeros((64, NGC), np.float32)
    ii = np.arange(64)
    gc[:, G_ID:G_ID + 64] = np.eye(64)
    gc[:, G_IDREP:G_IDREP + 512] = np.tile(np.eye(64), (1, 8))
    U = (ii[:, None] <= ii[None, :]).astype(np.float32)
    gc[:, G_U:G_U + 64] = U
    gc[:, G_NEGU:G_NEGU + 64] = -U
    gc[:, G_ONES:G_ONES + 128] = 1.0
    gc[:, G_NEGONES:G_NEGONES + 128] = -1.0
    gc[:, G_MLOW:G_MLOW + 512] = np.tile(np.where(ii[:, None] > ii[None, :], 0.0, -1e30), (1, 8))
    gc[:, G_MUP:G_MUP + 512] = np.tile(np.where(ii[None, :] >= ii[:, None], 0.0, -1e30), (1, 8))
    sbc = np.ascontiguousarray(I["state_b_conv"][0, s0:s1].reshape(NS, 3, 24, 128).transpose(2, 3, 0, 1))
    sbr = np.ascontiguousarray(I["state_b_rec"][0, s0:s1])
    import ml_dtypes
    cst = np.zeros((128, NCST), np.float32)
    cstb = np.zeros((128, NCSTB), np.float32)
    cstb[:, C_ONES:C_ONES + 128] = 1.0
    cstb[:, C_IDB:C_IDB + 128] = np.eye(128, dtype=np.float32)
    rr = np.arange(128)[:, None]
    pp = np.arange(256)[None, :]
    for kc in range(2):
        cstb[:, C_CM + kc * 256:C_CM + (kc + 1) * 256] = np.where(kc * 128 + rr <= pp, 0.0, NEG)
    cst[0:64, 0:64] = np.eye(64, dtype=np.float32)
    cst[:, C_PID] = np.arange(128, dtype=np.float32)
    cc_ = np.arange(128)
    cst[:, C_CM4:C_CM4 + 128] = ((cc_ % 4) <= ((cc_ // 4) % 4)).astype(np.float32)[None, :]
    cstb = cstb.astype(ml_dtypes.bfloat16)
    sa = np.ascontiguousarray(I["state_a_conv"][:, s0:s1].reshape(2, NS, 2, KC, 128).transpose(4, 0, 3, 1, 2))
    ck = I["cache_mem_k"][:, s0:s1].reshape(4, NS, MEM, 4, 2, 128)
    cmk = np.ascontiguousarray(ck.transpose(0, 1, 5, 3, 4, 2).reshape(4, NS, 128, 8, MEM))
    cmv = np.ascontiguousarray(I["cache_mem_v"][:, s0:s1].reshape(4, NS, 2, 128, D))
    return {"x0": x0, "memT": memT, "par": par, "cst": cst, "cstb": cstb, "sa": sa, "cmk": cmk, "cmv": cmv, "sbc": sbc, "sbr": sbr, "gcst": gc,
            "b_w_in": I["b_w_in"], "b_w_out": I["b_w_out"],
            "c_w_qkv": I["c_w_qkv"], "c_w_out": I["c_w_out"], "poolk": shared_inputs(I)["poolk"], "poolv": shared_inputs(I)["poolv"],
            "pt": np.ascontiguousarray(I["page_table"][s0:s1].reshape(1, NS * 16).astype(np.int32)),
            "a_w_in": I["a_w_in"], "a_w_out": I["a_w_out"], "x_w_q": I["x_w_q"], "x_w_kv": I["x_w_kv"], "x_w_o": I["x_w_o"],
            "f_w_up": I["f_w_up"], "f_w_down": I["f_w_down"]}


def kernel(**inputs):
    I = {k: np.asarray(v) for k, v in inputs.items()}
    _SHARED.clear()
    nc, kb = build()
    in_maps = [make_inputs(c, I) for c in range(NCORES)]
    res = run_bass_kernel_spmd(nc, in_maps, core_ids=list(range(NCORES)))
    R = res.results
    f32 = np.float32
    y_p = np.zeros((8, SEQ, D), f32); y_s = np.zeros((128, LS, D), f32)
    a_p = np.zeros((2, 8, 2, D), f32); a_s = np.zeros((2, 128, 2, D), f32)
    bc_p = np.zeros((1, 8, 3, 3 * D), f32); bc_s = np.zeros((1, 128, 3, 3 * D), f32)
    br_p = np.zeros((1, 8, 8, 128, 128), f32); br_s = np.zeros((1, 128, 8, 128, 128), f32)
    ck_p = np.zeros((1, 8, SEQ, 8, 128), f32); cv_p = np.zeros((1, 8, SEQ, 8, 128), f32)
    ck_s = np.zeros((1, 128, LS, 8, 128), f32); cv_s = np.zeros((1, 128, LS, 8, 128), f32)
    mk_p = np.zeros((4, 8, MEM, 4, 256), f32); mv_p = np.zeros((4, 8, MEM, 4, 256), f32)
    for c in range(NCORES):
        r = R[c]
        s0, s1 = c * NS, (c + 1) * NS
        y = np.asarray(r["yT"]).transpose(2, 1, 0).reshape(NT, D)
        y_p[c] = y[:SEQ]
        y_s[s0:s1] = y[SEQ:].reshape(NS, LS, D)
        oa = np.asarray(r["oa"])
        for j in range(2):
            a = oa[:, j].transpose(2, 1, 0).reshape(2 + 2 * NS, D)
            a_p[j, c] = a[:2]
            a_s[j, s0:s1] = a[2:].reshape(NS, 2, D)
        bc_p[0, c] = np.asarray(r["obc_p"]).transpose(2, 1, 0).reshape(3, 3 * D)
        bc_s[0, s0:s1] = np.asarray(r["obc_s"]).transpose(2, 3, 0, 1).reshape(NS, 3, 3 * D)
        br_p[0, c] = np.asarray(r["obr_p"])
        br_s[0, s0:s1] = np.asarray(r["obr_s"])
        k = np.asarray(r["ock"]).transpose(2, 1, 0)
        v = np.asarray(r["ocv"]).reshape(NT, 8, 128)
        ck_p[0, c] = k[:SEQ]; cv_p[0, c] = v[:SEQ]
        ck_s[0, s0:s1] = k[SEQ:].reshape(NS, LS, 8, 128); cv_s[0, s0:s1] = v[SEQ:].reshape(NS, LS, 8, 128)
        omk = np.asarray(r["omk"]); omv = np.asarray(r["omv"])
        for li in range(4):
            mk_p[li, c] = omk[li].transpose(2, 1, 0).reshape(MEM, 4, 256)
            mv_p[li, c] = omv[li].reshape(MEM, 4, 256)
    return (y_p, y_s, a_p, a_s, bc_p, bc_s, br_p, br_s, ck_p, cv_p, ck_s, cv_s, mk_p, mv_p)
```
